# Optimizing a Trainium2 kernel written in Bass

```python
import jax, jax.numpy as jnp
from jax import lax
import numpy as np

D_MODEL = 2048
BATCH = 8
SEQ = 2048
DEPTH = 4

GRID_W = 64
CTX_LEN = 256
EPS = 1e-6

CONV_W = 4
CONV_LEFT = 2

LRU_WIDTH = D_MODEL // 2
LRU_BLOCKS = 16
LRU_BLOCK = LRU_WIDTH // LRU_BLOCKS
LRU_C = 8.0

HEAD_DIM = 128
ATT_HEADS = D_MODEL // 256
ATT_KV_HEADS = 2
ATT_GROUP = ATT_HEADS // ATT_KV_HEADS
ATT_WIDTH = ATT_HEADS * HEAD_DIM
KV_WIDTH = ATT_KV_HEADS * HEAD_DIM
WINDOW = 128
ATT_BLOCK = 128
ROPE_THETA = 10000.0

SSD_WIDTH = D_MODEL // 2
SSD_HEAD_DIM = 64
SSD_HEADS = SSD_WIDTH // SSD_HEAD_DIM
SSD_GROUPS = 2
SSD_HPG = SSD_HEADS // SSD_GROUPS
SSD_STATE = 128
SSD_CHUNK = 128
SSD_BC = SSD_GROUPS * SSD_STATE
SSD_CONV_CH = SSD_WIDTH + 2 * SSD_BC

MIX_WIDTH = LRU_WIDTH + ATT_WIDTH + SSD_WIDTH
IN_SIZES = (LRU_WIDTH, LRU_WIDTH, ATT_WIDTH, KV_WIDTH, KV_WIDTH, ATT_WIDTH, SSD_CONV_CH, SSD_WIDTH, SSD_HEADS)
IN_WIDTH = sum(IN_SIZES)

kernel_name = "hybrid_lru_swa_ssd_dit_prefix"


def rmsnorm(x, w):
    xf = x.astype(jnp.float32)
    y = xf * lax.rsqrt(jnp.mean(xf * xf, axis=-1, keepdims=True) + EPS)
    return (y * w.astype(jnp.float32)).astype(x.dtype)


def modulation(cond, ada_w, ada_b):
    m = jax.nn.silu(cond) @ ada_w + ada_b
    return jnp.split(m, 3, axis=-1)


def split_cols(p):
    idx, acc = [], 0
    for s in IN_SIZES[:-1]:
        acc += s
        idx.append(acc)
    return jnp.split(p, idx, axis=-1)


def dwconv_centred(x, w, b):
    ch = x.shape[-1]
    y = lax.conv_general_dilated(x, w[:, None, :], window_strides=(1,),
                                 padding=[(CONV_LEFT, CONV_W - 1 - CONV_LEFT)],
                                 dimension_numbers=('NWC', 'WIO', 'NWC'),
                                 feature_group_count=ch)
    return y + b


def maybe_flip(t, rev):
    return jnp.flip(t, axis=1) if rev else t


def rope_2d(n_lat):
    rows = n_lat // GRID_W
    row = jnp.repeat(jnp.arange(rows), GRID_W).astype(jnp.float32)
    col = jnp.tile(jnp.arange(GRID_W), rows).astype(jnp.float32)
    n_freq = HEAD_DIM // 4
    inv = ROPE_THETA ** (-jnp.arange(n_freq, dtype=jnp.float32) / n_freq)
    ang = jnp.concatenate([row[:, None] * inv, col[:, None] * inv], axis=-1)
    return jnp.cos(ang), jnp.sin(ang)


def apply_rope(x, cos, sin):
    x1, x2 = jnp.split(x, 2, axis=-1)
    c = cos[None, :, None, :].astype(x.dtype)
    s = sin[None, :, None, :].astype(x.dtype)
    return jnp.concatenate([x1 * c - x2 * s, x1 * s + x2 * c], axis=-1)


def linear_scan(a, b, reverse):
    def combine(l, r):
        al, bl = l
        ar, br = r
        return al * ar, ar * bl + br
    _, h = lax.associative_scan(combine, (a, b), axis=1, reverse=reverse)
    return h


def rglru_scan(xc, w_a, b_a, w_x, b_x, lam, h0, reverse):
    bsz, L, W = xc.shape
    xf = xc.astype(jnp.float32)
    xb = xf.reshape(bsz, L, LRU_BLOCKS, LRU_BLOCK)
    r = jax.nn.sigmoid(jnp.einsum('blhi,hij->blhj', xb, w_a.astype(jnp.float32)).reshape(bsz, L, W) + b_a)
    i = jax.nn.sigmoid(jnp.einsum('blhi,hij->blhj', xb, w_x.astype(jnp.float32)).reshape(bsz, L, W) + b_x)
    log_a = -LRU_C * r * jax.nn.softplus(-lam.astype(jnp.float32))
    a = jnp.exp(log_a)
    b = jnp.sqrt(-jnp.expm1(2.0 * log_a)) * (i * xf)
    first = L - 1 if reverse else 0
    b = b.at[:, first].add(a[:, first] * h0)
    h = linear_scan(a, b, reverse)
    h_last = h[:, 0] if reverse else h[:, -1]
    return h, h_last


def rglru_mixer(xc, xl, conv_w, conv_b, ga_w, ga_b, gx_w, gx_b, lam):
    xc = dwconv_centred(xc, conv_w, conv_b)
    xl = dwconv_centred(xl, conv_w, conv_b)
    bsz = xl.shape[0]
    outs_c, outs_l = [], []
    for d, rev in enumerate((False, True)):
        h0 = jnp.zeros((bsz, LRU_WIDTH), jnp.float32)
        h_c, hc_last = rglru_scan(xc, ga_w[d], ga_b[d], gx_w[d], gx_b[d], lam[d], h0, rev)
        h_l, _ = rglru_scan(xl, ga_w[d], ga_b[d], gx_w[d], gx_b[d], lam[d], hc_last, rev)
        outs_c.append(h_c)
        outs_l.append(h_l)
    return (outs_c[0] + outs_c[1]).astype(xc.dtype), (outs_l[0] + outs_l[1]).astype(xl.dtype)


def window_attention(qc, kc, vc, ql, kl, vl, sink):
    bsz, S = ql.shape[:2]
    C = kc.shape[1]
    nb = S // ATT_BLOCK
    scale = HEAD_DIM ** -0.5
    sink_f = sink.astype(jnp.float32).reshape(ATT_KV_HEADS, ATT_GROUP)
    qb = ql.reshape(bsz, nb, ATT_BLOCK, ATT_KV_HEADS, ATT_GROUP, HEAD_DIM)
    pad = ((0, 0), (ATT_BLOCK, ATT_BLOCK), (0, 0), (0, 0))
    kp = jnp.pad(kl, pad).reshape(bsz, nb + 2, ATT_BLOCK, ATT_KV_HEADS, HEAD_DIM)
    vp = jnp.pad(vl, pad).reshape(bsz, nb + 2, ATT_BLOCK, ATT_KV_HEADS, HEAD_DIM)
    kb = jnp.concatenate([kp[:, :-2], kp[:, 1:-1], kp[:, 2:]], axis=2)
    vb = jnp.concatenate([vp[:, :-2], vp[:, 1:-1], vp[:, 2:]], axis=2)
    s_lat = jnp.einsum('bnqkgd,bnjkd->bnkgqj', qb, kb).astype(jnp.float32) * scale
    s_ctx = jnp.einsum('bnqkgd,bckd->bnkgqc', qb, kc).astype(jnp.float32) * scale
    qpos = jnp.arange(nb)[:, None] * ATT_BLOCK + jnp.arange(ATT_BLOCK)[None]
    kpos = jnp.arange(nb)[:, None] * ATT_BLOCK - ATT_BLOCK + jnp.arange(3 * ATT_BLOCK)[None]
    rel = qpos[:, :, None] - kpos[:, None, :]
    valid = (jnp.abs(rel) <= WINDOW) & (kpos[:, None, :] >= 0) & (kpos[:, None, :] < S)
    s_lat = jnp.where(valid[None, :, None, None], s_lat, -jnp.inf)
    sink_l = jnp.broadcast_to(sink_f[None, None, :, :, None, None], s_lat.shape[:-1] + (1,))
    p = jax.nn.softmax(jnp.concatenate([s_lat, s_ctx, sink_l], axis=-1), axis=-1)
    nk = 3 * ATT_BLOCK
    p_lat = p[..., :nk].astype(vl.dtype)
    p_ctx = p[..., nk:nk + C].astype(vl.dtype)
    o_l = jnp.einsum('bnkgqj,bnjkd->bnqkgd', p_lat, vb) + jnp.einsum('bnkgqc,bckd->bnqkgd', p_ctx, vc)
    o_l = o_l.reshape(bsz, S, ATT_WIDTH)
    qcg = qc.reshape(bsz, C, ATT_KV_HEADS, ATT_GROUP, HEAD_DIM)
    s_cc = jnp.einsum('bqkgd,bckd->bkgqc', qcg, kc).astype(jnp.float32) * scale
    sink_c = jnp.broadcast_to(sink_f[None, :, :, None, None], s_cc.shape[:-1] + (1,))
    pc = jax.nn.softmax(jnp.concatenate([s_cc, sink_c], axis=-1), axis=-1)[..., :C].astype(vc.dtype)
    o_c = jnp.einsum('bkgqc,bckd->bqkgd', pc, vc).reshape(bsz, C, ATT_WIDTH)
    return o_c, o_l


def ssd_chunked(x, dt, A, Bm, Cm, h0):
    bsz, L = x.shape[:2]
    Q = SSD_CHUNK
    nc = L // Q
    G, R = SSD_GROUPS, SSD_HPG
    x = x.astype(jnp.float32).reshape(bsz, nc, Q, G, R, SSD_HEAD_DIM)
    dt = dt.reshape(bsz, nc, Q, G, R)
    Bm = Bm.astype(jnp.float32).reshape(bsz, nc, Q, G, SSD_STATE)
    Cm = Cm.astype(jnp.float32).reshape(bsz, nc, Q, G, SSD_STATE)
    Acs = jnp.cumsum(dt * A.reshape(G, R), axis=2)
    A_last = Acs[:, :, -1]
    seg = Acs[:, :, :, None] - Acs[:, :, None, :]
    tril = jnp.tril(jnp.ones((Q, Q), bool))[:, :, None, None]
    decay = jnp.exp(jnp.where(tril, seg, -jnp.inf))
    CB = jnp.einsum('bcign,bcjgn->bcijg', Cm, Bm)
    Wm = CB[..., None] * decay * dt[:, :, None]
    y_diag = jnp.einsum('bcijgr,bcjgrp->bcigrp', Wm, x)
    w_state = jnp.exp(A_last[:, :, None] - Acs) * dt
    states = jnp.einsum('bcjgn,bcjgrp->bcgrpn', Bm, x * w_state[..., None])
    def step(h, inp):
        dec, st = inp
        return jnp.exp(dec)[..., None, None] * h + st, h
    h_final, h_in = lax.scan(step, h0, (jnp.moveaxis(A_last, 1, 0), jnp.moveaxis(states, 1, 0)))
    h_in = jnp.moveaxis(h_in, 0, 1)
    y_off = jnp.einsum('bcign,bcgrpn->bcigrp', Cm, h_in) * jnp.exp(Acs)[..., None]
    y = (y_diag + y_off).reshape(bsz, L, SSD_HEADS, SSD_HEAD_DIM)
    return y, h_final


def ssd_prep(xbc, conv_w, conv_b):
    xbc = jax.nn.silu(dwconv_centred(xbc, conv_w, conv_b))
    xs, Bm, Cm = jnp.split(xbc, [SSD_WIDTH, SSD_WIDTH + SSD_BC], axis=-1)
    bsz, L = xs.shape[:2]
    return (xs.reshape(bsz, L, SSD_HEADS, SSD_HEAD_DIM),
            Bm.reshape(bsz, L, SSD_GROUPS, SSD_STATE),
            Cm.reshape(bsz, L, SSD_GROUPS, SSD_STATE))


def ssd_mixer(xbc_c, z_c, dt_c, xbc_l, z_l, dt_l, conv_w, conv_b, dt_bias, A_log, D_skip, norm_w):
    xc, Bc, Cc = ssd_prep(xbc_c, conv_w, conv_b)
    xl, Bl, Cl = ssd_prep(xbc_l, conv_w, conv_b)
    bsz = xl.shape[0]
    Df = D_skip.astype(jnp.float32)[:, None]
    y_c = Df * xc.astype(jnp.float32)
    y_l = Df * xl.astype(jnp.float32)
    for d, rev in enumerate((False, True)):
        A = -jnp.exp(A_log[d].astype(jnp.float32))
        dtc = jax.nn.softplus(dt_c.astype(jnp.float32) + dt_bias[d])
        dtl = jax.nn.softplus(dt_l.astype(jnp.float32) + dt_bias[d])
        h0 = jnp.zeros((bsz, SSD_GROUPS, SSD_HPG, SSD_HEAD_DIM, SSD_STATE), jnp.float32)
        yc_d, hc = ssd_chunked(maybe_flip(xc, rev), maybe_flip(dtc, rev), A,
                               maybe_flip(Bc, rev), maybe_flip(Cc, rev), h0)
        yl_d, _ = ssd_chunked(maybe_flip(xl, rev), maybe_flip(dtl, rev), A,
                              maybe_flip(Bl, rev), maybe_flip(Cl, rev), hc)
        y_c = y_c + maybe_flip(yc_d, rev)
        y_l = y_l + maybe_flip(yl_d, rev)
    y_c = y_c.reshape(bsz, -1, SSD_WIDTH).astype(xbc_c.dtype)
    y_l = y_l.reshape(bsz, -1, SSD_WIDTH).astype(xbc_l.dtype)
    return rmsnorm(y_c * jax.nn.silu(z_c), norm_w), rmsnorm(y_l * jax.nn.silu(z_l), norm_w)


def hybrid_layer(ctx_h, x, c, c_ctx, cos, sin, update_ctx,
                 norm_w, ada_w, ada_b, w_in,
                 lru_conv_w, lru_conv_b, lru_ga_w, lru_ga_b, lru_gx_w, lru_gx_b, lru_lambda,
                 att_q_norm, att_k_norm, att_sink,
                 ssd_conv_w, ssd_conv_b, ssd_dt_bias, ssd_A_log, ssd_D, ssd_norm_w, w_out):
    bsz, S = x.shape[:2]
    C = ctx_h.shape[1]
    shift_l, scale_l, gate_l = modulation(c, ada_w, ada_b)
    shift_c, scale_c, gate_c = modulation(c_ctx, ada_w, ada_b)
    u_l = rmsnorm(x, norm_w) * (1.0 + scale_l[:, None]) + shift_l[:, None]
    u_c = rmsnorm(ctx_h, norm_w) * (1.0 + scale_c) + shift_c
    lx_c, lg_c, q_c, k_c, v_c, ag_c, xbc_c, z_c, dt_c = split_cols(u_c @ w_in)
    lx_l, lg_l, q_l, k_l, v_l, ag_l, xbc_l, z_l, dt_l = split_cols(u_l @ w_in)
    lru_c, lru_l = rglru_mixer(lx_c, lx_l, lru_conv_w, lru_conv_b, lru_ga_w, lru_ga_b,
                               lru_gx_w, lru_gx_b, lru_lambda)
    qc = rmsnorm(q_c.reshape(bsz, C, ATT_HEADS, HEAD_DIM), att_q_norm)
    kc = rmsnorm(k_c.reshape(bsz, C, ATT_KV_HEADS, HEAD_DIM), att_k_norm)
    vc = v_c.reshape(bsz, C, ATT_KV_HEADS, HEAD_DIM)
    ql = apply_rope(rmsnorm(q_l.reshape(bsz, S, ATT_HEADS, HEAD_DIM), att_q_norm), cos, sin)
    kl = apply_rope(rmsnorm(k_l.reshape(bsz, S, ATT_KV_HEADS, HEAD_DIM), att_k_norm), cos, sin)
    vl = v_l.reshape(bsz, S, ATT_KV_HEADS, HEAD_DIM)
    att_c, att_l = window_attention(qc, kc, vc, ql, kl, vl, att_sink)
    ssd_c, ssd_l = ssd_mixer(xbc_c, z_c, dt_c, xbc_l, z_l, dt_l, ssd_conv_w, ssd_conv_b,
                             ssd_dt_bias, ssd_A_log, ssd_D, ssd_norm_w)
    mix_l = jnp.concatenate([lru_l * jax.nn.silu(lg_l), att_l * jax.nn.silu(ag_l), ssd_l], axis=-1)
    x_new = x + gate_l[:, None] * (mix_l @ w_out)
    if update_ctx:
        mix_c = jnp.concatenate([lru_c * jax.nn.silu(lg_c), att_c * jax.nn.silu(ag_c), ssd_c], axis=-1)
        ctx_h = ctx_h + gate_c * (mix_c @ w_out)
    return ctx_h, x_new


def setup_inputs(seed: int = 0) -> dict:
    key = jax.random.key(seed)
    ks = jax.random.split(key, 28)
    f32 = jnp.float32
    nrm = lambda k, shape, s: jax.random.normal(k, shape, f32) * s
    a0 = jax.random.uniform(ks[14], (DEPTH, 2, LRU_WIDTH), f32, 0.9, 0.999)
    s_gate = a0 ** (1.0 / LRU_C)
    lru_lambda = jnp.log(s_gate) - jnp.log1p(-s_gate)
    dt0 = jnp.exp(jax.random.uniform(ks[19], (DEPTH, 2, SSD_HEADS), f32, np.log(1e-3), np.log(1e-1)))
    ssd_dt_bias = dt0 + jnp.log(-jnp.expm1(-dt0))
    ssd_A_log = jnp.log(jax.random.uniform(ks[20], (DEPTH, 2, SSD_HEADS), f32, 1.0, 16.0))
    return {
        'x': nrm(ks[0], (BATCH, SEQ, D_MODEL), 1.0),
        'c': nrm(ks[1], (BATCH, D_MODEL), 1.0),
        'ctx': nrm(ks[2], (BATCH, CTX_LEN, D_MODEL), 1.0),
        'c_ctx': nrm(ks[3], (D_MODEL,), 1.0),
        'norm_w': 1.0 + nrm(ks[4], (DEPTH, D_MODEL), 0.02),
        'ada_w': nrm(ks[5], (DEPTH, D_MODEL, 3 * D_MODEL), 0.5 * D_MODEL ** -0.5),
        'ada_b': nrm(ks[6], (DEPTH, 3 * D_MODEL), 0.01),
        'w_in': nrm(ks[7], (DEPTH, D_MODEL, IN_WIDTH), D_MODEL ** -0.5),
        'lru_conv_w': nrm(ks[8], (DEPTH, CONV_W, LRU_WIDTH), CONV_W ** -0.5),
        'lru_conv_b': nrm(ks[9], (DEPTH, LRU_WIDTH), 0.01),
        'lru_ga_w': nrm(ks[10], (DEPTH, 2, LRU_BLOCKS, LRU_BLOCK, LRU_BLOCK), LRU_BLOCK ** -0.5),
        'lru_ga_b': nrm(ks[11], (DEPTH, 2, LRU_WIDTH), 0.01),
        'lru_gx_w': nrm(ks[12], (DEPTH, 2, LRU_BLOCKS, LRU_BLOCK, LRU_BLOCK), LRU_BLOCK ** -0.5),
        'lru_gx_b': nrm(ks[13], (DEPTH, 2, LRU_WIDTH), 0.01),
        'lru_lambda': lru_lambda,
        'att_q_norm': 1.0 + nrm(ks[15], (DEPTH, HEAD_DIM), 0.02),
        'att_k_norm': 1.0 + nrm(ks[16], (DEPTH, HEAD_DIM), 0.02),
        'att_sink': nrm(ks[17], (DEPTH, ATT_HEADS), 0.5),
        'ssd_conv_w': nrm(ks[18], (DEPTH, CONV_W, SSD_CONV_CH), CONV_W ** -0.5),
        'ssd_conv_b': nrm(ks[21], (DEPTH, SSD_CONV_CH), 0.01),
        'ssd_dt_bias': ssd_dt_bias,
        'ssd_A_log': ssd_A_log,
        'ssd_D': 1.0 + nrm(ks[22], (DEPTH, SSD_HEADS), 0.1),
        'ssd_norm_w': 1.0 + nrm(ks[23], (DEPTH, SSD_WIDTH), 0.02),
        'w_out': nrm(ks[24], (DEPTH, MIX_WIDTH, D_MODEL), MIX_WIDTH ** -0.5),
    }


def reference(x, c, ctx, c_ctx, norm_w, ada_w, ada_b, w_in,
              lru_conv_w, lru_conv_b, lru_ga_w, lru_ga_b, lru_gx_w, lru_gx_b, lru_lambda,
              att_q_norm, att_k_norm, att_sink,
              ssd_conv_w, ssd_conv_b, ssd_dt_bias, ssd_A_log, ssd_D, ssd_norm_w, w_out):
    cos, sin = rope_2d(x.shape[1])
    ctx_h = ctx
    for l in range(DEPTH):
        ctx_h, x = hybrid_layer(
            ctx_h, x, c, c_ctx, cos, sin, l < DEPTH - 1,
            norm_w[l], ada_w[l], ada_b[l], w_in[l],
            lru_conv_w[l], lru_conv_b[l], lru_ga_w[l], lru_ga_b[l], lru_gx_w[l], lru_gx_b[l], lru_lambda[l],
            att_q_norm[l], att_k_norm[l], att_sink[l],
            ssd_conv_w[l], ssd_conv_b[l], ssd_dt_bias[l], ssd_A_log[l], ssd_D[l], ssd_norm_w[l], w_out[l])
    return x
```

```python
import contextlib
import numpy as np
import concourse.bass as bass
import concourse.mybir as mybir
from concourse.bass_utils import run_bass_kernel_spmd

F32 = mybir.dt.float32
BF16 = mybir.dt.bfloat16
AF = mybir.ActivationFunctionType
ALU = mybir.AluOpType

SAME_ENGINE_SYNC = True
SEM_LIMIT = 30000
DMA_ROT = 12

D = 2048
T = 2304
NT = 18
CTX = 256
LP = 2307
XPW = 2310
DEPTH = 4
EPS = 1e-6
RNG_T = [(0, 256), (256, 768), (768, 1280), (1280, 1792), (1792, 2304)]
RNG_P = [(0, 512), (512, 1024), (1024, 1536), (1536, 2048), (2048, 2307)]
O_LX, O_LG, O_Q, O_K, O_V, O_AG, O_XBC, O_Z, O_DT = 0, 1024, 2048, 3072, 3328, 3584, 4608, 6144, 7168
PV_NW, PV_ADAB, PV_LCW, PV_LCB, PV_GAB, PV_GXB, PV_LAM, PV_SCW, PV_SCB, PV_QN, PV_KN = 0, 16, 64, 96, 104, 120, 136, 152, 200, 212, 213
NPV = 214
PR_SNW, PR_DTB, PR_ALOG, PR_D, PR_SINK = 0, 1024, 1056, 1088, 1104
NPR = 1112
C_ID, C_ONE, C_LE, C_GE, C_GT, C_LT, C_SW = 0, 128, 256, 384, 512, 640, 768
NCST = 896


def pc(t):
    return t if t < CTX else t + 3


def tile_pc(t):
    return t * 128 if t < 2 else t * 128 + 3


class Buf:
    __slots__ = ("name", "w", "r")

    def __init__(self, name=""):
        self.name = name
        self.w = None
        self.r = {}


class Op:
    __slots__ = ("eng", "fn", "deps", "signal", "is_dma", "dsem", "dval", "cnt", "gid", "tag")


class V:
    __slots__ = ("ap", "bufs")

    def __init__(self, ap, bufs):
        self.ap = ap
        self.bufs = bufs

    def __getitem__(self, k):
        return V(self.ap[k], self.bufs)

    def bitcast(self, dt):
        return V(self.ap.bitcast(dt), self.bufs)

    def rr(self, pat, **kw):
        return V(self.ap.rearrange(pat, **kw), self.bufs)

    def bc(self, axis, n):
        a = self.ap.unsqueeze(axis)
        shp = list(a.shape)
        shp[axis] = n
        return V(a.to_broadcast(shp), self.bufs)

    def w(self, bufs):
        return V(self.ap, bufs)

    @property
    def shape(self):
        return self.ap.shape


class Sched:
    ENG = ("pe", "act", "dve", "pool", "sp")

    def __init__(self, nc):
        self.nc = nc
        self.ops = {e: [] for e in self.ENG}
        self.ndma = {e: 0 for e in self.ENG}
        self.dma_last = {}
        self.gid = 0
        self.pending_dma = []
        self.out_dmas = []
        self.tag = ""
        self.annotate = False

    def _add(self, eng, fn, reads, writes, is_dma, extra=()):
        o = Op()
        o.eng = eng; o.fn = fn; o.signal = False; o.is_dma = is_dma
        o.dsem = None; o.dval = 0; o.cnt = None
        o.gid = self.gid; self.gid += 1
        o.tag = self.tag
        deps = {}
        for b in reads:
            if b.w is not None:
                deps[id(b.w)] = b.w
        for b in writes:
            if b.w is not None:
                deps[id(b.w)] = b.w
            for r in b.r.values():
                deps[id(r)] = r
        for d in extra:
            deps[id(d)] = d
        if is_dma:
            k = self.ndma[eng]
            self.ndma[eng] += 1
            slot = (eng, k % DMA_ROT)
            prev = self.dma_last.get(slot)
            if prev is not None:
                deps[id(prev)] = prev
                o.dval = prev.dval + 16
            else:
                o.dval = 16
            o.dsem = slot
            self.dma_last[slot] = o
            self.pending_dma.append(o)
        deps.pop(id(o), None)
        for b in reads:
            key = ("dma", o.gid) if is_dma else eng
            b.r[key] = o
        for b in writes:
            b.w = o
            b.r = {}
        o.deps = list(deps.values())
        for d in o.deps:
            if not d.is_dma:
                if d.eng != eng or is_dma or (SAME_ENGINE_SYNC and eng != "pe"):
                    d.signal = True
        self.ops[eng].append(o)
        return o

    def op(self, eng, fn, reads=(), writes=()):
        return self._add(eng, fn, reads, writes, False)

    def dma(self, eng, fn, reads=(), writes=()):
        return self._add(eng, fn, reads, writes, True)

    def barrier(self):
        lasts = [self.ops[e][-1] for e in self.ENG if self.ops[e]]
        j = self._add("sp", lambda e: e.nop(), (), (), False, extra=lasts + self.pending_dma)
        self.pending_dma = []
        for e in self.ENG:
            if e != "sp":
                self._add(e, lambda en: en.nop(), (), (), False, extra=[j])

    def emit(self):
        nc = self.nc
        nsig = {}
        for e in self.ENG:
            n = 0
            for o in self.ops[e]:
                if (not o.is_dma) and o.signal:
                    n += 1
                    o.cnt = n
            nsig[e] = n
        with contextlib.ExitStack() as st:
            esems = {}
            for e in self.ENG:
                nsem = nsig[e] // SEM_LIMIT + 1
                esems[e] = [st.enter_context(nc.semaphore(f"s_{e}_{i}")) for i in range(nsem)]
            dsems = {}
            for slot in self.dma_last:
                dsems[slot] = st.enter_context(nc.semaphore(f"d_{slot[0]}_{slot[1]}"))
            block = st.enter_context(nc.Block())
            final = list(self.out_dmas)

            def run(e, eng):
                waited = {}
                for o in self.ops[e]:
                    for d in o.deps:
                        if d.is_dma:
                            sem = dsems[d.dsem]; val = d.dval; key = ("d",) + d.dsem
                        else:
                            if d.eng == e and not o.is_dma and (e == "pe" or not SAME_ENGINE_SYNC):
                                continue
                            c = d.cnt - 1
                            sem = esems[d.eng][c // SEM_LIMIT]; val = c % SEM_LIMIT + 1
                            key = ("e", d.eng, c // SEM_LIMIT)
                        if waited.get(key, 0) >= val:
                            continue
                        waited[key] = val
                        eng.wait_ge(sem, val)
                    ins = o.fn(eng)
                    if self.annotate:
                        ins.annotate(o.tag)
                    if o.is_dma:
                        ins.then_inc(dsems[o.dsem], 16)
                    elif o.signal:
                        c = o.cnt - 1
                        ins.then_inc(esems[e][c // SEM_LIMIT], 1)
                if e == "sp":
                    for d in final:
                        eng.wait_ge(dsems[d.dsem], d.dval)

            @block.tensor
            def _(eng):
                run("pe", eng)

            @block.scalar
            def _(eng):
                run("act", eng)

            @block.vector
            def _(eng):
                run("dve", eng)

            @block.gpsimd
            def _(eng):
                run("pool", eng)

            @block.sync
            def _(eng):
                run("sp", eng)


class KB:
    def __init__(self, nlayers=DEPTH, dbg=False, stop=None):
        self.nlayers = nlayers
        self.dbg = dbg
        self.stop = stop
        self.nc = bass.Bass("TRN2", target_bir_lowering=False)
        self.S = Sched(self.nc)
        self.st = contextlib.ExitStack()
        self.pb = 0

    def din(self, name, shape, dt=F32):
        return V(self.nc.dram_tensor(name, shape, dt, kind="ExternalInput").ap(), [Buf(name)])

    def dscr(self, name, shape, dt=F32, out=False):
        kind = "ExternalOutput" if (out or self.dbg) else "Internal"
        return V(self.nc.dram_tensor(name, shape, dt, kind=kind).ap(), [Buf(name)])

    def sb(self, name, shape, dt=F32):
        t = self.st.enter_context(self.nc.sbuf_tensor("sb_" + name, shape, dt))
        return V(t[tuple(slice(None) for _ in shape)], [Buf(name)])

    def al(self, region, n, dt=BF16, name=""):
        ne = n if dt == BF16 else 2 * n
        lo, hi = self.reg[region]
        cur = self.cur[region]
        if cur % 2:
            cur += 1
        assert cur + ne <= hi, f"arena region {region} overflow: {name} need {ne} at {cur} hi {hi}"
        self.cur[region] = cur + ne
        v = V(self.arena[:, cur:cur + ne], [Buf(name)])
        if dt != BF16:
            v = v.bitcast(dt)
        return v

    def areset(self, *regions):
        for r in regions:
            self.cur[r] = self.reg[r][0]

    def bank(self, pool=None):
        if pool is None:
            b = self.pb
            self.pb = (self.pb + 1) % 8
            return self.psb[b]
        lst = self.bpools[pool]
        i = self.bpos.get(pool, 0)
        self.bpos[pool] = i + 1
        return self.psb[lst[i % len(lst)]]

    def set_pools(self, **kw):
        self.bpools = kw
        self.bpos = {}

    @staticmethod
    def _rb(*xs):
        out = []
        for x in xs:
            if isinstance(x, V):
                out.extend(x.bufs)
        return out

    @staticmethod
    def _a(x):
        return x.ap if isinstance(x, V) else x

    def act(self, out, in_, func, bias=None, scale=1.0, accum=None):
        a = self._a
        kw = {}
        if bias is not None:
            kw["bias"] = a(bias)
        if accum is not None:
            kw["accum_out"] = a(accum)
        sc = a(scale)
        self.S.op("act", lambda e: e.activation(out=out.ap, in_=in_.ap, func=func, scale=sc, **kw),
                  self._rb(in_, bias, scale), self._rb(out, accum))

    def tt(self, out, in0, in1, op, eng="dve"):
        self.S.op(eng, lambda e: e.tensor_tensor(out=out.ap, in0=in0.ap, in1=in1.ap, op=op),
                  self._rb(in0, in1), self._rb(out))

    def ts(self, out, in0, s1, op0, s2=None, op1=None, eng="dve"):
        a = self._a
        if op1 is None:
            self.S.op(eng, lambda e: e.tensor_scalar(out=out.ap, in0=in0.ap, scalar1=a(s1), scalar2=None, op0=op0),
                      self._rb(in0, s1), self._rb(out))
        else:
            self.S.op(eng, lambda e: e.tensor_scalar(out=out.ap, in0=in0.ap, scalar1=a(s1), scalar2=a(s2), op0=op0, op1=op1),
                      self._rb(in0, s1, s2), self._rb(out))

    def stt(self, out, in0, scalar, in1, op0, op1):
        a = self._a
        self.S.op("dve", lambda e: e.scalar_tensor_tensor(out=out.ap, in0=in0.ap, scalar=a(scalar), in1=in1.ap, op0=op0, op1=op1),
                  self._rb(in0, scalar, in1), self._rb(out))

    def copy(self, out, in_, eng="dve"):
        if eng == "act":
            self.act(out, in_, AF.Copy)
        else:
            self.S.op(eng, lambda e: e.tensor_copy(out=out.ap, in_=in_.ap), self._rb(in_), self._rb(out))

    def memset(self, out, val, eng="pool"):
        self.S.op(eng, lambda e: e.memset(out.ap, val), (), self._rb(out))

    def recip(self, out, in_):
        self.S.op("dve", lambda e: e.reciprocal(out=out.ap, in_=in_.ap), self._rb(in_), self._rb(out))

    def scan(self, out, a, b, init):
        i = self._a(init)
        self.S.op("dve", lambda e: e.tensor_tensor_scan(out=out.ap, data0=a.ap, data1=b.ap, initial=i, op0=ALU.mult, op1=ALU.add),
                  self._rb(a, b, init), self._rb(out))

    def mm(self, out, pairs):
        n = len(pairs)
        rd = []
        for l, r in pairs:
            rd += l.bufs + r.bufs

        def fn(e):
            ins = None
            for i, (l, r) in enumerate(pairs):
                ins = e.matmul(out.ap, lhsT=l.ap, rhs=r.ap, start=(i == 0), stop=(i == n - 1))
            return ins
        self.S.op("pe", fn, rd, self._rb(out))

    def mm1(self, out, lhsT, rhs, start, stop):
        self.S.op("pe", lambda e: e.matmul(out.ap, lhsT=lhsT.ap, rhs=rhs.ap, start=start, stop=stop),
                  self._rb(lhsT, rhs), self._rb(out))

    def tr(self, out, in_, ident):
        self.S.op("pe", lambda e: e.transpose(out=out.ap, in_=in_.ap, identity=ident.ap),
                  self._rb(in_, ident), self._rb(out))

    def dma(self, out, in_, q="sp"):
        return self.S.dma(q, lambda e: e.dma_start(out=out.ap, in_=in_.ap), self._rb(in_), self._rb(out))

    def build(self):
        nc = self.nc
        L = self.nlayers
        self.x_d = self.din("x", [2048, D])
        self.ctx_d = self.din("ctx", [CTX, D])
        self.cc_d = self.din("cc", [128, 32])
        self.pv_d = self.din("pv", [DEPTH, 128, NPV])
        self.pr_d = self.din("pr", [DEPTH, NPR])
        self.cst_d = self.din("cst", [128, NCST])
        self.rope_d = self.din("rope", [128, 4096])
        self.adaw_d = self.din("ada_w", [DEPTH, D, 3 * D])
        self.win_d = self.din("w_in", [DEPTH, D, 7184])
        self.wout_d = self.din("w_out", [DEPTH, 3072, D])
        self.gaw_d = self.din("lru_ga_w", [DEPTH, 2, 16, 64, 64])
        self.gxw_d = self.din("lru_gx_w", [DEPTH, 2, 16, 64, 64])
        self.out_d = self.dscr("out", [2048, D], out=True)
        self.xT_d = self.dscr("xT_s", [16, 128, T])
        self.xT_p = [self.xT_d[m].w([Buf(f"xT{m}")]) for m in range(16)]
        self.mixT_d = self.dscr("mixT_s", [24, 128, T], BF16)
        self.mixT_p = [self.mixT_d[k].w([Buf(f"mixT{k}")]) for k in range(24)]
        self.yf_d = self.dscr("yf_s", [NT, 128, 1024])
        self.sz_d = self.dscr("sz_s", [NT, 128, 1024], BF16)
        if self.dbg:
            self.uT_dbg = self.dscr("uT_dbg", [16, 128, T], BF16)

        self.cst = self.sb("cst", [128, NCST])
        self.cstb = self.sb("cstb", [128, NCST], BF16)
        self.pvt = self.sb("pvt", [128, NPV])
        self.prt = self.sb("prt", [128, NPR])
        self.cc = self.sb("cc", [128, 32])
        self.scb = self.sb("scb", [128, 32], BF16)
        self.sm = self.sb("sm", [128, 512])
        self.ssm = self.sb("ssm", [128, 1440])
        arena_t = self.st.enter_context(nc.sbuf_tensor("arena", [128, 95872], BF16))
        self.arena = arena_t[:, :]
        self.reg = {"R": (0, 32768), "U": (32768, 69632), "T": (69632, 95872), "UT": (32768, 95872), "ALL": (0, 95872)}
        self.cur = {k: v[0] for k, v in self.reg.items()}
        ps_t = self.st.enter_context(nc.psum_tensor("ps", [128, 8, 512], F32))
        self.psb = [V(ps_t[:, b, :], [Buf(f"ps{b}")]) for b in range(8)]

        self.ident = self.cst[:, C_ID:C_ID + 128]
        self.ones = self.cst[:, C_ONE:C_ONE + 128]
        self.identb = self.cstb[:, C_ID:C_ID + 128]
        self.onesb = self.cstb[:, C_ONE:C_ONE + 128]

        self.dma(self.cst, self.cst_d)
        self.dma(self.cc, self.cc_d)
        self.copy(self.cstb, self.cst, eng="dve")
        self.memset(self.sm[:, 500:501], -0.5)
        self.memset(self.sm[:, 501:502], 0.5)
        self.nhalf = self.sm[:, 500:501]
        self.memset(self.sm[:, 502:503], EPS)
        self.memset(self.sm[:, 503:504], 0.25)
        self.epsc = self.sm[:, 502:503]
        self.quart = self.sm[:, 503:504]
        self.phalf = self.sm[:, 501:502]
        th = self.sm[:, 440:472]
        hf = self.sm[:, 400:432]
        self.act(th, self.cc, AF.Tanh, scale=0.5)
        self.ts(hf, self.cc, 0.5, ALU.mult)
        self.stt(self.scb, th, 1.0, hf, ALU.add, ALU.mult)

        self.S.tag = "init"
        self.phase_init()
        for l in range(L):
            self.layer(l)
            if self.stop is not None and self.stop[0] == l:
                break
        if self.stop is None:
            self.phase_final()
        self.S.barrier()
        self.S.emit()
        return nc

    def phase_init(self):
        self.areset("ALL")
        xin = [self.al("ALL", D, F32, f"xin{i}") for i in range(2)]
        xo = [self.al("ALL", D, F32, f"xo{i}") for i in range(2)]
        for t in range(NT):
            src = self.ctx_d[t * 128:(t + 1) * 128, :] if t < 2 else self.x_d[(t - 2) * 128:(t - 1) * 128, :]
            xi = xin[t % 2]
            xoo = xo[t % 2]
            self.dma(xi, src)
            for q in range(4):
                bk = self.bank()
                for i in range(4):
                    m = 4 * q + i
                    self.tr(bk[:, i * 128:(i + 1) * 128], xi[:, m * 128:(m + 1) * 128], self.ident)
                self.copy(xoo[:, q * 512:(q + 1) * 512], bk, eng=("act" if q % 2 else "dve"))
            dst = V(self.xT_d.ap[:, :, t * 128:(t + 1) * 128].rearrange("m p t -> p m t"), sum([p.bufs for p in self.xT_p], []))
            self.dma(dst, xoo.rr("p (m t) -> p m t", t=128))
        self.S.barrier()

    def phase_final(self):
        self.S.barrier()
        self.areset("ALL")
        xc = [self.al("ALL", 2048, F32, f"fx{i}") for i in range(3)]
        ob = self.al("ALL", 16 * D, F32, "fob")
        ob3 = ob.rr("p (t d) -> p t d", d=D)
        for m in range(16):
            xm = xc[m % 3]
            self.dma(xm, self.xT_p[m][:, CTX:T])
            for q in range(4):
                bk = self.bank()
                for i in range(4):
                    t = 4 * q + i
                    self.tr(bk[:, i * 128:(i + 1) * 128], xm[:, t * 128:(t + 1) * 128], self.ident)
                self.copy(ob3[:, 4 * q:4 * q + 4, m * 128:(m + 1) * 128], bk.rr("p (t d) -> p t d", d=128),
                          eng=("act" if q % 2 else "dve"))
        for t in range(16):
            o = self.dma(self.out_d[t * 128:(t + 1) * 128, :], ob3[:, t, :])
            self.S.out_dmas.append(o)

    def layer(self, l):
        st = self.stop[1] if (self.stop is not None and self.stop[0] == l) else None
        self.S.barrier()
        self.S.tag = "mod"
        self.load_params(l)
        self.phase_mod(l)
        self.S.barrier()
        self.S.tag = "norm"
        self.phase_norm(l)
        if st == "norm":
            return
        self.S.barrier()
        self.S.tag = "lru"
        self.phase_lru(l)
        if st == "lru":
            return
        self.S.barrier()
        self.S.tag = "att"
        self.phase_att(l)
        if st == "att":
            return
        self.S.barrier()
        self.S.tag = "ssd"
        self.phase_ssd(l)
        if st == "ssd":
            return
        self.S.barrier()
        self.S.tag = "out"
        self.phase_out(l)

    def load_params(self, l):
        self.dma(self.pvt, self.pv_d[l])
        self.dma(self.prt, V(self.pr_d.ap[l:l + 1, :].to_broadcast([128, NPR]), self.pr_d.bufs))
        sm = self.sm
        pv = self.pvt
        self.ch = sm[:, 0:16]; self.hba = sm[:, 16:32]; self.hbx = sm[:, 32:48]
        e1 = sm[:, 48:64]
        self.act(e1, pv[:, PV_LAM:PV_LAM + 16], AF.Exp, scale=-1.0)
        self.act(e1, e1, AF.Ln, bias=1.0)
        self.ts(self.ch, e1, -4.0, ALU.mult)
        self.ts(self.hba, pv[:, PV_GAB:PV_GAB + 16], 0.5, ALU.mult)
        self.ts(self.hbx, pv[:, PV_GXB:PV_GXB + 16], 0.5, ALU.mult)
        self.scwh = sm[:, 64:112]; self.scbh = sm[:, 112:124]
        self.ts(self.scwh, pv[:, PV_SCW:PV_SCW + 48], 0.5, ALU.mult)
        self.ts(self.scbh, pv[:, PV_SCB:PV_SCB + 12], 0.5, ALU.mult)
        self.negA = sm[:, 124:156]; self.esink = sm[:, 156:164]
        self.act(self.negA, self.prt[:, PR_ALOG:PR_ALOG + 32], AF.Exp)
        self.ts(self.negA, self.negA, -1.0, ALU.mult)
        self.act(self.esink, self.prt[:, PR_SINK:PR_SINK + 8], AF.Exp)
        self.modL = sm[:, 164:212]; self.modC = sm[:, 212:260]
        self.gL = sm[:, 260:276]; self.gC = sm[:, 276:292]

    def phase_mod(self, l):
        self.areset("ALL")
        wt = [self.al("ALL", 16 * 512, BF16, f"adaw{i}") for i in range(3)]
        bk = self.bank()
        for ct4 in range(12):
            w = wt[ct4 % 3]
            w3 = w.rr("p (k c) -> p k c", c=512)
            src = V(self.adaw_d.ap[l, :, ct4 * 512:(ct4 + 1) * 512].rearrange("(k p) c -> p k c", p=128), self.adaw_d.bufs)
            self.dma(w3, src, q="pool")
            for sub in range(4):
                ct = ct4 * 4 + sub
                self.mm(bk[:, 2 * ct:2 * ct + 2],
                        [(w3[:, k, sub * 128:(sub + 1) * 128], self.scb[:, k:32:16]) for k in range(16)])
        adab = self.pvt[:, PV_ADAB:PV_ADAB + 48]
        self.tt(self.modL, bk[:, 0:96:2], adab, ALU.add)
        self.tt(self.modC, bk[:, 1:96:2], adab, ALU.add)
        nw = self.pvt[:, PV_NW:PV_NW + 16]
        self.stt(self.gL, self.modL[:, 16:32], 1.0, nw, ALU.add, ALU.mult)
        self.stt(self.gC, self.modC[:, 16:32], 1.0, nw, ALU.add, ALU.mult)

    def phase_norm(self, l):
        self.areset("R", "U", "T")
        self.uT = self.al("U", 16 * T, BF16, "uT").rr("p (k t) -> p k t", t=T)
        xb = [self.al("R", T, F32, f"nx{i}") for i in range(3)]
        sq = [self.al("R", T, F32, f"nsq{i}") for i in range(2)]
        rstd = self.al("R", T, F32, "rstd")
        tmp = [self.al("T", T, F32, f"ntmp{i}") for i in range(2)]
        banks = [self.bank() for _ in range(5)]
        for m in range(16):
            x = xb[m % 3]
            s = sq[m % 2]
            self.dma(x, self.xT_p[m])
            self.act(s, x, AF.Square)
            for r, (a, b) in enumerate(RNG_T):
                self.mm1(banks[r][:, 0:b - a], self.ones, s[:, a:b], m == 0, m == 15)
        for r, (a, b) in enumerate(RNG_T):
            self.act(rstd[:, a:b], banks[r][:, 0:b - a], AF.Ln, bias=self.epsc, scale=1.0 / D)
        self.act(rstd, rstd, AF.Exp, scale=-0.5)
        shL = self.modL[:, 0:16]; shC = self.modC[:, 0:16]
        for m in range(16):
            x = xb[m % 3]
            tp = tmp[m % 2]
            self.dma(x, self.xT_p[m])
            for (a, b, g, sh) in ((0, CTX, self.gC, shC), (CTX, T, self.gL, shL)):
                self.stt(tp[:, a:b], x[:, a:b], g[:, m:m + 1], rstd[:, a:b], ALU.mult, ALU.mult)
                self.act(self.uT[:, m, a:b], tp[:, a:b], AF.Identity, bias=sh[:, m:m + 1])
        if self.dbg:
            self.dma(self.uT_dbg.rr("k p t -> p k t"), self.uT)

    def load_w(self, dst3, l, col0, ncols):
        src = V(self.win_d.ap[l, :, col0:col0 + ncols].rearrange("(k p) c -> p k c", p=128), self.win_d.bufs)
        self.dma(dst3, src, q="pool")

    def proj_fm(self, w3, c0, evac, pool=None):
        for r, (a, b) in enumerate(RNG_T):
            bk = self.bank(pool)
            self.mm(bk[:, 0:b - a], [(w3[:, k, c0:c0 + 128], self.uT[:, k, a:b]) for k in range(16)])
            evac(r, a, b, bk[:, 0:b - a])

    def phase_lru(self, l):
        self.areset("R", "T")
        self.set_pools(proj=[0, 1, 2, 3, 4], gate=[5, 6, 7])
        G = [self.al("R", XPW, F32, f"G{i}") for i in range(7)]
        XP0, LG, XC, TA0, TI0, TA1, TI1 = G
        XP1 = self.al("T", XPW, F32, "XP1")
        XPs = [XP0, XP1]
        TAd = [TA0, TA1]; TId = [TI0, TI1]
        HF = self.al("T", XPW, F32, "HF")
        SS = HF
        XCB = self.al("T", XPW, BF16, "xcb")
        OB = self.al("T", XPW, BF16, "ob")
        wl = [self.al("T", 16 * 256, BF16, f"wl{i}").rr("p (k c) -> p k c", c=256) for i in range(2)]
        gwt = self.al("T", 32 * 128, BF16, "gw")
        gw4 = gwt.rr("p (i c) -> p i c", c=128)
        self.memset(gwt, 0.0)
        for gi, wd in enumerate((self.gaw_d, self.gxw_d)):
            for d in range(2):
                base = (gi * 2 + d) * 8
                for par in range(2):
                    src = V(wd.ap[l, d, par::2, :, :].rearrange("j i o -> i j o"), wd.bufs)
                    dst = gw4[par * 64:(par + 1) * 64, base:base + 8, par * 64:(par + 1) * 64]
                    self.dma(dst, src, q="pool")
        pv = self.pvt
        self.memset(LG[:, 256:259], 0.0)
        for XP in XPs:
            self.memset(XP[:, 0:2], 0.0)
            self.memset(XP[:, 258:261], 0.0)
            self.memset(XP[:, 2309:2310], 0.0)
        for j in range(8):
            w3 = wl[j % 2]
            XP = XPs[j % 2]
            HB = XP
            self.load_w(w3[:, :, 0:128], l, O_LX + j * 128, 128)
            self.load_w(w3[:, :, 128:256], l, O_LG + j * 128, 128)
            if j >= 2:
                self.memset(XP[:, 0:2], 0.0)
                self.memset(XP[:, 258:261], 0.0)
                self.memset(XP[:, 2309:2310], 0.0)

            def ev_x(r, a, b, p):
                self.copy(XP[:, a + 2:b + 2] if a < CTX else XP[:, a + 5:b + 5], p, eng=("act" if r % 2 else "dve"))

            def ev_g(r, a, b, p):
                self.act(LG[:, pc(a):pc(a) + (b - a)], p, AF.Copy, scale=0.5)
            self.proj_fm(w3, 0, ev_x, pool="proj")
            cw = pv[:, PV_LCW:PV_LCW + 32]
            self.act(XC[:, 0:LP], XP[:, 0:LP], AF.Identity, bias=pv[:, PV_LCB + j:PV_LCB + j + 1], scale=cw[:, j:j + 1])
            for k in range(1, 4):
                self.stt(XC[:, 0:LP], XP[:, k:k + LP], cw[:, k * 8 + j:k * 8 + j + 1], XC[:, 0:LP], ALU.mult, ALU.add)
            self.copy(XCB[:, 0:LP], XC[:, 0:LP], eng="dve")
            for d in range(2):
                col = d * 8 + j
                TA = TAd[d]; TI = TId[d]
                for (a, b) in RNG_P:
                    bk = self.bank("gate")
                    self.mm(bk[:, 0:b - a], [(gw4[:, (0 * 2 + d) * 8 + j, :], XCB[:, a:b])])
                    self.act(TA[:, a:b], bk[:, 0:b - a], AF.Tanh, bias=self.hba[:, col:col + 1], scale=0.5)
                    bk2 = self.bank("gate")
                    self.mm(bk2[:, 0:b - a], [(gw4[:, (1 * 2 + d) * 8 + j, :], XCB[:, a:b])])
                    self.act(TI[:, a:b], bk2[:, 0:b - a], AF.Tanh, bias=self.hbx[:, col:col + 1], scale=0.5)
                self.act(TA[:, 0:LP], TA[:, 0:LP], AF.Exp, bias=self.ch[:, col:col + 1], scale=self.ch[:, col:col + 1])
            self.proj_fm(w3, 128, ev_g, pool="proj")
            self.act(HF[:, 0:LP], LG[:, 0:LP], AF.Tanh)
            self.stt(LG[:, 0:LP], HF[:, 0:LP], 1.0, LG[:, 0:LP], ALU.add, ALU.mult)
            for d in range(2):
                TA = TAd[d]; TI = TId[d]
                self.act(SS[:, 0:LP], TA[:, 0:LP], AF.Square)
                self.act(SS[:, 0:LP], SS[:, 0:LP], AF.Sqrt, bias=self.quart, scale=-0.25)
                self.stt(TI[:, 0:LP], TI[:, 0:LP], 1.0, XC[:, 0:LP], ALU.add, ALU.mult)
                self.tt(TI[:, 0:LP], TI[:, 0:LP], SS[:, 0:LP], ALU.mult)
            self.scan(HF[:, 0:CTX], TA0[:, 0:CTX], TI0[:, 0:CTX], 0.0)
            self.scan(HF[:, 259:LP], TA0[:, 259:LP], TI0[:, 259:LP], HF[:, 255:256])
            self.scan(HB[:, 0:CTX][:, ::-1], TA1[:, 0:CTX][:, ::-1], TI1[:, 0:CTX][:, ::-1], 0.0)
            self.scan(HB[:, 259:LP][:, ::-1], TA1[:, 259:LP][:, ::-1], TI1[:, 259:LP][:, ::-1], HB[:, 0:1])
            self.tt(HF[:, 0:LP], HF[:, 0:LP], HB[:, 0:LP], ALU.add)
            self.tt(OB[:, 0:LP], HF[:, 0:LP], LG[:, 0:LP], ALU.mult)
            self.dma(self.mixT_p[j][:, 0:CTX], OB[:, 0:CTX])
            self.dma(self.mixT_p[j][:, CTX:T], OB[:, 259:LP])

    def att_norm_rope(self, RAWp, RAWf, A, Bp, Bf, C, wn, dst):
        self.dma(C[:, 0:2048], self.rope_d[:, 0:2048])
        self.dma(A[:, 0:2048], self.rope_d[:, 2048:4096])
        for r, (a, b) in enumerate(RNG_T):
            self.act(Bp[r], RAWp[r], AF.Square)
            bk = self.bank("ssq")
            self.mm1(bk[:, 0:b - a], self.ones, Bp[r], True, True)
            self.act(Bp[r], bk[:, 0:b - a], AF.Ln, bias=self.epsc, scale=1.0 / 128)
        self.act(Bf, Bf, AF.Exp, scale=-0.5)
        self.stt(RAWf, RAWf, wn, Bf, ALU.mult, ALU.mult)
        self.copy(dst[:, 0:CTX], RAWf[:, 0:CTX], eng="act")
        sw = self.cst[:, C_SW:C_SW + 128]
        for (a, b) in RNG_T[1:]:
            bk = self.bank("swap")
            self.mm1(bk, sw, RAWf[:, a:b], True, True)
            self.tt(A[:, a - CTX:b - CTX], bk, A[:, a - CTX:b - CTX], ALU.mult)
        self.tt(C[:, 0:2048], RAWf[:, CTX:T], C[:, 0:2048], ALU.mult)
        self.tt(dst[:, CTX:T], C[:, 0:2048], A[:, 0:2048], ALU.add)

    def phase_att(self, l):
        self.areset("R", "T")
        Q4 = self.al("R", 4 * T, BF16, "Q4").rr("p (h t) -> p h t", t=T)
        SG4 = self.al("R", 4 * T, BF16, "SG4").rr("p (h t) -> p h t", t=T)
        KT = self.al("R", 2 * T, BF16, "KT").rr("p (g t) -> p g t", t=T)
        VT = self.al("R", NT * 256, BF16, "VT").rr("p (t c) -> p t c", c=256)
        RAWs = [self.al("T", T, F32, "RAW0"), self.al("R", T, F32, "RAW1")]
        off_alias = self.cur["T"]
        A = self.al("T", T, F32, "A"); B = self.al("T", T, F32, "B"); C = self.al("T", T, F32, "C")
        wq = [self.al("T", 16 * 128, BF16, f"wq{i}").rr("p (k c) -> p k c", c=128) for i in range(2)]
        def parts(v):
            ps = [V(v.ap[:, a:b], [Buf(f"part{r}")]) for r, (a, b) in enumerate(RNG_T)]
            return ps, V(v.ap, [p.bufs[0] for p in ps])
        RAWparts = [parts(r) for r in RAWs]
        Bp, Bf = parts(B)
        o = off_alias

        def alias(n, dt, name):
            nonlocal o
            ne = n if dt == BF16 else 2 * n
            v = V(self.arena[:, o:o + ne], [Buf(name)])
            o += ne
            return v if dt == BF16 else v.bitcast(dt)
        PTs = [[alias(512, BF16, f"PT{s_}_{i}") for i in range(5)] for s_ in range(2)]
        DEN = [alias(512, F32, f"DEN{i}") for i in range(2)]
        OO = [alias(512, F32, f"OO{i}") for i in range(2)]
        assert o <= off_alias + 3 * 2 * T
        pv = self.pvt
        GEb = self.cstb[:, C_GE:C_GE + 128]; LEb = self.cstb[:, C_LE:C_LE + 128]
        nw = [0]
        nr = [0]

        def nextw(col0):
            w = wq[nw[0] % 2]
            nw[0] += 1
            self.load_w(w, l, col0, 128)
            return w

        def nextraw():
            r = RAWparts[nr[0] % 2]
            nr[0] += 1
            return r
        scale = float(1.0 / np.sqrt(128.0))
        for g in range(2):
            self.S.barrier()
            self.S.tag = "att:proj"
            self.set_pools(proj=[0, 1, 2, 3], ssq=[4, 5], swap=[6, 7])
            w = nextw(O_K + g * 128)
            RAWp, RAWf = nextraw()

            def ev_raw(r, a, b, p, RAWp=RAWp):
                self.copy(RAWp[r], p, eng=("act" if r % 2 else "dve"))
            self.proj_fm(w, 0, ev_raw, pool="proj")
            self.att_norm_rope(RAWp, RAWf, A, Bp, Bf, C, pv[:, PV_KN:PV_KN + 1], KT[:, g, :])
            w = nextw(O_V + g * 128)
            for t0 in range(0, NT, 4):
                nt = min(4, NT - t0)
                bk = self.bank("proj")
                for i in range(nt):
                    t = t0 + i
                    self.mm(bk[:, i * 128:(i + 1) * 128], [(self.uT[:, k, t * 128:(t + 1) * 128], w[:, k, :]) for k in range(16)])
                self.copy(VT[:, t0:t0 + nt, g * 128:(g + 1) * 128], bk[:, 0:nt * 128].rr("p (t c) -> p t c", c=128), eng="act")
            for h in range(4):
                hh = g * 4 + h
                w = nextw(O_Q + hh * 128)
                RAWp, RAWf = nextraw()

                def ev_raw(r, a, b, p, RAWp=RAWp):
                    self.copy(RAWp[r], p, eng=("act" if r % 2 else "dve"))
                self.proj_fm(w, 0, ev_raw, pool="proj")
                self.att_norm_rope(RAWp, RAWf, A, Bp, Bf, C, pv[:, PV_QN:PV_QN + 1], Q4[:, h, :])
                w = nextw(O_AG + hh * 128)
                RAWp, RAWf = nextraw()

                def ev_ag(r, a, b, p, RAWp=RAWp):
                    self.act(RAWp[r], p, AF.Copy, scale=0.5)
                self.proj_fm(w, 0, ev_ag, pool="proj")
                self.act(Bf, RAWf, AF.Tanh)
                self.stt(SG4[:, h, :], Bf, 1.0, RAWf, ALU.add, ALU.mult)
            es = self.esink[:, g * 4:(g + 1) * 4].bc(2, 128)
            self.S.barrier()
            self.S.tag = "att:blk"
            self.set_pools(st=[0, 1, 2, 3])

            def keys_of(n):
                return [0, 1] + ([j for j in (n - 1, n, n + 1) if 2 <= j < NT] if n >= 2 else [])

            def stageA(n):
                qs = Q4[:, :, n * 128:(n + 1) * 128]
                for idx, j in enumerate(keys_of(n)):
                    sbk = self.bank("st")
                    self.mm(sbk.rr("p (h q) -> p h q", q=128), [(KT[:, g, j * 128:(j + 1) * 128], qs)])
                    pt = PTs[n % 2][idx]
                    self.act(pt, sbk, AF.Exp, scale=scale)
                    if n >= 2 and j >= 2 and j == n - 1:
                        p3 = pt.rr("p (h q) -> p h q", q=128)
                        self.tt(p3, p3, GEb.bc(1, 4), ALU.mult)
                    if n >= 2 and j == n + 1:
                        p3 = pt.rr("p (h q) -> p h q", q=128)
                        self.tt(p3, p3, LEb.bc(1, 4), ALU.mult)

            def stageB(n):
                otb = self.psb[4 + n % 2]; lb = self.psb[6 + n % 2]
                ks = keys_of(n)
                for idx, j in enumerate(ks):
                    pt = PTs[n % 2][idx]
                    self.mm1(otb, VT[:, j, g * 128:(g + 1) * 128], pt, idx == 0, idx == len(ks) - 1)
                    self.mm1(lb, self.onesb, pt, idx == 0, idx == len(ks) - 1)

            def stageC(n):
                otb = self.psb[4 + n % 2]; lb = self.psb[6 + n % 2]
                den = DEN[n % 2]; oo = OO[n % 2]
                self.tt(den.rr("p (h q) -> p h q", q=128), lb.rr("p (h q) -> p h q", q=128), es, ALU.add)
                self.act(den, den, AF.Ln)
                self.act(den, den, AF.Exp, scale=-1.0)
                self.tt(oo, otb, den, ALU.mult)
                sg = SG4[:, :, n * 128:(n + 1) * 128]
                self.tt(sg, oo.rr("p (h q) -> p h q", q=128), sg, ALU.mult)
            for step in range(NT + 2):
                if step < NT:
                    stageA(step)
                if 1 <= step <= NT:
                    stageB(step - 1)
                if step >= 2:
                    stageC(step - 2)
            for h in range(4):
                self.dma(self.mixT_p[8 + g * 4 + h], SG4[:, h, :])

    def phase_ssd(self, l):
        ssm = self.ssm
        DTR = ssm[:, 0:288]
        DTd = [ssm[:, 288:576], ssm[:, 576:864]]
        Ad = [ssm[:, 864:1152], ssm[:, 1152:1440]]
        pv = self.pvt
        self.S.tag = "ssd:z"
        self.areset("R", "T")
        wz = self.al("R", 16 * 1024, BF16, "wz").rr("p (k c) -> p k c", c=1024)
        wdt = self.al("R", 16 * 16, BF16, "wdt").rr("p (k c) -> p k c", c=16)
        ZH = [self.al("T", 1024, F32, f"ZH{i}") for i in range(2)]
        TGz = [self.al("T", 1024, F32, f"TGz{i}") for i in range(2)]
        SZt = [self.al("T", 1024, BF16, f"SZt{i}") for i in range(2)]
        self.load_w(wz[:, :, 0:512], l, O_Z, 512)
        self.load_w(wz[:, :, 512:1024], l, O_Z + 512, 512)
        self.load_w(wdt, l, O_DT, 16)
        self.set_pools(z=[0, 1, 2, 3, 4, 5], dt=[6, 7])
        for t in range(NT):
            zh = ZH[t % 2]; tg = TGz[t % 2]; sz = SZt[t % 2]
            ut = lambda k: self.uT[:, k, t * 128:(t + 1) * 128]
            for half in range(2):
                bk = self.bank("z")
                self.mm(bk, [(ut(k), wz[:, k, half * 512:(half + 1) * 512]) for k in range(16)])
                self.act(zh[:, half * 512:(half + 1) * 512], bk, AF.Copy, scale=0.5)
            bk = self.bank("dt")
            self.mm(bk[:, 0:16], [(ut(k), wdt[:, k, :]) for k in range(16)])
            self.copy(DTR[:, t * 16:(t + 1) * 16], bk[:, 0:16])
            self.act(tg, zh, AF.Tanh)
            self.stt(sz, tg, 1.0, zh, ALU.add, ALU.mult)
            self.dma(self.sz_d[t], sz)
        self.S.barrier()
        self.S.tag = "ssd:prep"
        self.areset("R", "T")
        XTOK = self.al("R", NT * 1024, BF16, "XTOK").rr("p (t c) -> p t c", c=1024)
        BTOK = self.al("R", NT * 256, BF16, "BTOK").rr("p (t c) -> p t c", c=256)
        BT = [self.al("R", LP + 1, BF16, f"BT{g}") for g in range(2)]
        CT = [self.al("R", LP + 1, BF16, f"CT{g}") for g in range(2)]
        XPs = [self.al("T", XPW, F32, "sXP0"), self.al("T", XPW, F32, "sXP1")]
        XH = self.al("T", XPW, F32, "sXH"); TG = self.al("T", XPW, F32, "sTG")
        SIb = self.al("T", LP + 1, BF16, "SIb")
        wl = [self.al("T", 16 * 128, BF16, f"swl{i}").rr("p (k c) -> p k c", c=128) for i in range(2)]
        self.set_pools(proj=[0, 1, 2, 3, 4], tr=[5, 6, 7])
        for XP in XPs:
            self.memset(XP[:, 0:2], 0.0)
            self.memset(XP[:, 258:261], 0.0)
            self.memset(XP[:, 2309:2310], 0.0)
        for cc in range(12):
            w = wl[cc % 2]
            XP = XPs[cc % 2]
            self.load_w(w, l, O_XBC + cc * 128, 128)

            def ev_x(r, a, b, p, XP=XP):
                self.copy(XP[:, a + 2:b + 2] if a < CTX else XP[:, a + 5:b + 5], p, eng=("act" if r % 2 else "dve"))
            self.proj_fm(w, 0, ev_x, pool="proj")
            cw = self.scwh
            self.act(XH[:, 0:LP], XP[:, 0:LP], AF.Identity, bias=self.scbh[:, cc:cc + 1], scale=cw[:, cc:cc + 1])
            for k in range(1, 4):
                self.stt(XH[:, 0:LP], XP[:, k:k + LP], cw[:, k * 12 + cc:k * 12 + cc + 1], XH[:, 0:LP], ALU.mult, ALU.add)
            self.act(TG[:, 0:LP], XH[:, 0:LP], AF.Tanh)
            if cc < 8:
                dest = SIb
            elif cc < 10:
                dest = BT[cc - 8]
            else:
                dest = CT[cc - 10]
            self.stt(dest[:, 0:LP], TG[:, 0:LP], 1.0, XH[:, 0:LP], ALU.add, ALU.mult)
            if cc < 10:
                for t0 in range(0, NT, 8):
                    nt = min(8, NT - t0)
                    bkb = self.bank("tr").bitcast(BF16)
                    for i in range(nt):
                        tp = tile_pc(t0 + i)
                        self.tr(bkb[:, i * 128:(i + 1) * 128], dest[:, tp:tp + 128], self.identb)
                    if cc < 8:
                        dd = XTOK[:, t0:t0 + nt, cc * 128:(cc + 1) * 128]
                    else:
                        dd = BTOK[:, t0:t0 + nt, (cc - 8) * 128:(cc - 7) * 128]
                    self.copy(dd, bkb[:, 0:nt * 128].rr("p (t c) -> p t c", c=128), eng=("act" if (t0 // 8) % 2 else "dve"))
        for d in range(2):
            d3 = DTd[d].rr("p (t h) -> p t h", h=16)
            self.tt(d3, DTR.rr("p (t h) -> p t h", h=16), self.prt[:, PR_DTB + d * 16:PR_DTB + (d + 1) * 16].bc(1, NT), ALU.add)
            self.act(DTd[d], DTd[d], AF.Exp)
            self.act(DTd[d], DTd[d], AF.Ln, bias=1.0)
            self.tt(Ad[d].rr("p (t h) -> p t h", h=16), d3, self.negA[:, d * 16:(d + 1) * 16].bc(1, NT), ALU.mult)
        self.S.barrier()
        self.S.tag = "ssd:loop"
        self.areset("UT")
        A_ = lambda n, dt, nm: [self.al("UT", n, dt, f"{nm}{i}") for i in range(2)]
        RSEG = A_(2048, F32, "RSEG"); E = A_(2048, BF16, "E"); WMT = A_(2048, BF16, "WMT"); CBM = A_(256, BF16, "CBM")
        XDT = A_(1024, BF16, "XDT"); XW = A_(1024, BF16, "XW"); HBF = A_(1024, BF16, "HBF")
        H = self.al("UT", 1024, F32, "H")
        SMALL = A_(128, F32, "SSM")
        YO = A_(1024, F32, "YO"); Y = A_(1024, F32, "Y"); SZl = A_(1024, BF16, "SZl"); YFL = A_(1024, F32, "YFL")
        YN = A_(1024, BF16, "YN"); MTt = A_(1024, BF16, "MTt")
        Dsk = self.prt[:, PR_D:PR_D + 16]
        snw = self.prt[:, PR_SNW:PR_SNW + 1024]
        self.set_pools(seg=[0, 1, 2, 3], ac=[4], rot=[5, 6, 7])
        mixT_ssd_bufs = sum([p.bufs for p in self.mixT_p[16:24]], [])
        it = 0
        for d in range(2):
            order = list(range(NT)) if d == 0 else [1, 0] + list(range(NT - 1, 1, -1))
            M2 = self.cst[:, C_LE:C_LE + 128] if d == 0 else self.cst[:, C_GE:C_GE + 128]
            M1 = self.cst[:, C_GT:C_GT + 128] if d == 0 else self.cst[:, C_LT:C_LT + 128]
            MK = M2
            self.memset(H, 0.0)
            self.memset(HBF[it % 2], 0.0)
            for c in order:
                i2 = it % 2
                tp = tile_pc(c)
                a_c = Ad[d][:, c * 16:(c + 1) * 16]
                dt_c = DTd[d][:, c * 16:(c + 1) * 16]
                rseg = RSEG[i2]; e_ = E[i2]; wmt = WMT[i2]; cbm = CBM[i2]; xdt = XDT[i2]; xw = XW[i2]
                hbf = HBF[i2]; hbf_n = HBF[(it + 1) % 2]; sml = SMALL[i2]; yo = YO[i2]; y = Y[i2]
                self.tt(rseg.rr("p (h i) -> p h i", i=128), M2.bc(1, 16), a_c.bc(2, 128), ALU.mult, eng="pool")
                for q in range(4):
                    bk = self.bank("seg")
                    self.mm1(bk, M1, rseg[:, q * 512:(q + 1) * 512], True, True)
                    self.act(e_[:, q * 512:(q + 1) * 512], bk, AF.Exp)
                bkA = self.bank("ac")
                self.mm1(bkA[:, 0:16], M2, a_c, True, True)
                self.mm1(bkA[:, 16:32], self.ones, a_c, True, True)
                ACS = sml[:, 0:32]; EA = sml[:, 32:48]; DEC = sml[:, 48:64]; WS = sml[:, 64:80]
                self.copy(ACS, bkA[:, 0:32])
                self.act(sml[:, 32:64], ACS, AF.Exp)
                self.tt(WS, ACS[:, 16:32], ACS[:, 0:16], ALU.subtract)
                self.act(WS, WS, AF.Exp)
                bkC = bkA
                for g in range(2):
                    self.mm1(bkC[:, 256 + g * 128:256 + (g + 1) * 128], BT[g][:, tp:tp + 128], CT[g][:, tp:tp + 128], True, True)
                self.tt(cbm.rr("p (g i) -> p g i", i=128), bkC[:, 256:512].rr("p (g i) -> p g i", i=128), MK.bc(1, 2), ALU.mult)
                self.tt(wmt.rr("p (g r i) -> p g r i", g=2, i=128), e_.rr("p (g r i) -> p g r i", g=2, i=128),
                        cbm.rr("p (g i) -> p g i", i=128).bc(2, 8), ALU.mult)
                x3 = XTOK[:, c, :].rr("p (h q) -> p h q", q=64)
                self.tt(xdt.rr("p (h q) -> p h q", q=64), x3, dt_c.bc(2, 64), ALU.mult, eng="pool")
                self.tt(xw.rr("p (h q) -> p h q", q=64), xdt.rr("p (h q) -> p h q", q=64), WS.bc(2, 64), ALU.mult, eng="pool")
                for g in range(2):
                    bkO = self.bank("rot")
                    self.mm1(bkO, CT[g][:, tp:tp + 128], hbf[:, g * 512:(g + 1) * 512], True, True)
                    self.tt(yo[:, g * 512:(g + 1) * 512].rr("p (h q) -> p h q", q=64), bkO.rr("p (h q) -> p h q", q=64),
                            EA[:, g * 8:(g + 1) * 8].bc(2, 64), ALU.mult)
                w3 = wmt.rr("p (h i) -> p h i", i=128)
                xd3 = xdt.rr("p (h q) -> p h q", q=64)
                for g in range(2):
                    bkY = self.bank("rot")
                    for r in range(8):
                        h = g * 8 + r
                        self.mm1(bkY[:, r * 64:(r + 1) * 64], w3[:, h, :], xd3[:, h, :], True, True)
                    self.tt(y[:, g * 512:(g + 1) * 512], bkY, yo[:, g * 512:(g + 1) * 512], ALU.add)
                self.tt(H.rr("p (h q) -> p h q", q=64), H.rr("p (h q) -> p h q", q=64), DEC.bc(2, 64), ALU.mult)
                for g in range(2):
                    bkH = self.bank("rot")
                    self.mm1(bkH, BTOK[:, c, g * 128:(g + 1) * 128], xw[:, g * 512:(g + 1) * 512], True, True)
                    self.tt(H[:, g * 512:(g + 1) * 512], bkH, H[:, g * 512:(g + 1) * 512], ALU.add)
                self.copy(hbf_n, H, eng="act")
                if d == 0:
                    self.tt(yo.rr("p (h q) -> p h q", q=64), x3, Dsk.bc(2, 64), ALU.mult)
                    self.tt(y, y, yo, ALU.add)
                    self.dma(self.yf_d[c], y)
                else:
                    yfl = YFL[i2]; szl = SZl[i2]; yn = YN[i2]; mtt = MTt[i2]
                    self.dma(yfl, self.yf_d[c])
                    self.dma(szl, self.sz_d[c])
                    self.tt(y, y, yfl, ALU.add)
                    self.tt(y, y, szl, ALU.mult)
                    SSQ = sml[:, 80:81]; RS = sml[:, 81:82]
                    self.act(yo, y, AF.Square, accum=SSQ)
                    self.ts(RS, SSQ, 1.0 / 1024, ALU.mult, EPS, ALU.add)
                    self.tt(RS, RS, self.nhalf, ALU.pow, eng="pool")
                    self.stt(yn, y, RS, snw, ALU.mult, ALU.mult)
                    bkb = self.bank("rot").bitcast(BF16)
                    for k in range(8):
                        self.tr(bkb[:, k * 128:(k + 1) * 128], yn[:, k * 128:(k + 1) * 128], self.identb)
                    self.copy(mtt, bkb, eng="act")
                    dst = V(self.mixT_d.ap[16:24, :, c * 128:(c + 1) * 128].rearrange("k p t -> p k t"), mixT_ssd_bufs)
                    self.dma(dst, mtt.rr("p (k t) -> p k t", t=128))
                it += 1

    def phase_out(self, l):
        self.areset("ALL")
        W = self.al("ALL", 24 * D, BF16, "wout").rr("p (k d) -> p k d", d=D)
        Wq = [V(W.ap[:, :, q * 512:(q + 1) * 512], [Buf(f"Wq{q}")]) for q in range(4)]
        for q in range(4):
            for k in range(24):
                self.dma(Wq[q][:, k, :], self.wout_d[l, k * 128:(k + 1) * 128, q * 512:(q + 1) * 512], q="pool")
        MT = [self.al("ALL", 24 * 512, BF16, f"MT{i}").rr("p (k t) -> p k t", t=512) for i in range(2)]
        XI = [self.al("ALL", 512, F32, f"XI{i}") for i in range(3)]
        XO = [self.al("ALL", 512, F32, f"XO{i}") for i in range(3)]
        all_mix = sum([p.bufs for p in self.mixT_p], [])
        i = 0
        for r, (a, b) in enumerate(RNG_T):
            if r == 0 and l == DEPTH - 1:
                continue
            n = b - a
            gate = self.modC[:, 32:48] if r == 0 else self.modL[:, 32:48]
            mt = MT[r % 2]
            self.dma(mt[:, :, 0:n], V(self.mixT_d.ap[:, :, a:b].rearrange("k p t -> p k t"), all_mix))
            for m in range(16):
                bk = self.bank()
                self.mm(bk[:, 0:n], [(Wq[m // 4][:, k, (m % 4) * 128:(m % 4 + 1) * 128], mt[:, k, 0:n]) for k in range(24)])
                xi = XI[i % 3]; xo = XO[i % 3]
                i += 1
                self.dma(xi[:, 0:n], self.xT_p[m][:, a:b])
                self.stt(xo[:, 0:n], bk[:, 0:n], gate[:, m:m + 1], xi[:, 0:n], ALU.mult, ALU.add)
                self.dma(self.xT_p[m][:, a:b], xo[:, 0:n])


def _consts():
    i = np.arange(128)
    a = i[:, None]; b = i[None, :]
    cst = np.zeros((128, NCST), np.float32)
    cst[:, C_ID:C_ID + 128] = (a == b)
    cst[:, C_ONE:C_ONE + 128] = 1.0
    cst[:, C_LE:C_LE + 128] = (a <= b)
    cst[:, C_GE:C_GE + 128] = (a >= b)
    cst[:, C_GT:C_GT + 128] = (a > b)
    cst[:, C_LT:C_LT + 128] = (a < b)
    cst[:, C_SW:C_SW + 128] = (a == ((b + 64) % 128))
    t = np.arange(2048)
    row = (t // 64).astype(np.float32); colp = (t % 64).astype(np.float32)
    inv = (np.float32(10000.0) ** (-np.arange(32, dtype=np.float32) / np.float32(32))).astype(np.float32)
    ang = np.concatenate([row[:, None] * inv[None, :], colp[:, None] * inv[None, :]], axis=-1).astype(np.float32)
    cos = np.cos(ang).astype(np.float32).T; sin = np.sin(ang).astype(np.float32).T
    rope = np.zeros((128, 4096), np.float32)
    rope[0:64, 0:2048] = cos; rope[64:128, 0:2048] = cos
    rope[0:64, 2048:4096] = -sin; rope[64:128, 2048:4096] = sin
    return cst, rope


def _fm(v, n):
    return np.ascontiguousarray(v.reshape(n, 128).T)


def _layout_params(I):
    pv = np.zeros((DEPTH, 128, NPV), np.float32)
    pr = np.zeros((DEPTH, NPR), np.float32)
    for l in range(DEPTH):
        pv[l, :, PV_NW:PV_NW + 16] = _fm(I["norm_w"][l], 16)
        pv[l, :, PV_ADAB:PV_ADAB + 48] = _fm(I["ada_b"][l], 48)
        for k in range(4):
            pv[l, :, PV_LCW + k * 8:PV_LCW + k * 8 + 8] = _fm(I["lru_conv_w"][l, k], 8)
            pv[l, :, PV_SCW + k * 12:PV_SCW + k * 12 + 12] = _fm(I["ssd_conv_w"][l, k], 12)
        pv[l, :, PV_LCB:PV_LCB + 8] = _fm(I["lru_conv_b"][l], 8)
        for d in range(2):
            pv[l, :, PV_GAB + d * 8:PV_GAB + d * 8 + 8] = _fm(I["lru_ga_b"][l, d], 8)
            pv[l, :, PV_GXB + d * 8:PV_GXB + d * 8 + 8] = _fm(I["lru_gx_b"][l, d], 8)
            pv[l, :, PV_LAM + d * 8:PV_LAM + d * 8 + 8] = _fm(I["lru_lambda"][l, d], 8)
        pv[l, :, PV_SCB:PV_SCB + 12] = _fm(I["ssd_conv_b"][l], 12)
        pv[l, :, PV_QN] = I["att_q_norm"][l]
        pv[l, :, PV_KN] = I["att_k_norm"][l]
        pr[l, PR_SNW:PR_SNW + 1024] = I["ssd_norm_w"][l]
        pr[l, PR_DTB:PR_DTB + 32] = I["ssd_dt_bias"][l].reshape(32)
        pr[l, PR_ALOG:PR_ALOG + 32] = I["ssd_A_log"][l].reshape(32)
        pr[l, PR_D:PR_D + 16] = I["ssd_D"][l]
        pr[l, PR_SINK:PR_SINK + 8] = I["att_sink"][l]
    return pv, pr


def make_in_maps(I, cores):
    cst, rope = _consts()
    pv, pr = _layout_params(I)
    f = lambda a: np.ascontiguousarray(a, dtype=np.float32)
    shared = dict(pv=pv, pr=pr, cst=cst, rope=rope, ada_w=f(I["ada_w"]), w_in=f(I["w_in"]), w_out=f(I["w_out"]),
                  lru_ga_w=f(I["lru_ga_w"]), lru_gx_w=f(I["lru_gx_w"]))
    maps = []
    for b in cores:
        cc = np.concatenate([_fm(f(I["c"][b]), 16), _fm(f(I["c_ctx"]), 16)], axis=1)
        m = dict(shared)
        m.update(x=f(I["x"][b]), ctx=f(I["ctx"][b]), cc=np.ascontiguousarray(cc))
        maps.append(m)
    return maps


_NC_CACHE = {}


def kernel(**inputs):
    if "nc" not in _NC_CACHE:
        kb = KB()
        _NC_CACHE["nc"] = kb.build()
    nc = _NC_CACHE["nc"]
    maps = make_in_maps(inputs, list(range(8)))
    res = run_bass_kernel_spmd(nc, maps, core_ids=list(range(8)))
    out = np.stack([np.asarray(res.results[b]["out"], dtype=np.float32) for b in range(8)], axis=0)
    return out
```

```python
import contextlib
import numpy as np
import concourse.bass as bass
import concourse.mybir as mybir
from concourse.bass_utils import run_bass_kernel_spmd

F32 = mybir.dt.float32
BF16 = mybir.dt.bfloat16
AF = mybir.ActivationFunctionType
ALU = mybir.AluOpType

SAME_ENGINE_SYNC = True
SEM_LIMIT = 30000
DMA_ROT = 12

D = 2048
T = 2304
NT = 18
CTX = 256
LP = 2307
XPW = 2310
DEPTH = 4
EPS = 1e-6
RNG_T = [(0, 256), (256, 768), (768, 1280), (1280, 1792), (1792, 2304)]
RNG_P = [(0, 512), (512, 1024), (1024, 1536), (1536, 2048), (2048, 2307)]
O_LX, O_LG, O_Q, O_K, O_V, O_AG, O_XBC, O_Z, O_DT = 0, 1024, 2048, 3072, 3328, 3584, 4608, 6144, 7168
PV_NW, PV_ADAB, PV_LCW, PV_LCB, PV_GAB, PV_GXB, PV_LAM, PV_SCW, PV_SCB, PV_QN, PV_KN = 0, 16, 64, 96, 104, 120, 136, 152, 200, 212, 213
NPV = 214
PR_SNW, PR_DTB, PR_ALOG, PR_D, PR_SINK = 0, 1024, 1056, 1088, 1104
NPR = 1112
C_ID, C_ONE, C_LE, C_GE, C_GT, C_LT, C_SW = 0, 128, 256, 384, 512, 640, 768
NCST = 896


def pc(t):
    return t if t < CTX else t + 3


def tile_pc(t):
    return t * 128 if t < 2 else t * 128 + 3


class Buf:
    __slots__ = ("name", "w", "r")

    def __init__(self, name=""):
        self.name = name
        self.w = None
        self.r = {}


class Op:
    __slots__ = ("eng", "fn", "deps", "signal", "is_dma", "dsem", "dval", "cnt", "gid", "tag")


class V:
    __slots__ = ("ap", "bufs")

    def __init__(self, ap, bufs):
        self.ap = ap
        self.bufs = bufs

    def __getitem__(self, k):
        return V(self.ap[k], self.bufs)

    def bitcast(self, dt):
        return V(self.ap.bitcast(dt), self.bufs)

    def rr(self, pat, **kw):
        return V(self.ap.rearrange(pat, **kw), self.bufs)

    def bc(self, axis, n):
        a = self.ap.unsqueeze(axis)
        shp = list(a.shape)
        shp[axis] = n
        return V(a.to_broadcast(shp), self.bufs)

    def w(self, bufs):
        return V(self.ap, bufs)

    @property
    def shape(self):
        return self.ap.shape


class Sched:
    ENG = ("pe", "act", "dve", "pool", "sp")

    def __init__(self, nc):
        self.nc = nc
        self.ops = {e: [] for e in self.ENG}
        self.ndma = {e: 0 for e in self.ENG}
        self.dma_last = {}
        self.gid = 0
        self.pending_dma = []
        self.out_dmas = []
        self.tag = ""
        self.annotate = False

    def _add(self, eng, fn, reads, writes, is_dma, extra=()):
        o = Op()
        o.eng = eng; o.fn = fn; o.signal = False; o.is_dma = is_dma
        o.dsem = None; o.dval = 0; o.cnt = None
        o.gid = self.gid; self.gid += 1
        o.tag = self.tag
        deps = {}
        for b in reads:
            if b.w is not None:
                deps[id(b.w)] = b.w
        for b in writes:
            if b.w is not None:
                deps[id(b.w)] = b.w
            for r in b.r.values():
                deps[id(r)] = r
        for d in extra:
            deps[id(d)] = d
        if is_dma:
            k = self.ndma[eng]
            self.ndma[eng] += 1
            slot = (eng, k % DMA_ROT)
            prev = self.dma_last.get(slot)
            if prev is not None:
                deps[id(prev)] = prev
                o.dval = prev.dval + 16
            else:
                o.dval = 16
            o.dsem = slot
            self.dma_last[slot] = o
            self.pending_dma.append(o)
        deps.pop(id(o), None)
        for b in reads:
            key = ("dma", o.gid) if is_dma else eng
            b.r[key] = o
        for b in writes:
            b.w = o
            b.r = {}
        o.deps = list(deps.values())
        for d in o.deps:
            if not d.is_dma:
                if d.eng != eng or is_dma or (SAME_ENGINE_SYNC and eng != "pe"):
                    d.signal = True
        self.ops[eng].append(o)
        return o

    def op(self, eng, fn, reads=(), writes=()):
        return self._add(eng, fn, reads, writes, False)

    def dma(self, eng, fn, reads=(), writes=()):
        return self._add(eng, fn, reads, writes, True)

    def barrier(self):
        lasts = [self.ops[e][-1] for e in self.ENG if self.ops[e]]
        j = self._add("sp", lambda e: e.nop(), (), (), False, extra=lasts + self.pending_dma)
        self.pending_dma = []
        for e in self.ENG:
            if e != "sp":
                self._add(e, lambda en: en.nop(), (), (), False, extra=[j])

    def emit(self):
        nc = self.nc
        nsig = {}
        for e in self.ENG:
            n = 0
            for o in self.ops[e]:
                if (not o.is_dma) and o.signal:
                    n += 1
                    o.cnt = n
            nsig[e] = n
        with contextlib.ExitStack() as st:
            esems = {}
            for e in self.ENG:
                nsem = nsig[e] // SEM_LIMIT + 1
                esems[e] = [st.enter_context(nc.semaphore(f"s_{e}_{i}")) for i in range(nsem)]
            dsems = {}
            for slot in self.dma_last:
                dsems[slot] = st.enter_context(nc.semaphore(f"d_{slot[0]}_{slot[1]}"))
            block = st.enter_context(nc.Block())
            final = list(self.out_dmas)

            def run(e, eng):
                waited = {}
                for o in self.ops[e]:
                    for d in o.deps:
                        if d.is_dma:
                            sem = dsems[d.dsem]; val = d.dval; key = ("d",) + d.dsem
                        else:
                            if d.eng == e and not o.is_dma and (e == "pe" or not SAME_ENGINE_SYNC):
                                continue
                            c = d.cnt - 1
                            sem = esems[d.eng][c // SEM_LIMIT]; val = c % SEM_LIMIT + 1
                            key = ("e", d.eng, c // SEM_LIMIT)
                        if waited.get(key, 0) >= val:
                            continue
                        waited[key] = val
                        eng.wait_ge(sem, val)
                    ins = o.fn(eng)
                    if self.annotate:
                        ins.annotate(o.tag)
                    if o.is_dma:
                        ins.then_inc(dsems[o.dsem], 16)
                    elif o.signal:
                        c = o.cnt - 1
                        ins.then_inc(esems[e][c // SEM_LIMIT], 1)
                if e == "sp":
                    for d in final:
                        eng.wait_ge(dsems[d.dsem], d.dval)

            @block.tensor
            def _(eng):
                run("pe", eng)

            @block.scalar
            def _(eng):
                run("act", eng)

            @block.vector
            def _(eng):
                run("dve", eng)

            @block.gpsimd
            def _(eng):
                run("pool", eng)

            @block.sync
            def _(eng):
                run("sp", eng)


class KB:
    def __init__(self, nlayers=DEPTH, dbg=False, stop=None):
        self.nlayers = nlayers
        self.dbg = dbg
        self.stop = stop
        self.nc = bass.Bass("TRN2", target_bir_lowering=False)
        self.S = Sched(self.nc)
        self.st = contextlib.ExitStack()
        self.pb = 0

    def din(self, name, shape, dt=F32):
        return V(self.nc.dram_tensor(name, shape, dt, kind="ExternalInput").ap(), [Buf(name)])

    def dscr(self, name, shape, dt=F32, out=False):
        kind = "ExternalOutput" if (out or self.dbg) else "Internal"
        return V(self.nc.dram_tensor(name, shape, dt, kind=kind).ap(), [Buf(name)])

    def sb(self, name, shape, dt=F32):
        t = self.st.enter_context(self.nc.sbuf_tensor("sb_" + name, shape, dt))
        return V(t[tuple(slice(None) for _ in shape)], [Buf(name)])

    def al(self, region, n, dt=BF16, name=""):
        ne = n if dt == BF16 else 2 * n
        lo, hi = self.reg[region]
        cur = self.cur[region]
        if cur % 2:
            cur += 1
        assert cur + ne <= hi, f"arena region {region} overflow: {name} need {ne} at {cur} hi {hi}"
        self.cur[region] = cur + ne
        v = V(self.arena[:, cur:cur + ne], [Buf(name)])
        if dt != BF16:
            v = v.bitcast(dt)
        return v

    def areset(self, *regions):
        for r in regions:
            self.cur[r] = self.reg[r][0]

    def bank(self, pool=None):
        if pool is None:
            b = self.pb
            self.pb = (self.pb + 1) % 8
            return self.psb[b]
        lst = self.bpools[pool]
        i = self.bpos.get(pool, 0)
        self.bpos[pool] = i + 1
        return self.psb[lst[i % len(lst)]]

    def set_pools(self, **kw):
        self.bpools = kw
        self.bpos = {}

    @staticmethod
    def _rb(*xs):
        out = []
        for x in xs:
            if isinstance(x, V):
                out.extend(x.bufs)
        return out

    @staticmethod
    def _a(x):
        return x.ap if isinstance(x, V) else x

    def act(self, out, in_, func, bias=None, scale=1.0, accum=None):
        a = self._a
        kw = {}
        if bias is not None:
            kw["bias"] = a(bias)
        if accum is not None:
            kw["accum_out"] = a(accum)
        sc = a(scale)
        self.S.op("act", lambda e: e.activation(out=out.ap, in_=in_.ap, func=func, scale=sc, **kw),
                  self._rb(in_, bias, scale), self._rb(out, accum))

    def tt(self, out, in0, in1, op, eng="dve"):
        self.S.op(eng, lambda e: e.tensor_tensor(out=out.ap, in0=in0.ap, in1=in1.ap, op=op),
                  self._rb(in0, in1), self._rb(out))

    def ts(self, out, in0, s1, op0, s2=None, op1=None, eng="dve"):
        a = self._a
        if op1 is None:
            self.S.op(eng, lambda e: e.tensor_scalar(out=out.ap, in0=in0.ap, scalar1=a(s1), scalar2=None, op0=op0),
                      self._rb(in0, s1), self._rb(out))
        else:
            self.S.op(eng, lambda e: e.tensor_scalar(out=out.ap, in0=in0.ap, scalar1=a(s1), scalar2=a(s2), op0=op0, op1=op1),
                      self._rb(in0, s1, s2), self._rb(out))

    def stt(self, out, in0, scalar, in1, op0, op1):
        a = self._a
        self.S.op("dve", lambda e: e.scalar_tensor_tensor(out=out.ap, in0=in0.ap, scalar=a(scalar), in1=in1.ap, op0=op0, op1=op1),
                  self._rb(in0, scalar, in1), self._rb(out))

    def copy(self, out, in_, eng="dve"):
        if eng == "act":
            self.act(out, in_, AF.Copy)
        else:
            self.S.op(eng, lambda e: e.tensor_copy(out=out.ap, in_=in_.ap), self._rb(in_), self._rb(out))

    def memset(self, out, val, eng="pool"):
        self.S.op(eng, lambda e: e.memset(out.ap, val), (), self._rb(out))

    def recip(self, out, in_):
        self.S.op("dve", lambda e: e.reciprocal(out=out.ap, in_=in_.ap), self._rb(in_), self._rb(out))

    def scan(self, out, a, b, init):
        i = self._a(init)
        self.S.op("dve", lambda e: e.tensor_tensor_scan(out=out.ap, data0=a.ap, data1=b.ap, initial=i, op0=ALU.mult, op1=ALU.add),
                  self._rb(a, b, init), self._rb(out))

    def mm(self, out, pairs):
        n = len(pairs)
        rd = []
        for l, r in pairs:
            rd += l.bufs + r.bufs

        def fn(e):
            ins = None
            for i, (l, r) in enumerate(pairs):
                ins = e.matmul(out.ap, lhsT=l.ap, rhs=r.ap, start=(i == 0), stop=(i == n - 1))
            return ins
        self.S.op("pe", fn, rd, self._rb(out))

    def mm1(self, out, lhsT, rhs, start, stop):
        self.S.op("pe", lambda e: e.matmul(out.ap, lhsT=lhsT.ap, rhs=rhs.ap, start=start, stop=stop),
                  self._rb(lhsT, rhs), self._rb(out))

    def tr(self, out, in_, ident):
        self.S.op("pe", lambda e: e.transpose(out=out.ap, in_=in_.ap, identity=ident.ap),
                  self._rb(in_, ident), self._rb(out))

    def dma(self, out, in_, q="sp"):
        return self.S.dma(q, lambda e: e.dma_start(out=out.ap, in_=in_.ap), self._rb(in_), self._rb(out))

    def build(self):
        nc = self.nc
        L = self.nlayers
        self.x_d = self.din("x", [2048, D])
        self.ctx_d = self.din("ctx", [CTX, D])
        self.cc_d = self.din("cc", [128, 32])
        self.pv_d = self.din("pv", [DEPTH, 128, NPV])
        self.pr_d = self.din("pr", [DEPTH, NPR])
        self.cst_d = self.din("cst", [128, NCST])
        self.rope_d = self.din("rope", [128, 4096])
        self.adaw_d = self.din("ada_w", [DEPTH, D, 3 * D])
        self.win_d = self.din("w_in", [DEPTH, D, 7184])
        self.wout_d = self.din("w_out", [DEPTH, 3072, D])
        self.gaw_d = self.din("lru_ga_w", [DEPTH, 2, 16, 64, 64])
        self.gxw_d = self.din("lru_gx_w", [DEPTH, 2, 16, 64, 64])
        self.out_d = self.dscr("out", [2048, D], out=True)
        self.xT_d = self.dscr("xT_s", [16, 128, T])
        self.xT_p = [self.xT_d[m].w([Buf(f"xT{m}")]) for m in range(16)]
        self.mixT_d = self.dscr("mixT_s", [24, 128, T], BF16)
        self.mixT_p = [self.mixT_d[k].w([Buf(f"mixT{k}")]) for k in range(24)]
        self.yf_d = self.dscr("yf_s", [NT, 128, 1024])
        self.sz_d = self.dscr("sz_s", [NT, 128, 1024], BF16)
        if self.dbg:
            self.uT_dbg = self.dscr("uT_dbg", [16, 128, T], BF16)

        self.cst = self.sb("cst", [128, NCST])
        self.cstb = self.sb("cstb", [128, NCST], BF16)
        self.pvt = self.sb("pvt", [128, NPV])
        self.prt = self.sb("prt", [128, NPR])
        self.cc = self.sb("cc", [128, 32])
        self.scb = self.sb("scb", [128, 32], BF16)
        self.sm = self.sb("sm", [128, 512])
        self.ssm = self.sb("ssm", [128, 1440])
        arena_t = self.st.enter_context(nc.sbuf_tensor("arena", [128, 95872], BF16))
        self.arena = arena_t[:, :]
        self.reg = {"R": (0, 32768), "U": (32768, 69632), "T": (69632, 95872), "UT": (32768, 95872), "ALL": (0, 95872)}
        self.cur = {k: v[0] for k, v in self.reg.items()}
        ps_t = self.st.enter_context(nc.psum_tensor("ps", [128, 8, 512], F32))
        self.psb = [V(ps_t[:, b, :], [Buf(f"ps{b}")]) for b in range(8)]

        self.ident = self.cst[:, C_ID:C_ID + 128]
        self.ones = self.cst[:, C_ONE:C_ONE + 128]
        self.identb = self.cstb[:, C_ID:C_ID + 128]
        self.onesb = self.cstb[:, C_ONE:C_ONE + 128]

        self.dma(self.cst, self.cst_d)
        self.dma(self.cc, self.cc_d)
        self.copy(self.cstb, self.cst, eng="dve")
        self.memset(self.sm[:, 500:501], -0.5)
        self.memset(self.sm[:, 501:502], 0.5)
        self.nhalf = self.sm[:, 500:501]
        self.memset(self.sm[:, 502:503], EPS)
        self.memset(self.sm[:, 503:504], 0.25)
        self.epsc = self.sm[:, 502:503]
        self.quart = self.sm[:, 503:504]
        self.phalf = self.sm[:, 501:502]
        th = self.sm[:, 440:472]
        hf = self.sm[:, 400:432]
        self.act(th, self.cc, AF.Tanh, scale=0.5)
        self.ts(hf, self.cc, 0.5, ALU.mult)
        self.stt(self.scb, th, 1.0, hf, ALU.add, ALU.mult)

        self.S.tag = "init"
        self.phase_init()
        for l in range(L):
            self.layer(l)
            if self.stop is not None and self.stop[0] == l:
                break
        if self.stop is None:
            self.phase_final()
        self.S.barrier()
        self.S.emit()
        return nc

    def phase_init(self):
        self.areset("ALL")
        xin = [self.al("ALL", D, F32, f"xin{i}") for i in range(2)]
        xo = [self.al("ALL", D, F32, f"xo{i}") for i in range(2)]
        for t in range(NT):
            src = self.ctx_d[t * 128:(t + 1) * 128, :] if t < 2 else self.x_d[(t - 2) * 128:(t - 1) * 128, :]
            xi = xin[t % 2]
            xoo = xo[t % 2]
            self.dma(xi, src)
            for q in range(4):
                bk = self.bank()
                for i in range(4):
                    m = 4 * q + i
                    self.tr(bk[:, i * 128:(i + 1) * 128], xi[:, m * 128:(m + 1) * 128], self.ident)
                self.copy(xoo[:, q * 512:(q + 1) * 512], bk, eng=("act" if q % 2 else "dve"))
            dst = V(self.xT_d.ap[:, :, t * 128:(t + 1) * 128].rearrange("m p t -> p m t"), sum([p.bufs for p in self.xT_p], []))
            self.dma(dst, xoo.rr("p (m t) -> p m t", t=128))
        self.S.barrier()

    def phase_final(self):
        self.S.barrier()
        self.areset("ALL")
        xc = [self.al("ALL", 2048, F32, f"fx{i}") for i in range(3)]
        ob = self.al("ALL", 16 * D, F32, "fob")
        ob3 = ob.rr("p (t d) -> p t d", d=D)
        for m in range(16):
            xm = xc[m % 3]
            self.dma(xm, self.xT_p[m][:, CTX:T])
            for q in range(4):
                bk = self.bank()
                for i in range(4):
                    t = 4 * q + i
                    self.tr(bk[:, i * 128:(i + 1) * 128], xm[:, t * 128:(t + 1) * 128], self.ident)
                self.copy(ob3[:, 4 * q:4 * q + 4, m * 128:(m + 1) * 128], bk.rr("p (t d) -> p t d", d=128),
                          eng=("act" if q % 2 else "dve"))
        for t in range(16):
            o = self.dma(self.out_d[t * 128:(t + 1) * 128, :], ob3[:, t, :])
            self.S.out_dmas.append(o)

    def layer(self, l):
        st = self.stop[1] if (self.stop is not None and self.stop[0] == l) else None
        self.S.barrier()
        self.S.tag = "mod"
        self.load_params(l)
        self.phase_mod(l)
        self.S.barrier()
        self.S.tag = "norm"
        self.phase_norm(l)
        if st == "norm":
            return
        self.S.barrier()
        self.S.tag = "lru"
        self.phase_lru(l)
        if st == "lru":
            return
        self.S.barrier()
        self.S.tag = "att"
        self.phase_att(l)
        if st == "att":
            return
        self.S.barrier()
        self.S.tag = "ssd"
        self.phase_ssd(l)
        if st == "ssd":
            return
        self.S.barrier()
        self.S.tag = "out"
        self.phase_out(l)

    def load_params(self, l):
        self.dma(self.pvt, self.pv_d[l])
        self.dma(self.prt, V(self.pr_d.ap[l:l + 1, :].to_broadcast([128, NPR]), self.pr_d.bufs))
        sm = self.sm
        pv = self.pvt
        self.ch = sm[:, 0:16]; self.hba = sm[:, 16:32]; self.hbx = sm[:, 32:48]
        e1 = sm[:, 48:64]
        self.act(e1, pv[:, PV_LAM:PV_LAM + 16], AF.Exp, scale=-1.0)
        self.act(e1, e1, AF.Ln, bias=1.0)
        self.ts(self.ch, e1, -4.0, ALU.mult)
        self.ts(self.hba, pv[:, PV_GAB:PV_GAB + 16], 0.5, ALU.mult)
        self.ts(self.hbx, pv[:, PV_GXB:PV_GXB + 16], 0.5, ALU.mult)
        self.scwh = sm[:, 64:112]; self.scbh = sm[:, 112:124]
        self.ts(self.scwh, pv[:, PV_SCW:PV_SCW + 48], 0.5, ALU.mult)
        self.ts(self.scbh, pv[:, PV_SCB:PV_SCB + 12], 0.5, ALU.mult)
        self.negA = sm[:, 124:156]; self.esink = sm[:, 156:164]
        self.act(self.negA, self.prt[:, PR_ALOG:PR_ALOG + 32], AF.Exp)
        self.ts(self.negA, self.negA, -1.0, ALU.mult)
        self.act(self.esink, self.prt[:, PR_SINK:PR_SINK + 8], AF.Exp)
        self.modL = sm[:, 164:212]; self.modC = sm[:, 212:260]
        self.gL = sm[:, 260:276]; self.gC = sm[:, 276:292]

    def phase_mod(self, l):
        self.areset("ALL")
        wt = [self.al("ALL", 16 * 512, BF16, f"adaw{i}") for i in range(3)]
        bk = self.bank()
        for ct4 in range(12):
            w = wt[ct4 % 3]
            w3 = w.rr("p (k c) -> p k c", c=512)
            src = V(self.adaw_d.ap[l, :, ct4 * 512:(ct4 + 1) * 512].rearrange("(k p) c -> p k c", p=128), self.adaw_d.bufs)
            self.dma(w3, src, q="pool")
            for sub in range(4):
                ct = ct4 * 4 + sub
                self.mm(bk[:, 2 * ct:2 * ct + 2],
                        [(w3[:, k, sub * 128:(sub + 1) * 128], self.scb[:, k:32:16]) for k in range(16)])
        adab = self.pvt[:, PV_ADAB:PV_ADAB + 48]
        self.tt(self.modL, bk[:, 0:96:2], adab, ALU.add)
        self.tt(self.modC, bk[:, 1:96:2], adab, ALU.add)
        nw = self.pvt[:, PV_NW:PV_NW + 16]
        self.stt(self.gL, self.modL[:, 16:32], 1.0, nw, ALU.add, ALU.mult)
        self.stt(self.gC, self.modC[:, 16:32], 1.0, nw, ALU.add, ALU.mult)

    def phase_norm(self, l):
        self.areset("R", "U", "T")
        self.uT = self.al("U", 16 * T, BF16, "uT").rr("p (k t) -> p k t", t=T)
        xb = [self.al("R", T, F32, f"nx{i}") for i in range(3)]
        sq = [self.al("R", T, F32, f"nsq{i}") for i in range(2)]
        rstd = self.al("R", T, F32, "rstd")
        tmp = [self.al("T", T, F32, f"ntmp{i}") for i in range(2)]
        banks = [self.bank() for _ in range(5)]
        for m in range(16):
            x = xb[m % 3]
            s = sq[m % 2]
            self.dma(x, self.xT_p[m])
            self.act(s, x, AF.Square)
            for r, (a, b) in enumerate(RNG_T):
                self.mm1(banks[r][:, 0:b - a], self.ones, s[:, a:b], m == 0, m == 15)
        for r, (a, b) in enumerate(RNG_T):
            self.act(rstd[:, a:b], banks[r][:, 0:b - a], AF.Ln, bias=self.epsc, scale=1.0 / D)
        self.act(rstd, rstd, AF.Exp, scale=-0.5)
        shL = self.modL[:, 0:16]; shC = self.modC[:, 0:16]
        for m in range(16):
            x = xb[m % 3]
            tp = tmp[m % 2]
            self.dma(x, self.xT_p[m])
            for (a, b, g, sh) in ((0, CTX, self.gC, shC), (CTX, T, self.gL, shL)):
                self.stt(tp[:, a:b], x[:, a:b], g[:, m:m + 1], rstd[:, a:b], ALU.mult, ALU.mult)
                self.act(self.uT[:, m, a:b], tp[:, a:b], AF.Identity, bias=sh[:, m:m + 1])
        if self.dbg:
            self.dma(self.uT_dbg.rr("k p t -> p k t"), self.uT)

    def load_w(self, dst3, l, col0, ncols):
        src = V(self.win_d.ap[l, :, col0:col0 + ncols].rearrange("(k p) c -> p k c", p=128), self.win_d.bufs)
        self.dma(dst3, src, q="pool")

    def proj_fm(self, w3, c0, evac, pool=None):
        for r, (a, b) in enumerate(RNG_T):
            bk = self.bank(pool)
            self.mm(bk[:, 0:b - a], [(w3[:, k, c0:c0 + 128], self.uT[:, k, a:b]) for k in range(16)])
            evac(r, a, b, bk[:, 0:b - a])

    def phase_lru(self, l):
        self.areset("R", "T")
        self.set_pools(proj=[0, 1, 2, 3, 4], gate=[5, 6, 7])
        G = [self.al("R", XPW, F32, f"G{i}") for i in range(7)]
        XP0, LG, XC, TA0, TI0, TA1, TI1 = G
        XP1 = self.al("T", XPW, F32, "XP1")
        XPs = [XP0, XP1]
        TAd = [TA0, TA1]; TId = [TI0, TI1]
        HF = self.al("T", XPW, F32, "HF")
        SS = HF
        XCB = self.al("T", XPW, BF16, "xcb")
        OB = self.al("T", XPW, BF16, "ob")
        wl = [self.al("T", 16 * 256, BF16, f"wl{i}").rr("p (k c) -> p k c", c=256) for i in range(2)]
        gwt = self.al("T", 32 * 128, BF16, "gw")
        gw4 = gwt.rr("p (i c) -> p i c", c=128)
        self.memset(gwt, 0.0)
        for gi, wd in enumerate((self.gaw_d, self.gxw_d)):
            for d in range(2):
                base = (gi * 2 + d) * 8
                for par in range(2):
                    src = V(wd.ap[l, d, par::2, :, :].rearrange("j i o -> i j o"), wd.bufs)
                    dst = gw4[par * 64:(par + 1) * 64, base:base + 8, par * 64:(par + 1) * 64]
                    self.dma(dst, src, q="pool")
        pv = self.pvt
        self.memset(LG[:, 256:259], 0.0)
        for XP in XPs:
            self.memset(XP[:, 0:2], 0.0)
            self.memset(XP[:, 258:261], 0.0)
            self.memset(XP[:, 2309:2310], 0.0)
        cw = pv[:, PV_LCW:PV_LCW + 32]

        def front(j):
            w3 = wl[j % 2]
            XP = XPs[j % 2]
            self.load_w(w3[:, :, 0:128], l, O_LX + j * 128, 128)
            self.load_w(w3[:, :, 128:256], l, O_LG + j * 128, 128)
            if j >= 2:
                self.memset(XP[:, 0:2], 0.0)
                self.memset(XP[:, 258:261], 0.0)
                self.memset(XP[:, 2309:2310], 0.0)

            def ev_x(r, a, b, p):
                self.copy(XP[:, a + 2:b + 2] if a < CTX else XP[:, a + 5:b + 5], p, eng=("act" if r % 2 else "dve"))
            self.proj_fm(w3, 0, ev_x, pool="proj")
            self.act(XC[:, 0:LP], XP[:, 0:LP], AF.Identity, bias=pv[:, PV_LCB + j:PV_LCB + j + 1], scale=cw[:, j:j + 1])
            for k in range(1, 4):
                self.stt(XC[:, 0:LP], XP[:, k:k + LP], cw[:, k * 8 + j:k * 8 + j + 1], XC[:, 0:LP], ALU.mult, ALU.add)
            self.copy(XCB[:, 0:LP], XC[:, 0:LP], eng="dve")

        def mid(j):
            for d in range(2):
                col = d * 8 + j
                TA = TAd[d]; TI = TId[d]
                for (a, b) in RNG_P:
                    bk = self.bank("gate")
                    self.mm(bk[:, 0:b - a], [(gw4[:, (0 * 2 + d) * 8 + j, :], XCB[:, a:b])])
                    self.act(TA[:, a:b], bk[:, 0:b - a], AF.Tanh, bias=self.hba[:, col:col + 1], scale=0.5)
                    bk2 = self.bank("gate")
                    self.mm(bk2[:, 0:b - a], [(gw4[:, (1 * 2 + d) * 8 + j, :], XCB[:, a:b])])
                    self.act(TI[:, a:b], bk2[:, 0:b - a], AF.Tanh, bias=self.hbx[:, col:col + 1], scale=0.5)
                self.act(TA[:, 0:LP], TA[:, 0:LP], AF.Exp, bias=self.ch[:, col:col + 1], scale=self.ch[:, col:col + 1])
                self.stt(TI[:, 0:LP], TI[:, 0:LP], 1.0, XC[:, 0:LP], ALU.add, ALU.mult)

        def tail(j):
            w3 = wl[j % 2]
            HB = XPs[j % 2]

            def ev_g(r, a, b, p):
                self.act(LG[:, pc(a):pc(a) + (b - a)], p, AF.Copy, scale=0.5)
            self.proj_fm(w3, 128, ev_g, pool="proj")
            self.act(HF[:, 0:LP], LG[:, 0:LP], AF.Tanh)
            self.stt(LG[:, 0:LP], HF[:, 0:LP], 1.0, LG[:, 0:LP], ALU.add, ALU.mult)
            for d in range(2):
                TA = TAd[d]; TI = TId[d]
                self.act(SS[:, 0:LP], TA[:, 0:LP], AF.Square)
                self.act(SS[:, 0:LP], SS[:, 0:LP], AF.Sqrt, bias=self.quart, scale=-0.25)
                self.tt(TI[:, 0:LP], TI[:, 0:LP], SS[:, 0:LP], ALU.mult)
            self.scan(HF[:, 0:CTX], TA0[:, 0:CTX], TI0[:, 0:CTX], 0.0)
            self.scan(HF[:, 259:LP], TA0[:, 259:LP], TI0[:, 259:LP], HF[:, 255:256])
            self.scan(HB[:, 0:CTX][:, ::-1], TA1[:, 0:CTX][:, ::-1], TI1[:, 0:CTX][:, ::-1], 0.0)
            self.scan(HB[:, 259:LP][:, ::-1], TA1[:, 259:LP][:, ::-1], TI1[:, 259:LP][:, ::-1], HB[:, 0:1])
            self.tt(HF[:, 0:LP], HF[:, 0:LP], HB[:, 0:LP], ALU.add)
            self.tt(OB[:, 0:LP], HF[:, 0:LP], LG[:, 0:LP], ALU.mult)
            self.dma(self.mixT_p[j][:, 0:CTX], OB[:, 0:CTX])
            self.dma(self.mixT_p[j][:, CTX:T], OB[:, 259:LP])

        front(0)
        for j in range(8):
            mid(j)
            if j + 1 < 8:
                front(j + 1)
            tail(j)

    def att_norm_rope(self, RAWp, RAWf, A, Bp, Bf, C, wn, dst):
        self.dma(C[:, 0:2048], self.rope_d[:, 0:2048])
        self.dma(A[:, 0:2048], self.rope_d[:, 2048:4096])
        for r, (a, b) in enumerate(RNG_T):
            self.act(Bp[r], RAWp[r], AF.Square)
            bk = self.bank("ssq")
            self.mm1(bk[:, 0:b - a], self.ones, Bp[r], True, True)
            self.act(Bp[r], bk[:, 0:b - a], AF.Ln, bias=self.epsc, scale=1.0 / 128)
        self.act(Bf, Bf, AF.Exp, scale=-0.5)
        self.stt(RAWf, RAWf, wn, Bf, ALU.mult, ALU.mult)
        self.copy(dst[:, 0:CTX], RAWf[:, 0:CTX], eng="act")
        sw = self.cst[:, C_SW:C_SW + 128]
        for (a, b) in RNG_T[1:]:
            bk = self.bank("swap")
            self.mm1(bk, sw, RAWf[:, a:b], True, True)
            self.tt(A[:, a - CTX:b - CTX], bk, A[:, a - CTX:b - CTX], ALU.mult)
        self.tt(C[:, 0:2048], RAWf[:, CTX:T], C[:, 0:2048], ALU.mult)
        self.tt(dst[:, CTX:T], C[:, 0:2048], A[:, 0:2048], ALU.add)

    def phase_att(self, l):
        self.areset("R", "T")
        Q4 = self.al("R", 4 * T, BF16, "Q4").rr("p (h t) -> p h t", t=T)
        SG4 = self.al("R", 4 * T, BF16, "SG4").rr("p (h t) -> p h t", t=T)
        KT = self.al("R", 2 * T, BF16, "KT").rr("p (g t) -> p g t", t=T)
        VT = self.al("R", NT * 256, BF16, "VT").rr("p (t c) -> p t c", c=256)
        RAWs = [self.al("T", T, F32, "RAW0"), self.al("R", T, F32, "RAW1")]
        off_alias = self.cur["T"]
        A = self.al("T", T, F32, "A"); B = self.al("T", T, F32, "B"); C = self.al("T", T, F32, "C")
        wq = [self.al("T", 16 * 128, BF16, f"wq{i}").rr("p (k c) -> p k c", c=128) for i in range(2)]
        def parts(v):
            ps = [V(v.ap[:, a:b], [Buf(f"part{r}")]) for r, (a, b) in enumerate(RNG_T)]
            return ps, V(v.ap, [p.bufs[0] for p in ps])
        RAWparts = [parts(r) for r in RAWs]
        Bp, Bf = parts(B)
        o = off_alias

        def alias(n, dt, name):
            nonlocal o
            ne = n if dt == BF16 else 2 * n
            v = V(self.arena[:, o:o + ne], [Buf(name)])
            o += ne
            return v if dt == BF16 else v.bitcast(dt)
        PTs = [[alias(512, BF16, f"PT{s_}_{i}") for i in range(5)] for s_ in range(2)]
        DEN = [alias(512, F32, f"DEN{i}") for i in range(2)]
        OO = [alias(512, F32, f"OO{i}") for i in range(2)]
        assert o <= off_alias + 3 * 2 * T
        pv = self.pvt
        GEb = self.cstb[:, C_GE:C_GE + 128]; LEb = self.cstb[:, C_LE:C_LE + 128]
        nw = [0]
        nr = [0]

        def nextw(col0):
            w = wq[nw[0] % 2]
            nw[0] += 1
            self.load_w(w, l, col0, 128)
            return w

        def nextraw():
            r = RAWparts[nr[0] % 2]
            nr[0] += 1
            return r
        scale = float(1.0 / np.sqrt(128.0))
        for g in range(2):
            self.S.barrier()
            self.S.tag = "att:proj"
            self.set_pools(proj=[0, 1, 2, 3], ssq=[4, 5], swap=[6, 7])
            w = nextw(O_K + g * 128)
            RAWp, RAWf = nextraw()

            def ev_raw(r, a, b, p, RAWp=RAWp):
                self.copy(RAWp[r], p, eng=("act" if r % 2 else "dve"))
            self.proj_fm(w, 0, ev_raw, pool="proj")
            self.att_norm_rope(RAWp, RAWf, A, Bp, Bf, C, pv[:, PV_KN:PV_KN + 1], KT[:, g, :])
            w = nextw(O_V + g * 128)
            for t0 in range(0, NT, 4):
                nt = min(4, NT - t0)
                bk = self.bank("proj")
                for i in range(nt):
                    t = t0 + i
                    self.mm(bk[:, i * 128:(i + 1) * 128], [(self.uT[:, k, t * 128:(t + 1) * 128], w[:, k, :]) for k in range(16)])
                self.copy(VT[:, t0:t0 + nt, g * 128:(g + 1) * 128], bk[:, 0:nt * 128].rr("p (t c) -> p t c", c=128), eng="act")
            for h in range(4):
                hh = g * 4 + h
                w = nextw(O_Q + hh * 128)
                RAWp, RAWf = nextraw()

                def ev_raw(r, a, b, p, RAWp=RAWp):
                    self.copy(RAWp[r], p, eng=("act" if r % 2 else "dve"))
                self.proj_fm(w, 0, ev_raw, pool="proj")
                self.att_norm_rope(RAWp, RAWf, A, Bp, Bf, C, pv[:, PV_QN:PV_QN + 1], Q4[:, h, :])
                w = nextw(O_AG + hh * 128)
                RAWp, RAWf = nextraw()

                def ev_ag(r, a, b, p, RAWp=RAWp):
                    self.act(RAWp[r], p, AF.Copy, scale=0.5)
                self.proj_fm(w, 0, ev_ag, pool="proj")
                self.act(Bf, RAWf, AF.Tanh)
                self.stt(SG4[:, h, :], Bf, 1.0, RAWf, ALU.add, ALU.mult)
            es = self.esink[:, g * 4:(g + 1) * 4].bc(2, 128)
            self.S.barrier()
            self.S.tag = "att:blk"
            self.set_pools(st=[0, 1, 2, 3])

            def keys_of(n):
                return [0, 1] + ([j for j in (n - 1, n, n + 1) if 2 <= j < NT] if n >= 2 else [])

            def stageA(n):
                qs = Q4[:, :, n * 128:(n + 1) * 128]
                for idx, j in enumerate(keys_of(n)):
                    sbk = self.bank("st")
                    self.mm(sbk.rr("p (h q) -> p h q", q=128), [(KT[:, g, j * 128:(j + 1) * 128], qs)])
                    pt = PTs[n % 2][idx]
                    self.act(pt, sbk, AF.Exp, scale=scale)
                    if n >= 2 and j >= 2 and j == n - 1:
                        p3 = pt.rr("p (h q) -> p h q", q=128)
                        self.tt(p3, p3, GEb.bc(1, 4), ALU.mult)
                    if n >= 2 and j == n + 1:
                        p3 = pt.rr("p (h q) -> p h q", q=128)
                        self.tt(p3, p3, LEb.bc(1, 4), ALU.mult)

            def stageB(n):
                otb = self.psb[4 + n % 2]; lb = self.psb[6 + n % 2]
                ks = keys_of(n)
                for idx, j in enumerate(ks):
                    pt = PTs[n % 2][idx]
                    self.mm1(otb, VT[:, j, g * 128:(g + 1) * 128], pt, idx == 0, idx == len(ks) - 1)
                    self.mm1(lb, self.onesb, pt, idx == 0, idx == len(ks) - 1)

            def stageC(n):
                otb = self.psb[4 + n % 2]; lb = self.psb[6 + n % 2]
                den = DEN[n % 2]; oo = OO[n % 2]
                self.tt(den.rr("p (h q) -> p h q", q=128), lb.rr("p (h q) -> p h q", q=128), es, ALU.add)
                self.act(den, den, AF.Ln)
                self.act(den, den, AF.Exp, scale=-1.0)
                self.tt(oo, otb, den, ALU.mult)
                sg = SG4[:, :, n * 128:(n + 1) * 128]
                self.tt(sg, oo.rr("p (h q) -> p h q", q=128), sg, ALU.mult)
            for step in range(NT + 2):
                if step < NT:
                    stageA(step)
                if 1 <= step <= NT:
                    stageB(step - 1)
                if step >= 2:
                    stageC(step - 2)
            for h in range(4):
                self.dma(self.mixT_p[8 + g * 4 + h], SG4[:, h, :])

    def phase_ssd(self, l):
        ssm = self.ssm
        DTR = ssm[:, 0:288]
        DTd = [ssm[:, 288:576], ssm[:, 576:864]]
        Ad = [ssm[:, 864:1152], ssm[:, 1152:1440]]
        pv = self.pvt
        self.S.tag = "ssd:z"
        self.areset("R", "T")
        wz = self.al("R", 16 * 1024, BF16, "wz").rr("p (k c) -> p k c", c=1024)
        wdt = self.al("R", 16 * 16, BF16, "wdt").rr("p (k c) -> p k c", c=16)
        ZH = [self.al("T", 1024, F32, f"ZH{i}") for i in range(2)]
        TGz = [self.al("T", 1024, F32, f"TGz{i}") for i in range(2)]
        SZt = [self.al("T", 1024, BF16, f"SZt{i}") for i in range(2)]
        self.load_w(wz[:, :, 0:512], l, O_Z, 512)
        self.load_w(wz[:, :, 512:1024], l, O_Z + 512, 512)
        self.load_w(wdt, l, O_DT, 16)
        self.set_pools(z=[0, 1, 2, 3, 4, 5], dt=[6, 7])
        for t in range(NT):
            zh = ZH[t % 2]; tg = TGz[t % 2]; sz = SZt[t % 2]
            ut = lambda k: self.uT[:, k, t * 128:(t + 1) * 128]
            for half in range(2):
                bk = self.bank("z")
                self.mm(bk, [(ut(k), wz[:, k, half * 512:(half + 1) * 512]) for k in range(16)])
                self.act(zh[:, half * 512:(half + 1) * 512], bk, AF.Copy, scale=0.5)
            bk = self.bank("dt")
            self.mm(bk[:, 0:16], [(ut(k), wdt[:, k, :]) for k in range(16)])
            self.copy(DTR[:, t * 16:(t + 1) * 16], bk[:, 0:16])
            self.act(tg, zh, AF.Tanh)
            self.stt(sz, tg, 1.0, zh, ALU.add, ALU.mult)
            self.dma(self.sz_d[t], sz)
        self.S.barrier()
        self.S.tag = "ssd:prep"
        self.areset("R", "T")
        XTOK = self.al("R", NT * 1024, BF16, "XTOK").rr("p (t c) -> p t c", c=1024)
        BTOK = self.al("R", NT * 256, BF16, "BTOK").rr("p (t c) -> p t c", c=256)
        BT = [self.al("R", LP + 1, BF16, f"BT{g}") for g in range(2)]
        CT = [self.al("R", LP + 1, BF16, f"CT{g}") for g in range(2)]
        XPs = [self.al("T", XPW, F32, "sXP0"), self.al("T", XPW, F32, "sXP1")]
        XH = self.al("T", XPW, F32, "sXH"); TG = self.al("T", XPW, F32, "sTG")
        SIb = self.al("T", LP + 1, BF16, "SIb")
        wl = [self.al("T", 16 * 128, BF16, f"swl{i}").rr("p (k c) -> p k c", c=128) for i in range(2)]
        self.set_pools(proj=[0, 1, 2, 3, 4], tr=[5, 6, 7])
        for XP in XPs:
            self.memset(XP[:, 0:2], 0.0)
            self.memset(XP[:, 258:261], 0.0)
            self.memset(XP[:, 2309:2310], 0.0)
        for cc in range(12):
            w = wl[cc % 2]
            XP = XPs[cc % 2]
            self.load_w(w, l, O_XBC + cc * 128, 128)

            def ev_x(r, a, b, p, XP=XP):
                self.copy(XP[:, a + 2:b + 2] if a < CTX else XP[:, a + 5:b + 5], p, eng=("act" if r % 2 else "dve"))
            self.proj_fm(w, 0, ev_x, pool="proj")
            cw = self.scwh
            self.act(XH[:, 0:LP], XP[:, 0:LP], AF.Identity, bias=self.scbh[:, cc:cc + 1], scale=cw[:, cc:cc + 1])
            for k in range(1, 4):
                self.stt(XH[:, 0:LP], XP[:, k:k + LP], cw[:, k * 12 + cc:k * 12 + cc + 1], XH[:, 0:LP], ALU.mult, ALU.add)
            self.act(TG[:, 0:LP], XH[:, 0:LP], AF.Tanh)
            if cc < 8:
                dest = SIb
            elif cc < 10:
                dest = BT[cc - 8]
            else:
                dest = CT[cc - 10]
            self.stt(dest[:, 0:LP], TG[:, 0:LP], 1.0, XH[:, 0:LP], ALU.add, ALU.mult)
            if cc < 10:
                for t0 in range(0, NT, 8):
                    nt = min(8, NT - t0)
                    bkb = self.bank("tr").bitcast(BF16)
                    for i in range(nt):
                        tp = tile_pc(t0 + i)
                        self.tr(bkb[:, i * 128:(i + 1) * 128], dest[:, tp:tp + 128], self.identb)
                    if cc < 8:
                        dd = XTOK[:, t0:t0 + nt, cc * 128:(cc + 1) * 128]
                    else:
                        dd = BTOK[:, t0:t0 + nt, (cc - 8) * 128:(cc - 7) * 128]
                    self.copy(dd, bkb[:, 0:nt * 128].rr("p (t c) -> p t c", c=128), eng=("act" if (t0 // 8) % 2 else "dve"))
        for d in range(2):
            d3 = DTd[d].rr("p (t h) -> p t h", h=16)
            self.tt(d3, DTR.rr("p (t h) -> p t h", h=16), self.prt[:, PR_DTB + d * 16:PR_DTB + (d + 1) * 16].bc(1, NT), ALU.add)
            self.act(DTd[d], DTd[d], AF.Exp)
            self.act(DTd[d], DTd[d], AF.Ln, bias=1.0)
            self.tt(Ad[d].rr("p (t h) -> p t h", h=16), d3, self.negA[:, d * 16:(d + 1) * 16].bc(1, NT), ALU.mult)
        self.S.barrier()
        self.S.tag = "ssd:loop"
        self.areset("UT")
        A_ = lambda n, dt, nm: [self.al("UT", n, dt, f"{nm}{i}") for i in range(2)]
        RSEG = A_(2048, F32, "RSEG"); E = A_(2048, BF16, "E"); WMT = A_(2048, BF16, "WMT"); CBM = A_(256, BF16, "CBM")
        XDT = A_(1024, BF16, "XDT"); XW = A_(1024, BF16, "XW"); HBF = A_(1024, BF16, "HBF")
        H = self.al("UT", 1024, F32, "H")
        SMALL = A_(128, F32, "SSM")
        YO = A_(1024, F32, "YO"); Y = A_(1024, F32, "Y"); SZl = A_(1024, BF16, "SZl"); YFL = A_(1024, F32, "YFL")
        YN = A_(1024, BF16, "YN"); MTt = A_(1024, BF16, "MTt")
        Dsk = self.prt[:, PR_D:PR_D + 16]
        snw = self.prt[:, PR_SNW:PR_SNW + 1024]
        self.set_pools(seg=[0, 1, 2, 3], ac=[4], rot=[5, 6, 7])
        mixT_ssd_bufs = sum([p.bufs for p in self.mixT_p[16:24]], [])
        it = 0
        for d in range(2):
            order = list(range(NT)) if d == 0 else [1, 0] + list(range(NT - 1, 1, -1))
            M2 = self.cst[:, C_LE:C_LE + 128] if d == 0 else self.cst[:, C_GE:C_GE + 128]
            M1 = self.cst[:, C_GT:C_GT + 128] if d == 0 else self.cst[:, C_LT:C_LT + 128]
            MK = M2
            self.memset(H, 0.0)
            self.memset(HBF[it % 2], 0.0)
            for c in order:
                i2 = it % 2
                tp = tile_pc(c)
                a_c = Ad[d][:, c * 16:(c + 1) * 16]
                dt_c = DTd[d][:, c * 16:(c + 1) * 16]
                rseg = RSEG[i2]; e_ = E[i2]; wmt = WMT[i2]; cbm = CBM[i2]; xdt = XDT[i2]; xw = XW[i2]
                hbf = HBF[i2]; hbf_n = HBF[(it + 1) % 2]; sml = SMALL[i2]; yo = YO[i2]; y = Y[i2]
                self.tt(rseg.rr("p (h i) -> p h i", i=128), M2.bc(1, 16), a_c.bc(2, 128), ALU.mult, eng="pool")
                for q in range(4):
                    bk = self.bank("seg")
                    self.mm1(bk, M1, rseg[:, q * 512:(q + 1) * 512], True, True)
                    self.act(e_[:, q * 512:(q + 1) * 512], bk, AF.Exp)
                bkA = self.bank("ac")
                self.mm1(bkA[:, 0:16], M2, a_c, True, True)
                self.mm1(bkA[:, 16:32], self.ones, a_c, True, True)
                ACS = sml[:, 0:32]; EA = sml[:, 32:48]; DEC = sml[:, 48:64]; WS = sml[:, 64:80]
                self.copy(ACS, bkA[:, 0:32])
                self.act(sml[:, 32:64], ACS, AF.Exp)
                self.tt(WS, ACS[:, 16:32], ACS[:, 0:16], ALU.subtract)
                self.act(WS, WS, AF.Exp)
                bkC = bkA
                for g in range(2):
                    self.mm1(bkC[:, 256 + g * 128:256 + (g + 1) * 128], BT[g][:, tp:tp + 128], CT[g][:, tp:tp + 128], True, True)
                self.tt(cbm.rr("p (g i) -> p g i", i=128), bkC[:, 256:512].rr("p (g i) -> p g i", i=128), MK.bc(1, 2), ALU.mult)
                self.tt(wmt.rr("p (g r i) -> p g r i", g=2, i=128), e_.rr("p (g r i) -> p g r i", g=2, i=128),
                        cbm.rr("p (g i) -> p g i", i=128).bc(2, 8), ALU.mult)
                x3 = XTOK[:, c, :].rr("p (h q) -> p h q", q=64)
                self.tt(xdt.rr("p (h q) -> p h q", q=64), x3, dt_c.bc(2, 64), ALU.mult, eng="pool")
                self.tt(xw.rr("p (h q) -> p h q", q=64), xdt.rr("p (h q) -> p h q", q=64), WS.bc(2, 64), ALU.mult, eng="pool")
                for g in range(2):
                    bkO = self.bank("rot")
                    self.mm1(bkO, CT[g][:, tp:tp + 128], hbf[:, g * 512:(g + 1) * 512], True, True)
                    self.tt(yo[:, g * 512:(g + 1) * 512].rr("p (h q) -> p h q", q=64), bkO.rr("p (h q) -> p h q", q=64),
                            EA[:, g * 8:(g + 1) * 8].bc(2, 64), ALU.mult)
                w3 = wmt.rr("p (h i) -> p h i", i=128)
                xd3 = xdt.rr("p (h q) -> p h q", q=64)
                for g in range(2):
                    bkY = self.bank("rot")
                    for r in range(8):
                        h = g * 8 + r
                        self.mm1(bkY[:, r * 64:(r + 1) * 64], w3[:, h, :], xd3[:, h, :], True, True)
                    self.tt(y[:, g * 512:(g + 1) * 512], bkY, yo[:, g * 512:(g + 1) * 512], ALU.add)
                self.tt(H.rr("p (h q) -> p h q", q=64), H.rr("p (h q) -> p h q", q=64), DEC.bc(2, 64), ALU.mult)
                for g in range(2):
                    bkH = self.bank("rot")
                    self.mm1(bkH, BTOK[:, c, g * 128:(g + 1) * 128], xw[:, g * 512:(g + 1) * 512], True, True)
                    self.tt(H[:, g * 512:(g + 1) * 512], bkH, H[:, g * 512:(g + 1) * 512], ALU.add)
                self.copy(hbf_n, H, eng="act")
                if d == 0:
                    self.tt(yo.rr("p (h q) -> p h q", q=64), x3, Dsk.bc(2, 64), ALU.mult)
                    self.tt(y, y, yo, ALU.add)
                    self.dma(self.yf_d[c], y)
                else:
                    yfl = YFL[i2]; szl = SZl[i2]; yn = YN[i2]; mtt = MTt[i2]
                    self.dma(yfl, self.yf_d[c])
                    self.dma(szl, self.sz_d[c])
                    self.tt(y, y, yfl, ALU.add)
                    self.tt(y, y, szl, ALU.mult)
                    SSQ = sml[:, 80:81]; RS = sml[:, 81:82]
                    self.act(yo, y, AF.Square, accum=SSQ)
                    self.ts(RS, SSQ, 1.0 / 1024, ALU.mult, EPS, ALU.add)
                    self.tt(RS, RS, self.nhalf, ALU.pow, eng="pool")
                    self.stt(yn, y, RS, snw, ALU.mult, ALU.mult)
                    bkb = self.bank("rot").bitcast(BF16)
                    for k in range(8):
                        self.tr(bkb[:, k * 128:(k + 1) * 128], yn[:, k * 128:(k + 1) * 128], self.identb)
                    self.copy(mtt, bkb, eng="act")
                    dst = V(self.mixT_d.ap[16:24, :, c * 128:(c + 1) * 128].rearrange("k p t -> p k t"), mixT_ssd_bufs)
                    self.dma(dst, mtt.rr("p (k t) -> p k t", t=128))
                it += 1

    def phase_out(self, l):
        self.areset("ALL")
        W = self.al("ALL", 24 * D, BF16, "wout").rr("p (k d) -> p k d", d=D)
        for k in range(24):
            self.dma(W[:, k, :], self.wout_d[l, k * 128:(k + 1) * 128, :], q="pool")
        MT = [self.al("ALL", 24 * 512, BF16, f"MT{i}").rr("p (k t) -> p k t", t=512) for i in range(2)]
        XI = [self.al("ALL", 512, F32, f"XI{i}") for i in range(3)]
        XO = [self.al("ALL", 512, F32, f"XO{i}") for i in range(3)]
        all_mix = sum([p.bufs for p in self.mixT_p], [])
        i = 0
        for r, (a, b) in enumerate(RNG_T):
            if r == 0 and l == DEPTH - 1:
                continue
            n = b - a
            gate = self.modC[:, 32:48] if r == 0 else self.modL[:, 32:48]
            mt = MT[r % 2]
            self.dma(mt[:, :, 0:n], V(self.mixT_d.ap[:, :, a:b].rearrange("k p t -> p k t"), all_mix))
            for m in range(16):
                bk = self.bank()
                self.mm(bk[:, 0:n], [(W[:, k, m * 128:(m + 1) * 128], mt[:, k, 0:n]) for k in range(24)])
                xi = XI[i % 3]; xo = XO[i % 3]
                i += 1
                self.dma(xi[:, 0:n], self.xT_p[m][:, a:b])
                self.stt(xo[:, 0:n], bk[:, 0:n], gate[:, m:m + 1], xi[:, 0:n], ALU.mult, ALU.add)
                self.dma(self.xT_p[m][:, a:b], xo[:, 0:n])


def _consts():
    i = np.arange(128)
    a = i[:, None]; b = i[None, :]
    cst = np.zeros((128, NCST), np.float32)
    cst[:, C_ID:C_ID + 128] = (a == b)
    cst[:, C_ONE:C_ONE + 128] = 1.0
    cst[:, C_LE:C_LE + 128] = (a <= b)
    cst[:, C_GE:C_GE + 128] = (a >= b)
    cst[:, C_GT:C_GT + 128] = (a > b)
    cst[:, C_LT:C_LT + 128] = (a < b)
    cst[:, C_SW:C_SW + 128] = (a == ((b + 64) % 128))
    t = np.arange(2048)
    row = (t // 64).astype(np.float32); colp = (t % 64).astype(np.float32)
    inv = (np.float32(10000.0) ** (-np.arange(32, dtype=np.float32) / np.float32(32))).astype(np.float32)
    ang = np.concatenate([row[:, None] * inv[None, :], colp[:, None] * inv[None, :]], axis=-1).astype(np.float32)
    cos = np.cos(ang).astype(np.float32).T; sin = np.sin(ang).astype(np.float32).T
    rope = np.zeros((128, 4096), np.float32)
    rope[0:64, 0:2048] = cos; rope[64:128, 0:2048] = cos
    rope[0:64, 2048:4096] = -sin; rope[64:128, 2048:4096] = sin
    return cst, rope


def _fm(v, n):
    return np.ascontiguousarray(v.reshape(n, 128).T)


def _layout_params(I):
    pv = np.zeros((DEPTH, 128, NPV), np.float32)
    pr = np.zeros((DEPTH, NPR), np.float32)
    for l in range(DEPTH):
        pv[l, :, PV_NW:PV_NW + 16] = _fm(I["norm_w"][l], 16)
        pv[l, :, PV_ADAB:PV_ADAB + 48] = _fm(I["ada_b"][l], 48)
        for k in range(4):
            pv[l, :, PV_LCW + k * 8:PV_LCW + k * 8 + 8] = _fm(I["lru_conv_w"][l, k], 8)
            pv[l, :, PV_SCW + k * 12:PV_SCW + k * 12 + 12] = _fm(I["ssd_conv_w"][l, k], 12)
        pv[l, :, PV_LCB:PV_LCB + 8] = _fm(I["lru_conv_b"][l], 8)
        for d in range(2):
            pv[l, :, PV_GAB + d * 8:PV_GAB + d * 8 + 8] = _fm(I["lru_ga_b"][l, d], 8)
            pv[l, :, PV_GXB + d * 8:PV_GXB + d * 8 + 8] = _fm(I["lru_gx_b"][l, d], 8)
            pv[l, :, PV_LAM + d * 8:PV_LAM + d * 8 + 8] = _fm(I["lru_lambda"][l, d], 8)
        pv[l, :, PV_SCB:PV_SCB + 12] = _fm(I["ssd_conv_b"][l], 12)
        pv[l, :, PV_QN] = I["att_q_norm"][l]
        pv[l, :, PV_KN] = I["att_k_norm"][l]
        pr[l, PR_SNW:PR_SNW + 1024] = I["ssd_norm_w"][l]
        pr[l, PR_DTB:PR_DTB + 32] = I["ssd_dt_bias"][l].reshape(32)
        pr[l, PR_ALOG:PR_ALOG + 32] = I["ssd_A_log"][l].reshape(32)
        pr[l, PR_D:PR_D + 16] = I["ssd_D"][l]
        pr[l, PR_SINK:PR_SINK + 8] = I["att_sink"][l]
    return pv, pr


def make_in_maps(I, cores):
    cst, rope = _consts()
    pv, pr = _layout_params(I)
    f = lambda a: np.ascontiguousarray(a, dtype=np.float32)
    shared = dict(pv=pv, pr=pr, cst=cst, rope=rope, ada_w=f(I["ada_w"]), w_in=f(I["w_in"]), w_out=f(I["w_out"]),
                  lru_ga_w=f(I["lru_ga_w"]), lru_gx_w=f(I["lru_gx_w"]))
    maps = []
    for b in cores:
        cc = np.concatenate([_fm(f(I["c"][b]), 16), _fm(f(I["c_ctx"]), 16)], axis=1)
        m = dict(shared)
        m.update(x=f(I["x"][b]), ctx=f(I["ctx"][b]), cc=np.ascontiguousarray(cc))
        maps.append(m)
    return maps


_NC_CACHE = {}


def kernel(**inputs):
    if "nc" not in _NC_CACHE:
        kb = KB()
        _NC_CACHE["nc"] = kb.build()
    nc = _NC_CACHE["nc"]
    maps = make_in_maps(inputs, list(range(8)))
    res = run_bass_kernel_spmd(nc, maps, core_ids=list(range(8)))
    out = np.stack([np.asarray(res.results[b]["out"], dtype=np.float32) for b in range(8)], axis=0)
    return out
```

```python
import contextlib
import numpy as np
import concourse.bass as bass
import concourse.mybir as mybir
from concourse.bass_utils import run_bass_kernel_spmd

F32 = mybir.dt.float32
BF16 = mybir.dt.bfloat16
AF = mybir.ActivationFunctionType
ALU = mybir.AluOpType

SAME_ENGINE_SYNC = True
SEM_LIMIT = 30000
DMA_ROT = 12

D = 2048
T = 2304
NT = 18
CTX = 256
LP = 2307
XPW = 2310
DEPTH = 4
EPS = 1e-6
RNG_T = [(0, 256), (256, 768), (768, 1280), (1280, 1792), (1792, 2304)]
RNG_P = [(0, 512), (512, 1024), (1024, 1536), (1536, 2048), (2048, 2307)]
O_LX, O_LG, O_Q, O_K, O_V, O_AG, O_XBC, O_Z, O_DT = 0, 1024, 2048, 3072, 3328, 3584, 4608, 6144, 7168
PV_NW, PV_ADAB, PV_LCW, PV_LCB, PV_GAB, PV_GXB, PV_LAM, PV_SCW, PV_SCB, PV_QN, PV_KN = 0, 16, 64, 96, 104, 120, 136, 152, 200, 212, 213
NPV = 214
PR_SNW, PR_DTB, PR_ALOG, PR_D, PR_SINK = 0, 1024, 1056, 1088, 1104
NPR = 1112
C_ID, C_ONE, C_LE, C_GE, C_GT, C_LT, C_SW = 0, 128, 256, 384, 512, 640, 768
NCST = 896


def pc(t):
    return t if t < CTX else t + 3


def tile_pc(t):
    return t * 128 if t < 2 else t * 128 + 3


class Buf:
    __slots__ = ("name", "w", "r")

    def __init__(self, name=""):
        self.name = name
        self.w = None
        self.r = {}


class Op:
    __slots__ = ("eng", "fn", "deps", "signal", "is_dma", "dsem", "dval", "cnt", "gid", "tag")


class V:
    __slots__ = ("ap", "bufs")

    def __init__(self, ap, bufs):
        self.ap = ap
        self.bufs = bufs

    def __getitem__(self, k):
        return V(self.ap[k], self.bufs)

    def bitcast(self, dt):
        return V(self.ap.bitcast(dt), self.bufs)

    def rr(self, pat, **kw):
        return V(self.ap.rearrange(pat, **kw), self.bufs)

    def bc(self, axis, n):
        a = self.ap.unsqueeze(axis)
        shp = list(a.shape)
        shp[axis] = n
        return V(a.to_broadcast(shp), self.bufs)

    def w(self, bufs):
        return V(self.ap, bufs)

    @property
    def shape(self):
        return self.ap.shape


class Sched:
    ENG = ("pe", "act", "dve", "pool", "sp")

    def __init__(self, nc):
        self.nc = nc
        self.ops = {e: [] for e in self.ENG}
        self.ndma = {e: 0 for e in self.ENG}
        self.dma_last = {}
        self.gid = 0
        self.pending_dma = []
        self.out_dmas = []
        self.tag = ""
        self.annotate = False

    def _add(self, eng, fn, reads, writes, is_dma, extra=()):
        o = Op()
        o.eng = eng; o.fn = fn; o.signal = False; o.is_dma = is_dma
        o.dsem = None; o.dval = 0; o.cnt = None
        o.gid = self.gid; self.gid += 1
        o.tag = self.tag
        deps = {}
        for b in reads:
            if b.w is not None:
                deps[id(b.w)] = b.w
        for b in writes:
            if b.w is not None:
                deps[id(b.w)] = b.w
            for r in b.r.values():
                deps[id(r)] = r
        for d in extra:
            deps[id(d)] = d
        if is_dma:
            k = self.ndma[eng]
            self.ndma[eng] += 1
            slot = (eng, k % DMA_ROT)
            prev = self.dma_last.get(slot)
            if prev is not None:
                deps[id(prev)] = prev
                o.dval = prev.dval + 16
            else:
                o.dval = 16
            o.dsem = slot
            self.dma_last[slot] = o
            self.pending_dma.append(o)
        deps.pop(id(o), None)
        for b in reads:
            key = ("dma", o.gid) if is_dma else eng
            b.r[key] = o
        for b in writes:
            b.w = o
            b.r = {}
        o.deps = list(deps.values())
        for d in o.deps:
            if not d.is_dma:
                if d.eng != eng or is_dma or (SAME_ENGINE_SYNC and eng != "pe"):
                    d.signal = True
        self.ops[eng].append(o)
        return o

    def op(self, eng, fn, reads=(), writes=()):
        return self._add(eng, fn, reads, writes, False)

    def dma(self, eng, fn, reads=(), writes=()):
        return self._add(eng, fn, reads, writes, True)

    def barrier(self):
        lasts = [self.ops[e][-1] for e in self.ENG if self.ops[e]]
        j = self._add("sp", lambda e: e.nop(), (), (), False, extra=lasts + self.pending_dma)
        self.pending_dma = []
        for e in self.ENG:
            if e != "sp":
                self._add(e, lambda en: en.nop(), (), (), False, extra=[j])

    def emit(self):
        nc = self.nc
        nsig = {}
        for e in self.ENG:
            n = 0
            for o in self.ops[e]:
                if (not o.is_dma) and o.signal:
                    n += 1
                    o.cnt = n
            nsig[e] = n
        with contextlib.ExitStack() as st:
            esems = {}
            for e in self.ENG:
                nsem = nsig[e] // SEM_LIMIT + 1
                esems[e] = [st.enter_context(nc.semaphore(f"s_{e}_{i}")) for i in range(nsem)]
            dsems = {}
            for slot in self.dma_last:
                dsems[slot] = st.enter_context(nc.semaphore(f"d_{slot[0]}_{slot[1]}"))
            block = st.enter_context(nc.Block())
            final = list(self.out_dmas)

            def run(e, eng):
                waited = {}
                for o in self.ops[e]:
                    for d in o.deps:
                        if d.is_dma:
                            sem = dsems[d.dsem]; val = d.dval; key = ("d",) + d.dsem
                        else:
                            if d.eng == e and not o.is_dma and (e == "pe" or not SAME_ENGINE_SYNC):
                                continue
                            c = d.cnt - 1
                            sem = esems[d.eng][c // SEM_LIMIT]; val = c % SEM_LIMIT + 1
                            key = ("e", d.eng, c // SEM_LIMIT)
                        if waited.get(key, 0) >= val:
                            continue
                        waited[key] = val
                        eng.wait_ge(sem, val)
                    ins = o.fn(eng)
                    if self.annotate:
                        ins.annotate(o.tag)
                    if o.is_dma:
                        ins.then_inc(dsems[o.dsem], 16)
                    elif o.signal:
                        c = o.cnt - 1
                        ins.then_inc(esems[e][c // SEM_LIMIT], 1)
                if e == "sp":
                    for d in final:
                        eng.wait_ge(dsems[d.dsem], d.dval)

            @block.tensor
            def _(eng):
                run("pe", eng)

            @block.scalar
            def _(eng):
                run("act", eng)

            @block.vector
            def _(eng):
                run("dve", eng)

            @block.gpsimd
            def _(eng):
                run("pool", eng)

            @block.sync
            def _(eng):
                run("sp", eng)


class KB:
    def __init__(self, nlayers=DEPTH, dbg=False, stop=None):
        self.nlayers = nlayers
        self.dbg = dbg
        self.stop = stop
        self.nc = bass.Bass("TRN2", target_bir_lowering=False)
        self.S = Sched(self.nc)
        self.st = contextlib.ExitStack()
        self.pb = 0

    def din(self, name, shape, dt=F32):
        return V(self.nc.dram_tensor(name, shape, dt, kind="ExternalInput").ap(), [Buf(name)])

    def dscr(self, name, shape, dt=F32, out=False):
        kind = "ExternalOutput" if (out or self.dbg) else "Internal"
        return V(self.nc.dram_tensor(name, shape, dt, kind=kind).ap(), [Buf(name)])

    def sb(self, name, shape, dt=F32):
        t = self.st.enter_context(self.nc.sbuf_tensor("sb_" + name, shape, dt))
        return V(t[tuple(slice(None) for _ in shape)], [Buf(name)])

    def al(self, region, n, dt=BF16, name=""):
        ne = n if dt == BF16 else 2 * n
        lo, hi = self.reg[region]
        cur = self.cur[region]
        if cur % 2:
            cur += 1
        assert cur + ne <= hi, f"arena region {region} overflow: {name} need {ne} at {cur} hi {hi}"
        self.cur[region] = cur + ne
        v = V(self.arena[:, cur:cur + ne], [Buf(name)])
        if dt != BF16:
            v = v.bitcast(dt)
        return v

    def areset(self, *regions):
        for r in regions:
            self.cur[r] = self.reg[r][0]

    def bank(self, pool=None):
        if pool is None:
            b = self.pb
            self.pb = (self.pb + 1) % 8
            return self.psb[b]
        lst = self.bpools[pool]
        i = self.bpos.get(pool, 0)
        self.bpos[pool] = i + 1
        return self.psb[lst[i % len(lst)]]

    def set_pools(self, **kw):
        self.bpools = kw
        self.bpos = {}

    @staticmethod
    def _rb(*xs):
        out = []
        for x in xs:
            if isinstance(x, V):
                out.extend(x.bufs)
        return out

    @staticmethod
    def _a(x):
        return x.ap if isinstance(x, V) else x

    def act(self, out, in_, func, bias=None, scale=1.0, accum=None):
        a = self._a
        kw = {}
        if bias is not None:
            kw["bias"] = a(bias)
        if accum is not None:
            kw["accum_out"] = a(accum)
        sc = a(scale)
        self.S.op("act", lambda e: e.activation(out=out.ap, in_=in_.ap, func=func, scale=sc, **kw),
                  self._rb(in_, bias, scale), self._rb(out, accum))

    def tt(self, out, in0, in1, op, eng="dve"):
        self.S.op(eng, lambda e: e.tensor_tensor(out=out.ap, in0=in0.ap, in1=in1.ap, op=op),
                  self._rb(in0, in1), self._rb(out))

    def ts(self, out, in0, s1, op0, s2=None, op1=None, eng="dve"):
        a = self._a
        if op1 is None:
            self.S.op(eng, lambda e: e.tensor_scalar(out=out.ap, in0=in0.ap, scalar1=a(s1), scalar2=None, op0=op0),
                      self._rb(in0, s1), self._rb(out))
        else:
            self.S.op(eng, lambda e: e.tensor_scalar(out=out.ap, in0=in0.ap, scalar1=a(s1), scalar2=a(s2), op0=op0, op1=op1),
                      self._rb(in0, s1, s2), self._rb(out))

    def stt(self, out, in0, scalar, in1, op0, op1):
        a = self._a
        self.S.op("dve", lambda e: e.scalar_tensor_tensor(out=out.ap, in0=in0.ap, scalar=a(scalar), in1=in1.ap, op0=op0, op1=op1),
                  self._rb(in0, scalar, in1), self._rb(out))

    def copy(self, out, in_, eng="dve"):
        if eng == "act":
            self.act(out, in_, AF.Copy)
        else:
            self.S.op(eng, lambda e: e.tensor_copy(out=out.ap, in_=in_.ap), self._rb(in_), self._rb(out))

    def memset(self, out, val, eng="pool"):
        self.S.op(eng, lambda e: e.memset(out.ap, val), (), self._rb(out))

    def recip(self, out, in_):
        self.S.op("dve", lambda e: e.reciprocal(out=out.ap, in_=in_.ap), self._rb(in_), self._rb(out))

    def scan(self, out, a, b, init):
        i = self._a(init)
        self.S.op("dve", lambda e: e.tensor_tensor_scan(out=out.ap, data0=a.ap, data1=b.ap, initial=i, op0=ALU.mult, op1=ALU.add),
                  self._rb(a, b, init), self._rb(out))

    def mm(self, out, pairs):
        n = len(pairs)
        rd = []
        for l, r in pairs:
            rd += l.bufs + r.bufs

        def fn(e):
            ins = None
            for i, (l, r) in enumerate(pairs):
                ins = e.matmul(out.ap, lhsT=l.ap, rhs=r.ap, start=(i == 0), stop=(i == n - 1))
            return ins
        self.S.op("pe", fn, rd, self._rb(out))

    def mm1(self, out, lhsT, rhs, start, stop):
        self.S.op("pe", lambda e: e.matmul(out.ap, lhsT=lhsT.ap, rhs=rhs.ap, start=start, stop=stop),
                  self._rb(lhsT, rhs), self._rb(out))

    def tr(self, out, in_, ident):
        self.S.op("pe", lambda e: e.transpose(out=out.ap, in_=in_.ap, identity=ident.ap),
                  self._rb(in_, ident), self._rb(out))

    def dma(self, out, in_, q="sp"):
        return self.S.dma(q, lambda e: e.dma_start(out=out.ap, in_=in_.ap), self._rb(in_), self._rb(out))

    def build(self):
        nc = self.nc
        L = self.nlayers
        self.x_d = self.din("x", [2048, D])
        self.ctx_d = self.din("ctx", [CTX, D])
        self.cc_d = self.din("cc", [128, 32])
        self.pv_d = self.din("pv", [DEPTH, 128, NPV])
        self.pr_d = self.din("pr", [DEPTH, NPR])
        self.cst_d = self.din("cst", [128, NCST])
        self.rope_d = self.din("rope", [128, 4096])
        self.adaw_d = self.din("ada_w", [DEPTH, D, 3 * D])
        self.win_d = self.din("w_in", [DEPTH, D, 7184])
        self.wout_d = self.din("w_out", [DEPTH, 3072, D])
        self.gaw_d = self.din("lru_ga_w", [DEPTH, 2, 16, 64, 64])
        self.gxw_d = self.din("lru_gx_w", [DEPTH, 2, 16, 64, 64])
        self.out_d = self.dscr("out", [2048, D], out=True)
        self.xT_d = self.dscr("xT_s", [16, 128, T])
        self.xT_p = [self.xT_d[m].w([Buf(f"xT{m}")]) for m in range(16)]
        self.mixT_d = self.dscr("mixT_s", [24, 128, T], BF16)
        self.mixT_p = [self.mixT_d[k].w([Buf(f"mixT{k}")]) for k in range(24)]
        self.yf_d = self.dscr("yf_s", [NT, 128, 1024])
        self.sz_d = self.dscr("sz_s", [NT, 128, 1024], BF16)
        if self.dbg:
            self.uT_dbg = self.dscr("uT_dbg", [16, 128, T], BF16)

        self.cst = self.sb("cst", [128, NCST])
        self.cstb = self.sb("cstb", [128, NCST], BF16)
        self.pvt = self.sb("pvt", [128, NPV])
        self.prt = self.sb("prt", [128, NPR])
        self.cc = self.sb("cc", [128, 32])
        self.scb = self.sb("scb", [128, 32], BF16)
        self.sm = self.sb("sm", [128, 512])
        self.ssm = self.sb("ssm", [128, 1440])
        self.pvn = self.sb("pvn", [128, 64])
        arena_t = self.st.enter_context(nc.sbuf_tensor("arena", [128, 95872], BF16))
        self.arena = arena_t[:, :]
        self.reg = {"R": (0, 32768), "U": (32768, 69632), "T": (69632, 95872), "UT": (32768, 95872), "ALL": (0, 95872)}
        self.cur = {k: v[0] for k, v in self.reg.items()}
        ps_t = self.st.enter_context(nc.psum_tensor("ps", [128, 8, 512], F32))
        self.psb = [V(ps_t[:, b, :], [Buf(f"ps{b}")]) for b in range(8)]

        self.ident = self.cst[:, C_ID:C_ID + 128]
        self.ones = self.cst[:, C_ONE:C_ONE + 128]
        self.identb = self.cstb[:, C_ID:C_ID + 128]
        self.onesb = self.cstb[:, C_ONE:C_ONE + 128]

        self.dma(self.cst, self.cst_d)
        self.dma(self.cc, self.cc_d)
        self.copy(self.cstb, self.cst, eng="dve")
        self.memset(self.sm[:, 500:501], -0.5)
        self.memset(self.sm[:, 501:502], 0.5)
        self.nhalf = self.sm[:, 500:501]
        self.memset(self.sm[:, 502:503], EPS)
        self.memset(self.sm[:, 503:504], 0.25)
        self.epsc = self.sm[:, 502:503]
        self.quart = self.sm[:, 503:504]
        self.phalf = self.sm[:, 501:502]
        th = self.sm[:, 440:472]
        hf = self.sm[:, 400:432]
        self.act(th, self.cc, AF.Tanh, scale=0.5)
        self.ts(hf, self.cc, 0.5, ALU.mult)
        self.stt(self.scb, th, 1.0, hf, ALU.add, ALU.mult)

        self.S.tag = "init"
        self.phase_init()
        for l in range(L):
            self.layer(l)
            if self.stop is not None and self.stop[0] == l:
                break
        if self.stop is None:
            self.phase_final()
        self.S.barrier()
        self.S.emit()
        return nc

    def phase_init(self):
        self.areset("ALL")
        xin = [self.al("ALL", D, F32, f"xin{i}") for i in range(2)]
        xo = [self.al("ALL", D, F32, f"xo{i}") for i in range(2)]
        for t in range(NT):
            src = self.ctx_d[t * 128:(t + 1) * 128, :] if t < 2 else self.x_d[(t - 2) * 128:(t - 1) * 128, :]
            xi = xin[t % 2]
            xoo = xo[t % 2]
            self.dma(xi, src)
            for q in range(4):
                bk = self.bank()
                for i in range(4):
                    m = 4 * q + i
                    self.tr(bk[:, i * 128:(i + 1) * 128], xi[:, m * 128:(m + 1) * 128], self.ident)
                self.copy(xoo[:, q * 512:(q + 1) * 512], bk, eng=("act" if q % 2 else "dve"))
            dst = V(self.xT_d.ap[:, :, t * 128:(t + 1) * 128].rearrange("m p t -> p m t"), sum([p.bufs for p in self.xT_p], []))
            self.dma(dst, xoo.rr("p (m t) -> p m t", t=128))
        self.S.barrier()

    def phase_final(self):
        self.S.barrier()
        self.areset("ALL")
        xc = [self.al("ALL", 2048, F32, f"fx{i}") for i in range(3)]
        ob = self.al("ALL", 16 * D, F32, "fob")
        ob3 = ob.rr("p (t d) -> p t d", d=D)
        for m in range(16):
            xm = xc[m % 3]
            self.dma(xm, self.xT_p[m][:, CTX:T])
            for q in range(4):
                bk = self.bank()
                for i in range(4):
                    t = 4 * q + i
                    self.tr(bk[:, i * 128:(i + 1) * 128], xm[:, t * 128:(t + 1) * 128], self.ident)
                self.copy(ob3[:, 4 * q:4 * q + 4, m * 128:(m + 1) * 128], bk.rr("p (t d) -> p t d", d=128),
                          eng=("act" if q % 2 else "dve"))
        for t in range(16):
            o = self.dma(self.out_d[t * 128:(t + 1) * 128, :], ob3[:, t, :])
            self.S.out_dmas.append(o)

    def layer(self, l):
        st = self.stop[1] if (self.stop is not None and self.stop[0] == l) else None
        self.S.barrier()
        self.S.tag = "mod"
        self.load_params(l)
        if l == 0:
            self.phase_mod(l)
        self.S.barrier()
        self.S.tag = "norm"
        self.phase_norm(l)
        if st == "norm":
            return
        self.S.barrier()
        self.S.tag = "lru"
        self.phase_lru(l)
        if st == "lru":
            return
        self.S.barrier()
        self.S.tag = "att"
        self.phase_att(l)
        if st == "att":
            return
        self.S.barrier()
        self.S.tag = "ssd"
        self.phase_ssd(l)
        if st == "ssd":
            return
        self.S.barrier()
        self.S.tag = "out"
        self.phase_out(l)

    def load_params(self, l):
        self.dma(self.pvt, self.pv_d[l])
        self.dma(self.prt, V(self.pr_d.ap[l:l + 1, :].to_broadcast([128, NPR]), self.pr_d.bufs))
        sm = self.sm
        pv = self.pvt
        self.ch = sm[:, 0:16]; self.hba = sm[:, 16:32]; self.hbx = sm[:, 32:48]
        e1 = sm[:, 48:64]
        self.act(e1, pv[:, PV_LAM:PV_LAM + 16], AF.Exp, scale=-1.0)
        self.act(e1, e1, AF.Ln, bias=1.0)
        self.ts(self.ch, e1, -4.0, ALU.mult)
        self.ts(self.hba, pv[:, PV_GAB:PV_GAB + 16], 0.5, ALU.mult)
        self.ts(self.hbx, pv[:, PV_GXB:PV_GXB + 16], 0.5, ALU.mult)
        self.scwh = sm[:, 64:112]; self.scbh = sm[:, 112:124]
        self.ts(self.scwh, pv[:, PV_SCW:PV_SCW + 48], 0.5, ALU.mult)
        self.ts(self.scbh, pv[:, PV_SCB:PV_SCB + 12], 0.5, ALU.mult)
        self.negA = sm[:, 124:156]; self.esink = sm[:, 156:164]
        self.act(self.negA, self.prt[:, PR_ALOG:PR_ALOG + 32], AF.Exp)
        self.ts(self.negA, self.negA, -1.0, ALU.mult)
        self.act(self.esink, self.prt[:, PR_SINK:PR_SINK + 8], AF.Exp)
        self.modL, self.modC, self.gL, self.gC = self.modset(l)

    def modset(self, l):
        base = 164 if l % 2 == 0 else 300
        sm = self.sm
        return sm[:, base:base + 48], sm[:, base + 48:base + 96], sm[:, base + 96:base + 112], sm[:, base + 112:base + 128]

    def phase_mod(self, l):
        self.areset("ALL")
        wt = [self.al("ALL", 16 * 512, BF16, f"adaw{i}") for i in range(3)]
        bk = self.bank()
        for ct4 in range(12):
            w = wt[ct4 % 3]
            w3 = w.rr("p (k c) -> p k c", c=512)
            src = V(self.adaw_d.ap[l, :, ct4 * 512:(ct4 + 1) * 512].rearrange("(k p) c -> p k c", p=128), self.adaw_d.bufs)
            self.dma(w3, src, q="pool")
            for sub in range(4):
                ct = ct4 * 4 + sub
                self.mm(bk[:, 2 * ct:2 * ct + 2],
                        [(w3[:, k, sub * 128:(sub + 1) * 128], self.scb[:, k:32:16]) for k in range(16)])
        adab = self.pvt[:, PV_ADAB:PV_ADAB + 48]
        self.tt(self.modL, bk[:, 0:96:2], adab, ALU.add)
        self.tt(self.modC, bk[:, 1:96:2], adab, ALU.add)
        nw = self.pvt[:, PV_NW:PV_NW + 16]
        self.stt(self.gL, self.modL[:, 16:32], 1.0, nw, ALU.add, ALU.mult)
        self.stt(self.gC, self.modC[:, 16:32], 1.0, nw, ALU.add, ALU.mult)

    def phase_norm(self, l):
        self.areset("R", "U", "T")
        self.uT = self.al("U", 16 * T, BF16, "uT").rr("p (k t) -> p k t", t=T)
        xb = [self.al("R", T, F32, f"nx{i}") for i in range(3)]
        sq = [self.al("R", T, F32, f"nsq{i}") for i in range(2)]
        rstd = self.al("R", T, F32, "rstd")
        tmp = [self.al("T", T, F32, f"ntmp{i}") for i in range(2)]
        banks = [self.bank() for _ in range(5)]
        for m in range(16):
            x = xb[m % 3]
            s = sq[m % 2]
            self.dma(x, self.xT_p[m])
            self.act(s, x, AF.Square)
            for r, (a, b) in enumerate(RNG_T):
                self.mm1(banks[r][:, 0:b - a], self.ones, s[:, a:b], m == 0, m == 15)
        for r, (a, b) in enumerate(RNG_T):
            self.act(rstd[:, a:b], banks[r][:, 0:b - a], AF.Ln, bias=self.epsc, scale=1.0 / D)
        self.act(rstd, rstd, AF.Exp, scale=-0.5)
        shL = self.modL[:, 0:16]; shC = self.modC[:, 0:16]
        for m in range(16):
            x = xb[m % 3]
            tp = tmp[m % 2]
            self.dma(x, self.xT_p[m])
            for (a, b, g, sh) in ((0, CTX, self.gC, shC), (CTX, T, self.gL, shL)):
                self.stt(tp[:, a:b], x[:, a:b], g[:, m:m + 1], rstd[:, a:b], ALU.mult, ALU.mult)
                self.act(self.uT[:, m, a:b], tp[:, a:b], AF.Identity, bias=sh[:, m:m + 1])
        if self.dbg:
            self.dma(self.uT_dbg.rr("k p t -> p k t"), self.uT)

    def load_w(self, dst3, l, col0, ncols):
        src = V(self.win_d.ap[l, :, col0:col0 + ncols].rearrange("(k p) c -> p k c", p=128), self.win_d.bufs)
        self.dma(dst3, src, q="pool")

    def proj_fm(self, w3, c0, evac, pool=None):
        for r, (a, b) in enumerate(RNG_T):
            bk = self.bank(pool)
            self.mm(bk[:, 0:b - a], [(w3[:, k, c0:c0 + 128], self.uT[:, k, a:b]) for k in range(16)])
            evac(r, a, b, bk[:, 0:b - a])

    def phase_lru(self, l):
        self.areset("R", "T")
        self.set_pools(proj=[0, 1, 2, 3, 4], gate=[5, 6, 7])
        G = [self.al("R", XPW, F32, f"G{i}") for i in range(7)]
        XP0, LG, XC, TA0, TI0, TA1, TI1 = G
        XP1 = self.al("T", XPW, F32, "XP1")
        XPs = [XP0, XP1]
        TAd = [TA0, TA1]; TId = [TI0, TI1]
        HF = self.al("T", XPW, F32, "HF")
        SS = HF
        XCB = self.al("T", XPW, BF16, "xcb")
        OB = self.al("T", XPW, BF16, "ob")
        wl = [self.al("T", 16 * 256, BF16, f"wl{i}").rr("p (k c) -> p k c", c=256) for i in range(2)]
        gwt = self.al("T", 32 * 128, BF16, "gw")
        gw4 = gwt.rr("p (i c) -> p i c", c=128)
        self.memset(gwt, 0.0)
        for gi, wd in enumerate((self.gaw_d, self.gxw_d)):
            for d in range(2):
                base = (gi * 2 + d) * 8
                for par in range(2):
                    src = V(wd.ap[l, d, par::2, :, :].rearrange("j i o -> i j o"), wd.bufs)
                    dst = gw4[par * 64:(par + 1) * 64, base:base + 8, par * 64:(par + 1) * 64]
                    self.dma(dst, src, q="pool")
        pv = self.pvt
        self.memset(LG[:, 256:259], 0.0)
        for XP in XPs:
            self.memset(XP[:, 0:2], 0.0)
            self.memset(XP[:, 258:261], 0.0)
            self.memset(XP[:, 2309:2310], 0.0)
        cw = pv[:, PV_LCW:PV_LCW + 32]

        def front(j):
            w3 = wl[j % 2]
            XP = XPs[j % 2]
            self.load_w(w3[:, :, 0:128], l, O_LX + j * 128, 128)
            self.load_w(w3[:, :, 128:256], l, O_LG + j * 128, 128)
            if j >= 2:
                self.memset(XP[:, 0:2], 0.0)
                self.memset(XP[:, 258:261], 0.0)
                self.memset(XP[:, 2309:2310], 0.0)

            def ev_x(r, a, b, p):
                self.copy(XP[:, a + 2:b + 2] if a < CTX else XP[:, a + 5:b + 5], p, eng=("act" if r % 2 else "dve"))
            self.proj_fm(w3, 0, ev_x, pool="proj")
            self.act(XC[:, 0:LP], XP[:, 0:LP], AF.Identity, bias=pv[:, PV_LCB + j:PV_LCB + j + 1], scale=cw[:, j:j + 1])
            for k in range(1, 4):
                self.stt(XC[:, 0:LP], XP[:, k:k + LP], cw[:, k * 8 + j:k * 8 + j + 1], XC[:, 0:LP], ALU.mult, ALU.add)
            self.copy(XCB[:, 0:LP], XC[:, 0:LP], eng="dve")

        def mid(j):
            for d in range(2):
                col = d * 8 + j
                TA = TAd[d]; TI = TId[d]
                for (a, b) in RNG_P:
                    bk = self.bank("gate")
                    self.mm(bk[:, 0:b - a], [(gw4[:, (0 * 2 + d) * 8 + j, :], XCB[:, a:b])])
                    self.act(TA[:, a:b], bk[:, 0:b - a], AF.Tanh, bias=self.hba[:, col:col + 1], scale=0.5)
                    bk2 = self.bank("gate")
                    self.mm(bk2[:, 0:b - a], [(gw4[:, (1 * 2 + d) * 8 + j, :], XCB[:, a:b])])
                    self.act(TI[:, a:b], bk2[:, 0:b - a], AF.Tanh, bias=self.hbx[:, col:col + 1], scale=0.5)
                self.act(TA[:, 0:LP], TA[:, 0:LP], AF.Exp, bias=self.ch[:, col:col + 1], scale=self.ch[:, col:col + 1])
                self.stt(TI[:, 0:LP], TI[:, 0:LP], 1.0, XC[:, 0:LP], ALU.add, ALU.mult)

        def tail(j):
            w3 = wl[j % 2]
            HB = XPs[j % 2]

            def ev_g(r, a, b, p):
                self.act(LG[:, pc(a):pc(a) + (b - a)], p, AF.Copy, scale=0.5)
            self.proj_fm(w3, 128, ev_g, pool="proj")
            self.act(HF[:, 0:LP], LG[:, 0:LP], AF.Tanh)
            self.stt(LG[:, 0:LP], HF[:, 0:LP], 1.0, LG[:, 0:LP], ALU.add, ALU.mult)
            for d in range(2):
                TA = TAd[d]; TI = TId[d]
                self.act(SS[:, 0:LP], TA[:, 0:LP], AF.Square)
                self.act(SS[:, 0:LP], SS[:, 0:LP], AF.Sqrt, bias=self.quart, scale=-0.25)
                self.tt(TI[:, 0:LP], TI[:, 0:LP], SS[:, 0:LP], ALU.mult)
            self.scan(HF[:, 0:CTX], TA0[:, 0:CTX], TI0[:, 0:CTX], 0.0)
            self.scan(HF[:, 259:LP], TA0[:, 259:LP], TI0[:, 259:LP], HF[:, 255:256])
            self.scan(HB[:, 0:CTX][:, ::-1], TA1[:, 0:CTX][:, ::-1], TI1[:, 0:CTX][:, ::-1], 0.0)
            self.scan(HB[:, 259:LP][:, ::-1], TA1[:, 259:LP][:, ::-1], TI1[:, 259:LP][:, ::-1], HB[:, 0:1])
            self.tt(HF[:, 0:LP], HF[:, 0:LP], HB[:, 0:LP], ALU.add)
            self.tt(OB[:, 0:LP], HF[:, 0:LP], LG[:, 0:LP], ALU.mult)
            self.dma(self.mixT_p[j][:, 0:CTX], OB[:, 0:CTX])
            self.dma(self.mixT_p[j][:, CTX:T], OB[:, 259:LP])

        front(0)
        for j in range(8):
            mid(j)
            if j + 1 < 8:
                front(j + 1)
            tail(j)

    def att_norm_rope(self, RAWp, RAWf, A, Bp, Bf, C, wn, dst):
        self.dma(C[:, 0:2048], self.rope_d[:, 0:2048])
        self.dma(A[:, 0:2048], self.rope_d[:, 2048:4096])
        for r, (a, b) in enumerate(RNG_T):
            self.act(Bp[r], RAWp[r], AF.Square)
            bk = self.bank("ssq")
            self.mm1(bk[:, 0:b - a], self.ones, Bp[r], True, True)
            self.act(Bp[r], bk[:, 0:b - a], AF.Ln, bias=self.epsc, scale=1.0 / 128)
        self.act(Bf, Bf, AF.Exp, scale=-0.5)
        self.stt(RAWf, RAWf, wn, Bf, ALU.mult, ALU.mult)
        self.copy(dst[:, 0:CTX], RAWf[:, 0:CTX], eng="act")
        sw = self.cst[:, C_SW:C_SW + 128]
        for (a, b) in RNG_T[1:]:
            bk = self.bank("swap")
            self.mm1(bk, sw, RAWf[:, a:b], True, True)
            self.tt(A[:, a - CTX:b - CTX], bk, A[:, a - CTX:b - CTX], ALU.mult)
        self.tt(C[:, 0:2048], RAWf[:, CTX:T], C[:, 0:2048], ALU.mult)
        self.tt(dst[:, CTX:T], C[:, 0:2048], A[:, 0:2048], ALU.add)

    def phase_att(self, l):
        self.areset("R", "T")
        Q4 = self.al("R", 4 * T, BF16, "Q4").rr("p (h t) -> p h t", t=T)
        SG4 = self.al("R", 4 * T, BF16, "SG4").rr("p (h t) -> p h t", t=T)
        KT = self.al("R", 2 * T, BF16, "KT").rr("p (g t) -> p g t", t=T)
        VT = self.al("R", NT * 256, BF16, "VT").rr("p (t c) -> p t c", c=256)
        RAWs = [self.al("T", T, F32, "RAW0"), self.al("R", T, F32, "RAW1")]
        off_alias = self.cur["T"]
        A = self.al("T", T, F32, "A"); B = self.al("T", T, F32, "B"); C = self.al("T", T, F32, "C")
        wq = [self.al("T", 16 * 128, BF16, f"wq{i}").rr("p (k c) -> p k c", c=128) for i in range(2)]
        def parts(v):
            ps = [V(v.ap[:, a:b], [Buf(f"part{r}")]) for r, (a, b) in enumerate(RNG_T)]
            return ps, V(v.ap, [p.bufs[0] for p in ps])
        RAWparts = [parts(r) for r in RAWs]
        Bp, Bf = parts(B)
        o = off_alias

        def alias(n, dt, name):
            nonlocal o
            ne = n if dt == BF16 else 2 * n
            v = V(self.arena[:, o:o + ne], [Buf(name)])
            o += ne
            return v if dt == BF16 else v.bitcast(dt)
        PTs = [[alias(512, BF16, f"PT{s_}_{i}") for i in range(5)] for s_ in range(2)]
        DEN = [alias(512, F32, f"DEN{i}") for i in range(2)]
        OO = [alias(512, F32, f"OO{i}") for i in range(2)]
        assert o <= off_alias + 3 * 2 * T
        pv = self.pvt
        GEb = self.cstb[:, C_GE:C_GE + 128]; LEb = self.cstb[:, C_LE:C_LE + 128]
        nw = [0]
        nr = [0]

        def nextw(col0):
            w = wq[nw[0] % 2]
            nw[0] += 1
            self.load_w(w, l, col0, 128)
            return w

        def nextraw():
            r = RAWparts[nr[0] % 2]
            nr[0] += 1
            return r
        scale = float(1.0 / np.sqrt(128.0))
        for g in range(2):
            self.S.barrier()
            self.S.tag = "att:proj"
            self.set_pools(proj=[0, 1, 2, 3], ssq=[4, 5], swap=[6, 7])
            w = nextw(O_K + g * 128)
            RAWp, RAWf = nextraw()

            def ev_raw(r, a, b, p, RAWp=RAWp):
                self.copy(RAWp[r], p, eng=("act" if r % 2 else "dve"))
            self.proj_fm(w, 0, ev_raw, pool="proj")
            self.att_norm_rope(RAWp, RAWf, A, Bp, Bf, C, pv[:, PV_KN:PV_KN + 1], KT[:, g, :])
            w = nextw(O_V + g * 128)
            for t0 in range(0, NT, 4):
                nt = min(4, NT - t0)
                bk = self.bank("proj")
                for i in range(nt):
                    t = t0 + i
                    self.mm(bk[:, i * 128:(i + 1) * 128], [(self.uT[:, k, t * 128:(t + 1) * 128], w[:, k, :]) for k in range(16)])
                self.copy(VT[:, t0:t0 + nt, g * 128:(g + 1) * 128], bk[:, 0:nt * 128].rr("p (t c) -> p t c", c=128), eng="act")
            for h in range(4):
                hh = g * 4 + h
                w = nextw(O_Q + hh * 128)
                RAWp, RAWf = nextraw()

                def ev_raw(r, a, b, p, RAWp=RAWp):
                    self.copy(RAWp[r], p, eng=("act" if r % 2 else "dve"))
                self.proj_fm(w, 0, ev_raw, pool="proj")
                self.att_norm_rope(RAWp, RAWf, A, Bp, Bf, C, pv[:, PV_QN:PV_QN + 1], Q4[:, h, :])
                w = nextw(O_AG + hh * 128)
                RAWp, RAWf = nextraw()

                def ev_ag(r, a, b, p, RAWp=RAWp):
                    self.act(RAWp[r], p, AF.Copy, scale=0.5)
                self.proj_fm(w, 0, ev_ag, pool="proj")
                self.act(Bf, RAWf, AF.Tanh)
                self.stt(SG4[:, h, :], Bf, 1.0, RAWf, ALU.add, ALU.mult)
            es = self.esink[:, g * 4:(g + 1) * 4].bc(2, 128)
            self.S.barrier()
            self.S.tag = "att:blk"
            self.set_pools(st=[0, 1, 2, 3])

            def keys_of(n):
                return [0, 1] + ([j for j in (n - 1, n, n + 1) if 2 <= j < NT] if n >= 2 else [])

            def stageA(n):
                qs = Q4[:, :, n * 128:(n + 1) * 128]
                for idx, j in enumerate(keys_of(n)):
                    sbk = self.bank("st")
                    self.mm(sbk.rr("p (h q) -> p h q", q=128), [(KT[:, g, j * 128:(j + 1) * 128], qs)])
                    pt = PTs[n % 2][idx]
                    self.act(pt, sbk, AF.Exp, scale=scale)
                    if n >= 2 and j >= 2 and j == n - 1:
                        p3 = pt.rr("p (h q) -> p h q", q=128)
                        self.tt(p3, p3, GEb.bc(1, 4), ALU.mult)
                    if n >= 2 and j == n + 1:
                        p3 = pt.rr("p (h q) -> p h q", q=128)
                        self.tt(p3, p3, LEb.bc(1, 4), ALU.mult)

            def stageB(n):
                otb = self.psb[4 + n % 2]; lb = self.psb[6 + n % 2]
                ks = keys_of(n)
                for idx, j in enumerate(ks):
                    pt = PTs[n % 2][idx]
                    self.mm1(otb, VT[:, j, g * 128:(g + 1) * 128], pt, idx == 0, idx == len(ks) - 1)
                    self.mm1(lb, self.onesb, pt, idx == 0, idx == len(ks) - 1)

            def stageC(n):
                otb = self.psb[4 + n % 2]; lb = self.psb[6 + n % 2]
                den = DEN[n % 2]; oo = OO[n % 2]
                self.tt(den.rr("p (h q) -> p h q", q=128), lb.rr("p (h q) -> p h q", q=128), es, ALU.add)
                self.act(den, den, AF.Ln)
                self.act(den, den, AF.Exp, scale=-1.0)
                self.tt(oo, otb, den, ALU.mult)
                sg = SG4[:, :, n * 128:(n + 1) * 128]
                self.tt(sg, oo.rr("p (h q) -> p h q", q=128), sg, ALU.mult)
            for step in range(NT + 2):
                if step < NT:
                    stageA(step)
                if 1 <= step <= NT:
                    stageB(step - 1)
                if step >= 2:
                    stageC(step - 2)
            for h in range(4):
                self.dma(self.mixT_p[8 + g * 4 + h], SG4[:, h, :])

    def phase_ssd(self, l):
        ssm = self.ssm
        DTR = ssm[:, 0:288]
        DTd = [ssm[:, 288:576], ssm[:, 576:864]]
        Ad = [ssm[:, 864:1152], ssm[:, 1152:1440]]
        pv = self.pvt
        self.S.tag = "ssd:z"
        self.areset("R", "T")
        wz = self.al("R", 16 * 1024, BF16, "wz").rr("p (k c) -> p k c", c=1024)
        wdt = self.al("R", 16 * 16, BF16, "wdt").rr("p (k c) -> p k c", c=16)
        ZH = [self.al("T", 1024, F32, f"ZH{i}") for i in range(2)]
        TGz = [self.al("T", 1024, F32, f"TGz{i}") for i in range(2)]
        SZt = [self.al("T", 1024, BF16, f"SZt{i}") for i in range(2)]
        self.load_w(wz[:, :, 0:512], l, O_Z, 512)
        self.load_w(wz[:, :, 512:1024], l, O_Z + 512, 512)
        self.load_w(wdt, l, O_DT, 16)
        self.set_pools(z=[0, 1, 2, 3, 4, 5], dt=[6, 7])
        for t in range(NT):
            zh = ZH[t % 2]; tg = TGz[t % 2]; sz = SZt[t % 2]
            ut = lambda k: self.uT[:, k, t * 128:(t + 1) * 128]
            for half in range(2):
                bk = self.bank("z")
                self.mm(bk, [(ut(k), wz[:, k, half * 512:(half + 1) * 512]) for k in range(16)])
                self.act(zh[:, half * 512:(half + 1) * 512], bk, AF.Copy, scale=0.5)
            bk = self.bank("dt")
            self.mm(bk[:, 0:16], [(ut(k), wdt[:, k, :]) for k in range(16)])
            self.copy(DTR[:, t * 16:(t + 1) * 16], bk[:, 0:16])
            self.act(tg, zh, AF.Tanh)
            self.stt(sz, tg, 1.0, zh, ALU.add, ALU.mult)
            self.dma(self.sz_d[t], sz)
        self.S.barrier()
        self.S.tag = "ssd:prep"
        self.areset("R", "T")
        XTOK = self.al("R", NT * 1024, BF16, "XTOK").rr("p (t c) -> p t c", c=1024)
        BTOK = self.al("R", NT * 256, BF16, "BTOK").rr("p (t c) -> p t c", c=256)
        BT = [self.al("R", LP + 1, BF16, f"BT{g}") for g in range(2)]
        CT = [self.al("R", LP + 1, BF16, f"CT{g}") for g in range(2)]
        XPs = [self.al("T", XPW, F32, "sXP0"), self.al("T", XPW, F32, "sXP1")]
        XH = self.al("T", XPW, F32, "sXH"); TG = self.al("T", XPW, F32, "sTG")
        SIb = self.al("T", LP + 1, BF16, "SIb")
        wl = [self.al("T", 16 * 128, BF16, f"swl{i}").rr("p (k c) -> p k c", c=128) for i in range(2)]
        self.set_pools(proj=[0, 1, 2, 3, 4], tr=[5, 6, 7])
        for XP in XPs:
            self.memset(XP[:, 0:2], 0.0)
            self.memset(XP[:, 258:261], 0.0)
            self.memset(XP[:, 2309:2310], 0.0)
        for cc in range(12):
            w = wl[cc % 2]
            XP = XPs[cc % 2]
            self.load_w(w, l, O_XBC + cc * 128, 128)

            def ev_x(r, a, b, p, XP=XP):
                self.copy(XP[:, a + 2:b + 2] if a < CTX else XP[:, a + 5:b + 5], p, eng=("act" if r % 2 else "dve"))
            self.proj_fm(w, 0, ev_x, pool="proj")
            cw = self.scwh
            self.act(XH[:, 0:LP], XP[:, 0:LP], AF.Identity, bias=self.scbh[:, cc:cc + 1], scale=cw[:, cc:cc + 1])
            for k in range(1, 4):
                self.stt(XH[:, 0:LP], XP[:, k:k + LP], cw[:, k * 12 + cc:k * 12 + cc + 1], XH[:, 0:LP], ALU.mult, ALU.add)
            self.act(TG[:, 0:LP], XH[:, 0:LP], AF.Tanh)
            if cc < 8:
                dest = SIb
            elif cc < 10:
                dest = BT[cc - 8]
            else:
                dest = CT[cc - 10]
            self.stt(dest[:, 0:LP], TG[:, 0:LP], 1.0, XH[:, 0:LP], ALU.add, ALU.mult)
            if cc < 10:
                for t0 in range(0, NT, 8):
                    nt = min(8, NT - t0)
                    bkb = self.bank("tr").bitcast(BF16)
                    for i in range(nt):
                        tp = tile_pc(t0 + i)
                        self.tr(bkb[:, i * 128:(i + 1) * 128], dest[:, tp:tp + 128], self.identb)
                    if cc < 8:
                        dd = XTOK[:, t0:t0 + nt, cc * 128:(cc + 1) * 128]
                    else:
                        dd = BTOK[:, t0:t0 + nt, (cc - 8) * 128:(cc - 7) * 128]
                    self.copy(dd, bkb[:, 0:nt * 128].rr("p (t c) -> p t c", c=128), eng=("act" if (t0 // 8) % 2 else "dve"))
        for d in range(2):
            d3 = DTd[d].rr("p (t h) -> p t h", h=16)
            self.tt(d3, DTR.rr("p (t h) -> p t h", h=16), self.prt[:, PR_DTB + d * 16:PR_DTB + (d + 1) * 16].bc(1, NT), ALU.add)
            self.act(DTd[d], DTd[d], AF.Exp)
            self.act(DTd[d], DTd[d], AF.Ln, bias=1.0)
            self.tt(Ad[d].rr("p (t h) -> p t h", h=16), d3, self.negA[:, d * 16:(d + 1) * 16].bc(1, NT), ALU.mult)
        self.S.barrier()
        self.S.tag = "ssd:loop"
        self.areset("UT")
        A_ = lambda n, dt, nm: [self.al("UT", n, dt, f"{nm}{i}") for i in range(2)]
        RSEG = A_(2048, F32, "RSEG"); E = A_(2048, BF16, "E"); WMT = A_(2048, BF16, "WMT"); CBM = A_(256, BF16, "CBM")
        XDT = A_(1024, BF16, "XDT"); XW = A_(1024, BF16, "XW"); HBF = A_(1024, BF16, "HBF")
        H = self.al("UT", 1024, F32, "H")
        SMALL = A_(128, F32, "SSM")
        YO = A_(1024, F32, "YO"); Y = A_(1024, F32, "Y"); SZl = A_(1024, BF16, "SZl"); YFL = A_(1024, F32, "YFL")
        YN = A_(1024, BF16, "YN"); MTt = A_(1024, BF16, "MTt")
        Dsk = self.prt[:, PR_D:PR_D + 16]
        snw = self.prt[:, PR_SNW:PR_SNW + 1024]
        self.set_pools(seg=[0, 1, 2, 3], ac=[4], rot=[5, 6, 7])
        mixT_ssd_bufs = sum([p.bufs for p in self.mixT_p[16:24]], [])
        pre = (l + 1 < self.nlayers)
        if pre:
            wtp = [self.al("UT", 16 * 512, BF16, f"pwt{i}").rr("p (k c) -> p k c", c=512) for i in range(2)]
            modps = V(self.psb[4].ap[:, 64:160], [Buf("modps")])
            self.dma(self.pvn, self.pv_d[l + 1][:, 0:64])

            def mod_dma(ct4):
                src = V(self.adaw_d.ap[l + 1, :, ct4 * 512:(ct4 + 1) * 512].rearrange("(k p) c -> p k c", p=128), self.adaw_d.bufs)
                self.dma(wtp[ct4 % 2], src, q="pool")

            def mod_mm(ct4):
                w3 = wtp[ct4 % 2]
                for sub in range(4):
                    ct = ct4 * 4 + sub
                    self.mm(modps[:, 2 * ct:2 * ct + 2],
                            [(w3[:, k, sub * 128:(sub + 1) * 128], self.scb[:, k:32:16]) for k in range(16)])

            def mod_fin():
                mL, mC, gL, gC = self.modset(l + 1)
                adab = self.pvn[:, 16:64]; nw = self.pvn[:, 0:16]
                self.tt(mL, modps[:, 0:96:2], adab, ALU.add)
                self.tt(mC, modps[:, 1:96:2], adab, ALU.add)
                self.stt(gL, mL[:, 16:32], 1.0, nw, ALU.add, ALU.mult)
                self.stt(gC, mC[:, 16:32], 1.0, nw, ALU.add, ALU.mult)
        it = 0
        for d in range(2):
            order = list(range(NT)) if d == 0 else [1, 0] + list(range(NT - 1, 1, -1))
            M2 = self.cst[:, C_LE:C_LE + 128] if d == 0 else self.cst[:, C_GE:C_GE + 128]
            M1 = self.cst[:, C_GT:C_GT + 128] if d == 0 else self.cst[:, C_LT:C_LT + 128]
            MK = M2
            self.memset(H, 0.0)
            self.memset(HBF[it % 2], 0.0)
            for c in order:
                if pre and it % 3 == 0:
                    stp = it // 3
                    if stp < 12:
                        mod_dma(stp)
                    if stp >= 1:
                        mod_mm(stp - 1)
                i2 = it % 2
                tp = tile_pc(c)
                a_c = Ad[d][:, c * 16:(c + 1) * 16]
                dt_c = DTd[d][:, c * 16:(c + 1) * 16]
                rseg = RSEG[i2]; e_ = E[i2]; wmt = WMT[i2]; cbm = CBM[i2]; xdt = XDT[i2]; xw = XW[i2]
                hbf = HBF[i2]; hbf_n = HBF[(it + 1) % 2]; sml = SMALL[i2]; yo = YO[i2]; y = Y[i2]
                self.tt(rseg.rr("p (h i) -> p h i", i=128), M2.bc(1, 16), a_c.bc(2, 128), ALU.mult, eng="pool")
                for q in range(4):
                    bk = self.bank("seg")
                    self.mm1(bk, M1, rseg[:, q * 512:(q + 1) * 512], True, True)
                    self.act(e_[:, q * 512:(q + 1) * 512], bk, AF.Exp)
                bkA = self.bank("ac")
                self.mm1(bkA[:, 0:16], M2, a_c, True, True)
                self.mm1(bkA[:, 16:32], self.ones, a_c, True, True)
                ACS = sml[:, 0:32]; EA = sml[:, 32:48]; DEC = sml[:, 48:64]; WS = sml[:, 64:80]
                self.copy(ACS, bkA[:, 0:32])
                self.act(sml[:, 32:64], ACS, AF.Exp)
                self.tt(WS, ACS[:, 16:32], ACS[:, 0:16], ALU.subtract)
                self.act(WS, WS, AF.Exp)
                bkC = bkA
                for g in range(2):
                    self.mm1(bkC[:, 256 + g * 128:256 + (g + 1) * 128], BT[g][:, tp:tp + 128], CT[g][:, tp:tp + 128], True, True)
                self.tt(cbm.rr("p (g i) -> p g i", i=128), bkC[:, 256:512].rr("p (g i) -> p g i", i=128), MK.bc(1, 2), ALU.mult)
                self.tt(wmt.rr("p (g r i) -> p g r i", g=2, i=128), e_.rr("p (g r i) -> p g r i", g=2, i=128),
                        cbm.rr("p (g i) -> p g i", i=128).bc(2, 8), ALU.mult)
                x3 = XTOK[:, c, :].rr("p (h q) -> p h q", q=64)
                self.tt(xdt.rr("p (h q) -> p h q", q=64), x3, dt_c.bc(2, 64), ALU.mult, eng="pool")
                self.tt(xw.rr("p (h q) -> p h q", q=64), xdt.rr("p (h q) -> p h q", q=64), WS.bc(2, 64), ALU.mult, eng="pool")
                for g in range(2):
                    bkO = self.bank("rot")
                    self.mm1(bkO, CT[g][:, tp:tp + 128], hbf[:, g * 512:(g + 1) * 512], True, True)
                    self.tt(yo[:, g * 512:(g + 1) * 512].rr("p (h q) -> p h q", q=64), bkO.rr("p (h q) -> p h q", q=64),
                            EA[:, g * 8:(g + 1) * 8].bc(2, 64), ALU.mult)
                w3 = wmt.rr("p (h i) -> p h i", i=128)
                xd3 = xdt.rr("p (h q) -> p h q", q=64)
                for g in range(2):
                    bkY = self.bank("rot")
                    for r in range(8):
                        h = g * 8 + r
                        self.mm1(bkY[:, r * 64:(r + 1) * 64], w3[:, h, :], xd3[:, h, :], True, True)
                    self.tt(y[:, g * 512:(g + 1) * 512], bkY, yo[:, g * 512:(g + 1) * 512], ALU.add)
                self.tt(H.rr("p (h q) -> p h q", q=64), H.rr("p (h q) -> p h q", q=64), DEC.bc(2, 64), ALU.mult)
                for g in range(2):
                    bkH = self.bank("rot")
                    self.mm1(bkH, BTOK[:, c, g * 128:(g + 1) * 128], xw[:, g * 512:(g + 1) * 512], True, True)
                    self.tt(H[:, g * 512:(g + 1) * 512], bkH, H[:, g * 512:(g + 1) * 512], ALU.add)
                self.copy(hbf_n, H, eng="act")
                if d == 0:
                    self.tt(yo.rr("p (h q) -> p h q", q=64), x3, Dsk.bc(2, 64), ALU.mult)
                    self.tt(y, y, yo, ALU.add)
                    self.dma(self.yf_d[c], y)
                else:
                    yfl = YFL[i2]; szl = SZl[i2]; yn = YN[i2]; mtt = MTt[i2]
                    self.dma(yfl, self.yf_d[c])
                    self.dma(szl, self.sz_d[c])
                    self.tt(y, y, yfl, ALU.add)
                    self.tt(y, y, szl, ALU.mult)
                    SSQ = sml[:, 80:81]; RS = sml[:, 81:82]
                    self.act(yo, y, AF.Square, accum=SSQ)
                    self.ts(RS, SSQ, 1.0 / 1024, ALU.mult, EPS, ALU.add)
                    self.tt(RS, RS, self.nhalf, ALU.pow, eng="pool")
                    self.stt(yn, y, RS, snw, ALU.mult, ALU.mult)
                    bkb = self.bank("rot").bitcast(BF16)
                    for k in range(8):
                        self.tr(bkb[:, k * 128:(k + 1) * 128], yn[:, k * 128:(k + 1) * 128], self.identb)
                    self.copy(mtt, bkb, eng="act")
                    dst = V(self.mixT_d.ap[16:24, :, c * 128:(c + 1) * 128].rearrange("k p t -> p k t"), mixT_ssd_bufs)
                    self.dma(dst, mtt.rr("p (k t) -> p k t", t=128))
                it += 1
        if pre:
            mod_mm(11)
            mod_fin()

    def phase_out(self, l):
        self.areset("ALL")
        W = self.al("ALL", 24 * D, BF16, "wout").rr("p (k d) -> p k d", d=D)
        for k in range(24):
            self.dma(W[:, k, :], self.wout_d[l, k * 128:(k + 1) * 128, :], q="pool")
        MT = [self.al("ALL", 24 * 512, BF16, f"MT{i}").rr("p (k t) -> p k t", t=512) for i in range(2)]
        XI = [self.al("ALL", 512, F32, f"XI{i}") for i in range(3)]
        XO = [self.al("ALL", 512, F32, f"XO{i}") for i in range(3)]
        all_mix = sum([p.bufs for p in self.mixT_p], [])
        i = 0
        for r, (a, b) in enumerate(RNG_T):
            if r == 0 and l == DEPTH - 1:
                continue
            n = b - a
            gate = self.modC[:, 32:48] if r == 0 else self.modL[:, 32:48]
            mt = MT[r % 2]
            self.dma(mt[:, :, 0:n], V(self.mixT_d.ap[:, :, a:b].rearrange("k p t -> p k t"), all_mix))
            for m in range(16):
                bk = self.bank()
                self.mm(bk[:, 0:n], [(W[:, k, m * 128:(m + 1) * 128], mt[:, k, 0:n]) for k in range(24)])
                xi = XI[i % 3]; xo = XO[i % 3]
                i += 1
                self.dma(xi[:, 0:n], self.xT_p[m][:, a:b])
                self.stt(xo[:, 0:n], bk[:, 0:n], gate[:, m:m + 1], xi[:, 0:n], ALU.mult, ALU.add)
                self.dma(self.xT_p[m][:, a:b], xo[:, 0:n])


def _consts():
    i = np.arange(128)
    a = i[:, None]; b = i[None, :]
    cst = np.zeros((128, NCST), np.float32)
    cst[:, C_ID:C_ID + 128] = (a == b)
    cst[:, C_ONE:C_ONE + 128] = 1.0
    cst[:, C_LE:C_LE + 128] = (a <= b)
    cst[:, C_GE:C_GE + 128] = (a >= b)
    cst[:, C_GT:C_GT + 128] = (a > b)
    cst[:, C_LT:C_LT + 128] = (a < b)
    cst[:, C_SW:C_SW + 128] = (a == ((b + 64) % 128))
    t = np.arange(2048)
    row = (t // 64).astype(np.float32); colp = (t % 64).astype(np.float32)
    inv = (np.float32(10000.0) ** (-np.arange(32, dtype=np.float32) / np.float32(32))).astype(np.float32)
    ang = np.concatenate([row[:, None] * inv[None, :], colp[:, None] * inv[None, :]], axis=-1).astype(np.float32)
    cos = np.cos(ang).astype(np.float32).T; sin = np.sin(ang).astype(np.float32).T
    rope = np.zeros((128, 4096), np.float32)
    rope[0:64, 0:2048] = cos; rope[64:128, 0:2048] = cos
    rope[0:64, 2048:4096] = -sin; rope[64:128, 2048:4096] = sin
    return cst, rope


def _fm(v, n):
    return np.ascontiguousarray(v.reshape(n, 128).T)


def _layout_params(I):
    pv = np.zeros((DEPTH, 128, NPV), np.float32)
    pr = np.zeros((DEPTH, NPR), np.float32)
    for l in range(DEPTH):
        pv[l, :, PV_NW:PV_NW + 16] = _fm(I["norm_w"][l], 16)
        pv[l, :, PV_ADAB:PV_ADAB + 48] = _fm(I["ada_b"][l], 48)
        for k in range(4):
            pv[l, :, PV_LCW + k * 8:PV_LCW + k * 8 + 8] = _fm(I["lru_conv_w"][l, k], 8)
            pv[l, :, PV_SCW + k * 12:PV_SCW + k * 12 + 12] = _fm(I["ssd_conv_w"][l, k], 12)
        pv[l, :, PV_LCB:PV_LCB + 8] = _fm(I["lru_conv_b"][l], 8)
        for d in range(2):
            pv[l, :, PV_GAB + d * 8:PV_GAB + d * 8 + 8] = _fm(I["lru_ga_b"][l, d], 8)
            pv[l, :, PV_GXB + d * 8:PV_GXB + d * 8 + 8] = _fm(I["lru_gx_b"][l, d], 8)
            pv[l, :, PV_LAM + d * 8:PV_LAM + d * 8 + 8] = _fm(I["lru_lambda"][l, d], 8)
        pv[l, :, PV_SCB:PV_SCB + 12] = _fm(I["ssd_conv_b"][l], 12)
        pv[l, :, PV_QN] = I["att_q_norm"][l]
        pv[l, :, PV_KN] = I["att_k_norm"][l]
        pr[l, PR_SNW:PR_SNW + 1024] = I["ssd_norm_w"][l]
        pr[l, PR_DTB:PR_DTB + 32] = I["ssd_dt_bias"][l].reshape(32)
        pr[l, PR_ALOG:PR_ALOG + 32] = I["ssd_A_log"][l].reshape(32)
        pr[l, PR_D:PR_D + 16] = I["ssd_D"][l]
        pr[l, PR_SINK:PR_SINK + 8] = I["att_sink"][l]
    return pv, pr


def make_in_maps(I, cores):
    cst, rope = _consts()
    pv, pr = _layout_params(I)
    f = lambda a: np.ascontiguousarray(a, dtype=np.float32)
    shared = dict(pv=pv, pr=pr, cst=cst, rope=rope, ada_w=f(I["ada_w"]), w_in=f(I["w_in"]), w_out=f(I["w_out"]),
                  lru_ga_w=f(I["lru_ga_w"]), lru_gx_w=f(I["lru_gx_w"]))
    maps = []
    for b in cores:
        cc = np.concatenate([_fm(f(I["c"][b]), 16), _fm(f(I["c_ctx"]), 16)], axis=1)
        m = dict(shared)
        m.update(x=f(I["x"][b]), ctx=f(I["ctx"][b]), cc=np.ascontiguousarray(cc))
        maps.append(m)
    return maps


_NC_CACHE = {}


def kernel(**inputs):
    if "nc" not in _NC_CACHE:
        kb = KB()
        _NC_CACHE["nc"] = kb.build()
    nc = _NC_CACHE["nc"]
    maps = make_in_maps(inputs, list(range(8)))
    res = run_bass_kernel_spmd(nc, maps, core_ids=list(range(8)))
    out = np.stack([np.asarray(res.results[b]["out"], dtype=np.float32) for b in range(8)], axis=0)
    return out
```

```python
import contextlib
import numpy as np
import concourse.bass as bass
import concourse.mybir as mybir
from concourse.bass_utils import run_bass_kernel_spmd

F32 = mybir.dt.float32
BF16 = mybir.dt.bfloat16
AF = mybir.ActivationFunctionType
ALU = mybir.AluOpType

SAME_ENGINE_SYNC = True
SEM_LIMIT = 30000
DMA_ROT = 12

D = 2048
T = 2304
NT = 18
CTX = 256
LP = 2307
XPW = 2310
DEPTH = 4
EPS = 1e-6
RNG_T = [(0, 256), (256, 768), (768, 1280), (1280, 1792), (1792, 2304)]
RNG_P = [(0, 512), (512, 1024), (1024, 1536), (1536, 2048), (2048, 2307)]
O_LX, O_LG, O_Q, O_K, O_V, O_AG, O_XBC, O_Z, O_DT = 0, 1024, 2048, 3072, 3328, 3584, 4608, 6144, 7168
PV_NW, PV_ADAB, PV_LCW, PV_LCB, PV_GAB, PV_GXB, PV_LAM, PV_SCW, PV_SCB, PV_QN, PV_KN = 0, 16, 64, 96, 104, 120, 136, 152, 200, 212, 213
NPV = 214
PR_SNW, PR_DTB, PR_ALOG, PR_D, PR_SINK = 0, 1024, 1056, 1088, 1104
NPR = 1112
C_ID, C_ONE, C_LE, C_GE, C_GT, C_LT, C_SW = 0, 128, 256, 384, 512, 640, 768
NCST = 896


def pc(t):
    return t if t < CTX else t + 3


def tile_pc(t):
    return t * 128 if t < 2 else t * 128 + 3


class Buf:
    __slots__ = ("name", "w", "r")

    def __init__(self, name=""):
        self.name = name
        self.w = None
        self.r = {}


class Op:
    __slots__ = ("eng", "fn", "deps", "signal", "is_dma", "dsem", "dval", "cnt", "gid", "tag")


class V:
    __slots__ = ("ap", "bufs")

    def __init__(self, ap, bufs):
        self.ap = ap
        self.bufs = bufs

    def __getitem__(self, k):
        return V(self.ap[k], self.bufs)

    def bitcast(self, dt):
        return V(self.ap.bitcast(dt), self.bufs)

    def rr(self, pat, **kw):
        return V(self.ap.rearrange(pat, **kw), self.bufs)

    def bc(self, axis, n):
        a = self.ap.unsqueeze(axis)
        shp = list(a.shape)
        shp[axis] = n
        return V(a.to_broadcast(shp), self.bufs)

    def w(self, bufs):
        return V(self.ap, bufs)

    @property
    def shape(self):
        return self.ap.shape


class Sched:
    ENG = ("pe", "act", "dve", "pool", "sp")

    def __init__(self, nc):
        self.nc = nc
        self.ops = {e: [] for e in self.ENG}
        self.ndma = {e: 0 for e in self.ENG}
        self.dma_last = {}
        self.gid = 0
        self.pending_dma = []
        self.out_dmas = []
        self.tag = ""
        self.annotate = False

    def _add(self, eng, fn, reads, writes, is_dma, extra=()):
        o = Op()
        o.eng = eng; o.fn = fn; o.signal = False; o.is_dma = is_dma
        o.dsem = None; o.dval = 0; o.cnt = None
        o.gid = self.gid; self.gid += 1
        o.tag = self.tag
        deps = {}
        for b in reads:
            if b.w is not None:
                deps[id(b.w)] = b.w
        for b in writes:
            if b.w is not None:
                deps[id(b.w)] = b.w
            for r in b.r.values():
                deps[id(r)] = r
        for d in extra:
            deps[id(d)] = d
        if is_dma:
            k = self.ndma[eng]
            self.ndma[eng] += 1
            slot = (eng, k % DMA_ROT)
            prev = self.dma_last.get(slot)
            if prev is not None:
                deps[id(prev)] = prev
                o.dval = prev.dval + 16
            else:
                o.dval = 16
            o.dsem = slot
            self.dma_last[slot] = o
            self.pending_dma.append(o)
        deps.pop(id(o), None)
        for b in reads:
            key = ("dma", o.gid) if is_dma else eng
            b.r[key] = o
        for b in writes:
            b.w = o
            b.r = {}
        o.deps = list(deps.values())
        for d in o.deps:
            if not d.is_dma:
                if d.eng != eng or is_dma or (SAME_ENGINE_SYNC and eng != "pe"):
                    d.signal = True
        self.ops[eng].append(o)
        return o

    def op(self, eng, fn, reads=(), writes=()):
        return self._add(eng, fn, reads, writes, False)

    def dma(self, eng, fn, reads=(), writes=()):
        return self._add(eng, fn, reads, writes, True)

    def barrier(self):
        lasts = [self.ops[e][-1] for e in self.ENG if self.ops[e]]
        j = self._add("sp", lambda e: e.nop(), (), (), False, extra=lasts + self.pending_dma)
        self.pending_dma = []
        for e in self.ENG:
            if e != "sp":
                self._add(e, lambda en: en.nop(), (), (), False, extra=[j])

    def emit(self):
        nc = self.nc
        nsig = {}
        for e in self.ENG:
            n = 0
            for o in self.ops[e]:
                if (not o.is_dma) and o.signal:
                    n += 1
                    o.cnt = n
            nsig[e] = n
        with contextlib.ExitStack() as st:
            esems = {}
            for e in self.ENG:
                nsem = nsig[e] // SEM_LIMIT + 1
                esems[e] = [st.enter_context(nc.semaphore(f"s_{e}_{i}")) for i in range(nsem)]
            dsems = {}
            for slot in self.dma_last:
                dsems[slot] = st.enter_context(nc.semaphore(f"d_{slot[0]}_{slot[1]}"))
            block = st.enter_context(nc.Block())
            final = list(self.out_dmas)

            def run(e, eng):
                waited = {}
                for o in self.ops[e]:
                    for d in o.deps:
                        if d.is_dma:
                            sem = dsems[d.dsem]; val = d.dval; key = ("d",) + d.dsem
                        else:
                            if d.eng == e and not o.is_dma and (e == "pe" or not SAME_ENGINE_SYNC):
                                continue
                            c = d.cnt - 1
                            sem = esems[d.eng][c // SEM_LIMIT]; val = c % SEM_LIMIT + 1
                            key = ("e", d.eng, c // SEM_LIMIT)
                        if waited.get(key, 0) >= val:
                            continue
                        waited[key] = val
                        eng.wait_ge(sem, val)
                    ins = o.fn(eng)
                    if self.annotate:
                        ins.annotate(o.tag)
                    if o.is_dma:
                        ins.then_inc(dsems[o.dsem], 16)
                    elif o.signal:
                        c = o.cnt - 1
                        ins.then_inc(esems[e][c // SEM_LIMIT], 1)
                if e == "sp":
                    for d in final:
                        eng.wait_ge(dsems[d.dsem], d.dval)

            @block.tensor
            def _(eng):
                run("pe", eng)

            @block.scalar
            def _(eng):
                run("act", eng)

            @block.vector
            def _(eng):
                run("dve", eng)

            @block.gpsimd
            def _(eng):
                run("pool", eng)

            @block.sync
            def _(eng):
                run("sp", eng)


class KB:
    def __init__(self, nlayers=DEPTH, dbg=False, stop=None):
        self.nlayers = nlayers
        self.dbg = dbg
        self.stop = stop
        self.nc = bass.Bass("TRN2", target_bir_lowering=False)
        self.S = Sched(self.nc)
        self.st = contextlib.ExitStack()
        self.pb = 0

    def din(self, name, shape, dt=F32):
        return V(self.nc.dram_tensor(name, shape, dt, kind="ExternalInput").ap(), [Buf(name)])

    def dscr(self, name, shape, dt=F32, out=False):
        kind = "ExternalOutput" if (out or self.dbg) else "Internal"
        return V(self.nc.dram_tensor(name, shape, dt, kind=kind).ap(), [Buf(name)])

    def sb(self, name, shape, dt=F32):
        t = self.st.enter_context(self.nc.sbuf_tensor("sb_" + name, shape, dt))
        return V(t[tuple(slice(None) for _ in shape)], [Buf(name)])

    def al(self, region, n, dt=BF16, name=""):
        ne = n if dt == BF16 else 2 * n
        lo, hi = self.reg[region]
        cur = self.cur[region]
        if cur % 2:
            cur += 1
        assert cur + ne <= hi, f"arena region {region} overflow: {name} need {ne} at {cur} hi {hi}"
        self.cur[region] = cur + ne
        v = V(self.arena[:, cur:cur + ne], [Buf(name)])
        if dt != BF16:
            v = v.bitcast(dt)
        return v

    def areset(self, *regions):
        for r in regions:
            self.cur[r] = self.reg[r][0]

    def bank(self, pool=None):
        if pool is None:
            b = self.pb
            self.pb = (self.pb + 1) % 8
            return self.psb[b]
        lst = self.bpools[pool]
        i = self.bpos.get(pool, 0)
        self.bpos[pool] = i + 1
        return self.psb[lst[i % len(lst)]]

    def set_pools(self, **kw):
        self.bpools = kw
        self.bpos = {}

    @staticmethod
    def _rb(*xs):
        out = []
        for x in xs:
            if isinstance(x, V):
                out.extend(x.bufs)
        return out

    @staticmethod
    def _a(x):
        return x.ap if isinstance(x, V) else x

    def act(self, out, in_, func, bias=None, scale=1.0, accum=None):
        a = self._a
        kw = {}
        if bias is not None:
            kw["bias"] = a(bias)
        if accum is not None:
            kw["accum_out"] = a(accum)
        sc = a(scale)
        self.S.op("act", lambda e: e.activation(out=out.ap, in_=in_.ap, func=func, scale=sc, **kw),
                  self._rb(in_, bias, scale), self._rb(out, accum))

    def tt(self, out, in0, in1, op, eng="dve"):
        self.S.op(eng, lambda e: e.tensor_tensor(out=out.ap, in0=in0.ap, in1=in1.ap, op=op),
                  self._rb(in0, in1), self._rb(out))

    def ts(self, out, in0, s1, op0, s2=None, op1=None, eng="dve"):
        a = self._a
        if op1 is None:
            self.S.op(eng, lambda e: e.tensor_scalar(out=out.ap, in0=in0.ap, scalar1=a(s1), scalar2=None, op0=op0),
                      self._rb(in0, s1), self._rb(out))
        else:
            self.S.op(eng, lambda e: e.tensor_scalar(out=out.ap, in0=in0.ap, scalar1=a(s1), scalar2=a(s2), op0=op0, op1=op1),
                      self._rb(in0, s1, s2), self._rb(out))

    def stt(self, out, in0, scalar, in1, op0, op1):
        a = self._a
        self.S.op("dve", lambda e: e.scalar_tensor_tensor(out=out.ap, in0=in0.ap, scalar=a(scalar), in1=in1.ap, op0=op0, op1=op1),
                  self._rb(in0, scalar, in1), self._rb(out))

    def copy(self, out, in_, eng="dve"):
        if eng == "act":
            self.act(out, in_, AF.Copy)
        else:
            self.S.op(eng, lambda e: e.tensor_copy(out=out.ap, in_=in_.ap), self._rb(in_), self._rb(out))

    def memset(self, out, val, eng="pool"):
        self.S.op(eng, lambda e: e.memset(out.ap, val), (), self._rb(out))

    def recip(self, out, in_):
        self.S.op("dve", lambda e: e.reciprocal(out=out.ap, in_=in_.ap), self._rb(in_), self._rb(out))

    def scan(self, out, a, b, init):
        i = self._a(init)
        self.S.op("dve", lambda e: e.tensor_tensor_scan(out=out.ap, data0=a.ap, data1=b.ap, initial=i, op0=ALU.mult, op1=ALU.add),
                  self._rb(a, b, init), self._rb(out))

    def mm(self, out, pairs):
        n = len(pairs)
        rd = []
        for l, r in pairs:
            rd += l.bufs + r.bufs

        def fn(e):
            ins = None
            for i, (l, r) in enumerate(pairs):
                ins = e.matmul(out.ap, lhsT=l.ap, rhs=r.ap, start=(i == 0), stop=(i == n - 1))
            return ins
        self.S.op("pe", fn, rd, self._rb(out))

    def mm1(self, out, lhsT, rhs, start, stop):
        self.S.op("pe", lambda e: e.matmul(out.ap, lhsT=lhsT.ap, rhs=rhs.ap, start=start, stop=stop),
                  self._rb(lhsT, rhs), self._rb(out))

    def tr(self, out, in_, ident):
        self.S.op("pe", lambda e: e.transpose(out=out.ap, in_=in_.ap, identity=ident.ap),
                  self._rb(in_, ident), self._rb(out))

    def dma(self, out, in_, q="sp"):
        return self.S.dma(q, lambda e: e.dma_start(out=out.ap, in_=in_.ap), self._rb(in_), self._rb(out))

    def build(self):
        nc = self.nc
        L = self.nlayers
        self.x_d = self.din("x", [2048, D])
        self.ctx_d = self.din("ctx", [CTX, D])
        self.cc_d = self.din("cc", [128, 32])
        self.pv_d = self.din("pv", [DEPTH, 128, NPV])
        self.pr_d = self.din("pr", [DEPTH, NPR])
        self.cst_d = self.din("cst", [128, NCST])
        self.rope_d = self.din("rope", [128, 4096])
        self.adaw_d = self.din("ada_w", [DEPTH, D, 3 * D])
        self.win_d = self.din("w_in", [DEPTH, D, 7184])
        self.wout_d = self.din("w_out", [DEPTH, 3072, D])
        self.gaw_d = self.din("lru_ga_w", [DEPTH, 2, 16, 64, 64])
        self.gxw_d = self.din("lru_gx_w", [DEPTH, 2, 16, 64, 64])
        self.out_d = self.dscr("out", [2048, D], out=True)
        self.xT_d = self.dscr("xT_s", [16, 128, T])
        self.xT_p = [self.xT_d[m].w([Buf(f"xT{m}")]) for m in range(16)]
        self.mixT_d = self.dscr("mixT_s", [24, 128, T], BF16)
        self.mixT_p = [self.mixT_d[k].w([Buf(f"mixT{k}")]) for k in range(24)]
        self.yf_d = self.dscr("yf_s", [NT, 128, 1024])
        self.sz_d = self.dscr("sz_s", [NT, 128, 1024], BF16)
        if self.dbg:
            self.uT_dbg = self.dscr("uT_dbg", [16, 128, T], BF16)

        self.cst = self.sb("cst", [128, NCST])
        self.cstb = self.sb("cstb", [128, NCST], BF16)
        self.pvt = self.sb("pvt", [128, NPV])
        self.prt = self.sb("prt", [128, NPR])
        self.cc = self.sb("cc", [128, 32])
        self.scb = self.sb("scb", [128, 32], BF16)
        self.sm = self.sb("sm", [128, 512])
        self.ssm = self.sb("ssm", [128, 1440])
        self.pvn = self.sb("pvn", [128, 64])
        arena_t = self.st.enter_context(nc.sbuf_tensor("arena", [128, 95872], BF16))
        self.arena = arena_t[:, :]
        self.reg = {"R": (0, 32768), "U": (32768, 69632), "T": (69632, 95872), "UT": (32768, 95872), "ALL": (0, 95872)}
        self.cur = {k: v[0] for k, v in self.reg.items()}
        ps_t = self.st.enter_context(nc.psum_tensor("ps", [128, 8, 512], F32))
        self.psb = [V(ps_t[:, b, :], [Buf(f"ps{b}")]) for b in range(8)]

        self.ident = self.cst[:, C_ID:C_ID + 128]
        self.ones = self.cst[:, C_ONE:C_ONE + 128]
        self.identb = self.cstb[:, C_ID:C_ID + 128]
        self.onesb = self.cstb[:, C_ONE:C_ONE + 128]

        self.dma(self.cst, self.cst_d)
        self.dma(self.cc, self.cc_d)
        self.copy(self.cstb, self.cst, eng="dve")
        self.memset(self.sm[:, 500:501], -0.5)
        self.memset(self.sm[:, 501:502], 0.5)
        self.nhalf = self.sm[:, 500:501]
        self.memset(self.sm[:, 502:503], EPS)
        self.memset(self.sm[:, 503:504], 0.25)
        self.epsc = self.sm[:, 502:503]
        self.quart = self.sm[:, 503:504]
        self.phalf = self.sm[:, 501:502]
        th = self.sm[:, 440:472]
        hf = self.sm[:, 400:432]
        self.act(th, self.cc, AF.Tanh, scale=0.5)
        self.ts(hf, self.cc, 0.5, ALU.mult)
        self.stt(self.scb, th, 1.0, hf, ALU.add, ALU.mult)

        self.S.tag = "init"
        self.phase_init()
        for l in range(L):
            self.layer(l)
            if self.stop is not None and self.stop[0] == l:
                break
        if self.stop is None:
            self.phase_final()
        self.S.barrier()
        self.S.emit()
        return nc

    def phase_init(self):
        self.areset("ALL")
        xin = [self.al("ALL", D, F32, f"xin{i}") for i in range(2)]
        xo = [self.al("ALL", D, F32, f"xo{i}") for i in range(2)]
        for t in range(NT):
            src = self.ctx_d[t * 128:(t + 1) * 128, :] if t < 2 else self.x_d[(t - 2) * 128:(t - 1) * 128, :]
            xi = xin[t % 2]
            xoo = xo[t % 2]
            self.dma(xi, src)
            for q in range(4):
                bk = self.bank()
                for i in range(4):
                    m = 4 * q + i
                    self.tr(bk[:, i * 128:(i + 1) * 128], xi[:, m * 128:(m + 1) * 128], self.ident)
                self.copy(xoo[:, q * 512:(q + 1) * 512], bk, eng=("act" if q % 2 else "dve"))
            dst = V(self.xT_d.ap[:, :, t * 128:(t + 1) * 128].rearrange("m p t -> p m t"), sum([p.bufs for p in self.xT_p], []))
            self.dma(dst, xoo.rr("p (m t) -> p m t", t=128))
        self.S.barrier()

    def phase_final(self):
        self.S.barrier()
        self.areset("ALL")
        xc = [self.al("ALL", 2048, F32, f"fx{i}") for i in range(3)]
        ob = self.al("ALL", 16 * D, F32, "fob")
        ob3 = ob.rr("p (t d) -> p t d", d=D)
        for m in range(16):
            xm = xc[m % 3]
            self.dma(xm, self.xT_p[m][:, CTX:T])
            for q in range(4):
                bk = self.bank()
                for i in range(4):
                    t = 4 * q + i
                    self.tr(bk[:, i * 128:(i + 1) * 128], xm[:, t * 128:(t + 1) * 128], self.ident)
                self.copy(ob3[:, 4 * q:4 * q + 4, m * 128:(m + 1) * 128], bk.rr("p (t d) -> p t d", d=128),
                          eng=("act" if q % 2 else "dve"))
        for t in range(16):
            o = self.dma(self.out_d[t * 128:(t + 1) * 128, :], ob3[:, t, :])
            self.S.out_dmas.append(o)

    def layer(self, l):
        st = self.stop[1] if (self.stop is not None and self.stop[0] == l) else None
        self.S.barrier()
        self.S.tag = "mod"
        self.load_params(l)
        if l == 0:
            self.phase_mod(l)
        self.S.barrier()
        self.S.tag = "norm"
        self.phase_norm(l)
        if st == "norm":
            return
        self.S.barrier()
        self.S.tag = "lru"
        self.phase_lru(l)
        if st == "lru":
            return
        self.S.barrier()
        self.S.tag = "att"
        self.phase_att(l)
        if st == "att":
            return
        self.S.barrier()
        self.S.tag = "ssd"
        self.phase_ssd(l)
        if st == "ssd":
            return
        self.S.barrier()
        self.S.tag = "out"
        self.phase_out(l)

    def load_params(self, l):
        self.dma(self.pvt, self.pv_d[l])
        self.dma(self.prt, V(self.pr_d.ap[l:l + 1, :].to_broadcast([128, NPR]), self.pr_d.bufs))
        sm = self.sm
        pv = self.pvt
        self.ch = sm[:, 0:16]; self.hba = sm[:, 16:32]; self.hbx = sm[:, 32:48]
        e1 = sm[:, 48:64]
        self.act(e1, pv[:, PV_LAM:PV_LAM + 16], AF.Exp, scale=-1.0)
        self.act(e1, e1, AF.Ln, bias=1.0)
        self.ts(self.ch, e1, -4.0, ALU.mult)
        self.ts(self.hba, pv[:, PV_GAB:PV_GAB + 16], 0.5, ALU.mult)
        self.ts(self.hbx, pv[:, PV_GXB:PV_GXB + 16], 0.5, ALU.mult)
        self.scwh = sm[:, 64:112]; self.scbh = sm[:, 112:124]
        self.ts(self.scwh, pv[:, PV_SCW:PV_SCW + 48], 0.5, ALU.mult)
        self.ts(self.scbh, pv[:, PV_SCB:PV_SCB + 12], 0.5, ALU.mult)
        self.negA = sm[:, 124:156]; self.esink = sm[:, 156:164]
        self.act(self.negA, self.prt[:, PR_ALOG:PR_ALOG + 32], AF.Exp)
        self.ts(self.negA, self.negA, -1.0, ALU.mult)
        self.act(self.esink, self.prt[:, PR_SINK:PR_SINK + 8], AF.Exp)
        self.modL, self.modC, self.gL, self.gC = self.modset(l)

    def modset(self, l):
        base = 164 if l % 2 == 0 else 300
        sm = self.sm
        return sm[:, base:base + 48], sm[:, base + 48:base + 96], sm[:, base + 96:base + 112], sm[:, base + 112:base + 128]

    def phase_mod(self, l):
        self.areset("ALL")
        wt = [self.al("ALL", 16 * 512, BF16, f"adaw{i}") for i in range(3)]
        bk = self.bank()
        for ct4 in range(12):
            w = wt[ct4 % 3]
            w3 = w.rr("p (k c) -> p k c", c=512)
            src = V(self.adaw_d.ap[l, :, ct4 * 512:(ct4 + 1) * 512].rearrange("(k p) c -> p k c", p=128), self.adaw_d.bufs)
            self.dma(w3, src, q="pool")
            for sub in range(4):
                ct = ct4 * 4 + sub
                self.mm(bk[:, 2 * ct:2 * ct + 2],
                        [(w3[:, k, sub * 128:(sub + 1) * 128], self.scb[:, k:32:16]) for k in range(16)])
        adab = self.pvt[:, PV_ADAB:PV_ADAB + 48]
        self.tt(self.modL, bk[:, 0:96:2], adab, ALU.add)
        self.tt(self.modC, bk[:, 1:96:2], adab, ALU.add)
        nw = self.pvt[:, PV_NW:PV_NW + 16]
        self.stt(self.gL, self.modL[:, 16:32], 1.0, nw, ALU.add, ALU.mult)
        self.stt(self.gC, self.modC[:, 16:32], 1.0, nw, ALU.add, ALU.mult)

    def phase_norm(self, l):
        self.areset("R", "U", "T")
        self.uT = self.al("U", 16 * T, BF16, "uT").rr("p (k t) -> p k t", t=T)
        xb = [self.al("R", T, F32, f"nx{i}") for i in range(3)]
        sq = [self.al("R", T, F32, f"nsq{i}") for i in range(2)]
        rstd = self.al("R", T, F32, "rstd")
        tmp = [self.al("T", T, F32, f"ntmp{i}") for i in range(2)]
        banks = [self.bank() for _ in range(5)]
        for m in range(16):
            x = xb[m % 3]
            s = sq[m % 2]
            self.dma(x, self.xT_p[m])
            self.act(s, x, AF.Square)
            for r, (a, b) in enumerate(RNG_T):
                self.mm1(banks[r][:, 0:b - a], self.ones, s[:, a:b], m == 0, m == 15)
        for r, (a, b) in enumerate(RNG_T):
            self.act(rstd[:, a:b], banks[r][:, 0:b - a], AF.Ln, bias=self.epsc, scale=1.0 / D)
        self.act(rstd, rstd, AF.Exp, scale=-0.5)
        shL = self.modL[:, 0:16]; shC = self.modC[:, 0:16]
        for m in range(16):
            x = xb[m % 3]
            tp = tmp[m % 2]
            self.dma(x, self.xT_p[m])
            for (a, b, g, sh) in ((0, CTX, self.gC, shC), (CTX, T, self.gL, shL)):
                self.stt(tp[:, a:b], x[:, a:b], g[:, m:m + 1], rstd[:, a:b], ALU.mult, ALU.mult)
                self.act(self.uT[:, m, a:b], tp[:, a:b], AF.Identity, bias=sh[:, m:m + 1])
        if self.dbg:
            self.dma(self.uT_dbg.rr("k p t -> p k t"), self.uT)

    def load_w(self, dst3, l, col0, ncols):
        src = V(self.win_d.ap[l, :, col0:col0 + ncols].rearrange("(k p) c -> p k c", p=128), self.win_d.bufs)
        self.dma(dst3, src, q="pool")

    def proj_fm(self, w3, c0, evac, pool=None):
        for r, (a, b) in enumerate(RNG_T):
            bk = self.bank(pool)
            self.mm(bk[:, 0:b - a], [(w3[:, k, c0:c0 + 128], self.uT[:, k, a:b]) for k in range(16)])
            evac(r, a, b, bk[:, 0:b - a])

    def phase_lru(self, l):
        self.areset("R", "T")
        self.set_pools(proj=[0, 1, 2, 3, 4], gate=[5, 6, 7])
        G = [self.al("R", XPW, F32, f"G{i}") for i in range(7)]
        XP0, LG, XC, TA0, TI0, TA1, TI1 = G
        XP1 = self.al("T", XPW, F32, "XP1")
        XPs = [XP0, XP1]
        TAd = [TA0, TA1]; TId = [TI0, TI1]
        HF = self.al("T", XPW, F32, "HF")
        SS = HF
        XCB = self.al("T", XPW, BF16, "xcb")
        OB = self.al("T", XPW, BF16, "ob")
        wl = [self.al("T", 16 * 256, BF16, f"wl{i}").rr("p (k c) -> p k c", c=256) for i in range(2)]
        gwt = self.al("T", 32 * 128, BF16, "gw")
        gw4 = gwt.rr("p (i c) -> p i c", c=128)
        self.memset(gwt, 0.0)
        for gi, wd in enumerate((self.gaw_d, self.gxw_d)):
            for d in range(2):
                base = (gi * 2 + d) * 8
                for par in range(2):
                    src = V(wd.ap[l, d, par::2, :, :].rearrange("j i o -> i j o"), wd.bufs)
                    dst = gw4[par * 64:(par + 1) * 64, base:base + 8, par * 64:(par + 1) * 64]
                    self.dma(dst, src, q="pool")
        pv = self.pvt
        self.memset(LG[:, 256:259], 0.0)
        for XP in XPs:
            self.memset(XP[:, 0:2], 0.0)
            self.memset(XP[:, 258:261], 0.0)
            self.memset(XP[:, 2309:2310], 0.0)
        cw = pv[:, PV_LCW:PV_LCW + 32]

        def front(j):
            w3 = wl[j % 2]
            XP = XPs[j % 2]
            self.load_w(w3[:, :, 0:128], l, O_LX + j * 128, 128)
            self.load_w(w3[:, :, 128:256], l, O_LG + j * 128, 128)
            if j >= 2:
                self.memset(XP[:, 0:2], 0.0)
                self.memset(XP[:, 258:261], 0.0)
                self.memset(XP[:, 2309:2310], 0.0)

            def ev_x(r, a, b, p):
                self.copy(XP[:, a + 2:b + 2] if a < CTX else XP[:, a + 5:b + 5], p, eng=("act" if r % 2 else "dve"))
            self.proj_fm(w3, 0, ev_x, pool="proj")
            self.act(XC[:, 0:LP], XP[:, 0:LP], AF.Identity, bias=pv[:, PV_LCB + j:PV_LCB + j + 1], scale=cw[:, j:j + 1])
            for k in range(1, 4):
                self.stt(XC[:, 0:LP], XP[:, k:k + LP], cw[:, k * 8 + j:k * 8 + j + 1], XC[:, 0:LP], ALU.mult, ALU.add)
            self.copy(XCB[:, 0:LP], XC[:, 0:LP], eng="dve")

        def mid(j):
            for d in range(2):
                col = d * 8 + j
                TA = TAd[d]; TI = TId[d]
                for (a, b) in RNG_P:
                    bk = self.bank("gate")
                    self.mm(bk[:, 0:b - a], [(gw4[:, (0 * 2 + d) * 8 + j, :], XCB[:, a:b])])
                    self.act(TA[:, a:b], bk[:, 0:b - a], AF.Tanh, bias=self.hba[:, col:col + 1], scale=0.5)
                    bk2 = self.bank("gate")
                    self.mm(bk2[:, 0:b - a], [(gw4[:, (1 * 2 + d) * 8 + j, :], XCB[:, a:b])])
                    self.act(TI[:, a:b], bk2[:, 0:b - a], AF.Tanh, bias=self.hbx[:, col:col + 1], scale=0.5)
                self.act(TA[:, 0:LP], TA[:, 0:LP], AF.Exp, bias=self.ch[:, col:col + 1], scale=self.ch[:, col:col + 1])
                self.stt(TI[:, 0:LP], TI[:, 0:LP], 1.0, XC[:, 0:LP], ALU.add, ALU.mult)

        def tail(j):
            w3 = wl[j % 2]
            HB = XPs[j % 2]

            def ev_g(r, a, b, p):
                self.act(LG[:, pc(a):pc(a) + (b - a)], p, AF.Copy, scale=0.5)
            self.proj_fm(w3, 128, ev_g, pool="proj")
            self.act(HF[:, 0:LP], LG[:, 0:LP], AF.Tanh)
            self.stt(LG[:, 0:LP], HF[:, 0:LP], 1.0, LG[:, 0:LP], ALU.add, ALU.mult)
            for d in range(2):
                TA = TAd[d]; TI = TId[d]
                self.act(SS[:, 0:LP], TA[:, 0:LP], AF.Square)
                self.act(SS[:, 0:LP], SS[:, 0:LP], AF.Sqrt, bias=self.quart, scale=-0.25)
                self.tt(TI[:, 0:LP], TI[:, 0:LP], SS[:, 0:LP], ALU.mult)
            self.scan(HF[:, 0:CTX], TA0[:, 0:CTX], TI0[:, 0:CTX], 0.0)
            self.scan(HF[:, 259:LP], TA0[:, 259:LP], TI0[:, 259:LP], HF[:, 255:256])
            self.scan(HB[:, 0:CTX][:, ::-1], TA1[:, 0:CTX][:, ::-1], TI1[:, 0:CTX][:, ::-1], 0.0)
            self.scan(HB[:, 259:LP][:, ::-1], TA1[:, 259:LP][:, ::-1], TI1[:, 259:LP][:, ::-1], HB[:, 0:1])
            self.tt(HF[:, 0:LP], HF[:, 0:LP], HB[:, 0:LP], ALU.add)
            self.tt(OB[:, 0:LP], HF[:, 0:LP], LG[:, 0:LP], ALU.mult)
            self.dma(self.mixT_p[j][:, 0:CTX], OB[:, 0:CTX])
            self.dma(self.mixT_p[j][:, CTX:T], OB[:, 259:LP])

        front(0)
        for j in range(8):
            mid(j)
            if j + 1 < 8:
                front(j + 1)
            tail(j)

    def att_norm_rope(self, RAWp, RAWf, A, Bp, Bf, C, wn, dst):
        self.dma(C[:, 0:2048], self.rope_d[:, 0:2048])
        self.dma(A[:, 0:2048], self.rope_d[:, 2048:4096])
        for r, (a, b) in enumerate(RNG_T):
            self.act(Bp[r], RAWp[r], AF.Square)
            bk = self.bank("ssq")
            self.mm1(bk[:, 0:b - a], self.ones, Bp[r], True, True)
            self.act(Bp[r], bk[:, 0:b - a], AF.Ln, bias=self.epsc, scale=1.0 / 128)
        self.act(Bf, Bf, AF.Exp, scale=-0.5)
        self.stt(RAWf, RAWf, wn, Bf, ALU.mult, ALU.mult)
        self.copy(dst[:, 0:CTX], RAWf[:, 0:CTX], eng="act")
        sw = self.cst[:, C_SW:C_SW + 128]
        for (a, b) in RNG_T[1:]:
            bk = self.bank("swap")
            self.mm1(bk, sw, RAWf[:, a:b], True, True)
            self.tt(A[:, a - CTX:b - CTX], bk, A[:, a - CTX:b - CTX], ALU.mult)
        self.tt(C[:, 0:2048], RAWf[:, CTX:T], C[:, 0:2048], ALU.mult)
        self.tt(dst[:, CTX:T], C[:, 0:2048], A[:, 0:2048], ALU.add)

    def phase_att(self, l):
        self.areset("R", "T")
        Q4 = self.al("R", 4 * T, BF16, "Q4").rr("p (h t) -> p h t", t=T)
        SG4 = self.al("R", 4 * T, BF16, "SG4").rr("p (h t) -> p h t", t=T)
        KT = self.al("R", 2 * T, BF16, "KT").rr("p (g t) -> p g t", t=T)
        VT = self.al("R", NT * 256, BF16, "VT").rr("p (t c) -> p t c", c=256)
        RAWs = [self.al("T", T, F32, "RAW0"), self.al("R", T, F32, "RAW1")]
        off_alias = self.cur["T"]
        A = self.al("T", T, F32, "A"); B = self.al("T", T, F32, "B"); C = self.al("T", T, F32, "C")
        wq = [self.al("T", 16 * 128, BF16, f"wq{i}").rr("p (k c) -> p k c", c=128) for i in range(2)]
        def parts(v):
            ps = [V(v.ap[:, a:b], [Buf(f"part{r}")]) for r, (a, b) in enumerate(RNG_T)]
            return ps, V(v.ap, [p.bufs[0] for p in ps])
        RAWparts = [parts(r) for r in RAWs]
        Bp, Bf = parts(B)
        o = off_alias

        def alias(n, dt, name):
            nonlocal o
            ne = n if dt == BF16 else 2 * n
            v = V(self.arena[:, o:o + ne], [Buf(name)])
            o += ne
            return v if dt == BF16 else v.bitcast(dt)
        PTs = [[alias(512, BF16, f"PT{s_}_{i}") for i in range(5)] for s_ in range(2)]
        DEN = [alias(512, F32, f"DEN{i}") for i in range(2)]
        OO = [alias(512, F32, f"OO{i}") for i in range(2)]
        assert o <= off_alias + 3 * 2 * T
        pv = self.pvt
        GEb = self.cstb[:, C_GE:C_GE + 128]; LEb = self.cstb[:, C_LE:C_LE + 128]
        nw = [0]
        nr = [0]

        def nextw(col0):
            w = wq[nw[0] % 2]
            nw[0] += 1
            self.load_w(w, l, col0, 128)
            return w

        def nextraw():
            r = RAWparts[nr[0] % 2]
            nr[0] += 1
            return r
        scale = float(1.0 / np.sqrt(128.0))
        for g in range(2):
            self.S.barrier()
            self.S.tag = "att:proj"
            self.set_pools(proj=[0, 1, 2, 3], ssq=[4, 5], swap=[6, 7])
            w = nextw(O_K + g * 128)
            RAWp, RAWf = nextraw()

            def ev_raw(r, a, b, p, RAWp=RAWp):
                self.copy(RAWp[r], p, eng=("act" if r % 2 else "dve"))
            self.proj_fm(w, 0, ev_raw, pool="proj")
            self.att_norm_rope(RAWp, RAWf, A, Bp, Bf, C, pv[:, PV_KN:PV_KN + 1], KT[:, g, :])
            w = nextw(O_V + g * 128)
            for t0 in range(0, NT, 4):
                nt = min(4, NT - t0)
                bk = self.bank("proj")
                for i in range(nt):
                    t = t0 + i
                    self.mm(bk[:, i * 128:(i + 1) * 128], [(self.uT[:, k, t * 128:(t + 1) * 128], w[:, k, :]) for k in range(16)])
                self.copy(VT[:, t0:t0 + nt, g * 128:(g + 1) * 128], bk[:, 0:nt * 128].rr("p (t c) -> p t c", c=128), eng="act")
            for h in range(4):
                hh = g * 4 + h
                w = nextw(O_Q + hh * 128)
                RAWp, RAWf = nextraw()

                def ev_raw(r, a, b, p, RAWp=RAWp):
                    self.copy(RAWp[r], p, eng=("act" if r % 2 else "dve"))
                self.proj_fm(w, 0, ev_raw, pool="proj")
                self.att_norm_rope(RAWp, RAWf, A, Bp, Bf, C, pv[:, PV_QN:PV_QN + 1], Q4[:, h, :])
                w = nextw(O_AG + hh * 128)
                RAWp, RAWf = nextraw()

                def ev_ag(r, a, b, p, RAWp=RAWp):
                    self.act(RAWp[r], p, AF.Copy, scale=0.5)
                self.proj_fm(w, 0, ev_ag, pool="proj")
                self.act(Bf, RAWf, AF.Tanh)
                self.stt(SG4[:, h, :], Bf, 1.0, RAWf, ALU.add, ALU.mult)
            es = self.esink[:, g * 4:(g + 1) * 4].bc(2, 128)
            self.S.barrier()
            self.S.tag = "att:blk"
            self.set_pools(st=[0, 1, 2, 3])

            def keys_of(n):
                return [0, 1] + ([j for j in (n - 1, n, n + 1) if 2 <= j < NT] if n >= 2 else [])

            def stageA(n):
                qs = Q4[:, :, n * 128:(n + 1) * 128]
                for idx, j in enumerate(keys_of(n)):
                    sbk = self.bank("st")
                    self.mm(sbk.rr("p (h q) -> p h q", q=128), [(KT[:, g, j * 128:(j + 1) * 128], qs)])
                    pt = PTs[n % 2][idx]
                    self.act(pt, sbk, AF.Exp, scale=scale)
                    if n >= 2 and j >= 2 and j == n - 1:
                        p3 = pt.rr("p (h q) -> p h q", q=128)
                        self.tt(p3, p3, GEb.bc(1, 4), ALU.mult)
                    if n >= 2 and j == n + 1:
                        p3 = pt.rr("p (h q) -> p h q", q=128)
                        self.tt(p3, p3, LEb.bc(1, 4), ALU.mult)

            def stageB(n):
                otb = self.psb[4 + n % 2]; lb = self.psb[6 + n % 2]
                ks = keys_of(n)
                for idx, j in enumerate(ks):
                    pt = PTs[n % 2][idx]
                    self.mm1(otb, VT[:, j, g * 128:(g + 1) * 128], pt, idx == 0, idx == len(ks) - 1)
                    self.mm1(lb, self.onesb, pt, idx == 0, idx == len(ks) - 1)

            def stageC(n):
                otb = self.psb[4 + n % 2]; lb = self.psb[6 + n % 2]
                den = DEN[n % 2]; oo = OO[n % 2]
                self.tt(den.rr("p (h q) -> p h q", q=128), lb.rr("p (h q) -> p h q", q=128), es, ALU.add)
                self.act(den, den, AF.Ln)
                self.act(den, den, AF.Exp, scale=-1.0)
                self.tt(oo, otb, den, ALU.mult)
                sg = SG4[:, :, n * 128:(n + 1) * 128]
                self.tt(sg, oo.rr("p (h q) -> p h q", q=128), sg, ALU.mult)
            for step in range(NT + 2):
                if step < NT:
                    stageA(step)
                if 1 <= step <= NT:
                    stageB(step - 1)
                if step >= 2:
                    stageC(step - 2)
            for h in range(4):
                self.dma(self.mixT_p[8 + g * 4 + h], SG4[:, h, :])

    def phase_ssd(self, l):
        ssm = self.ssm
        DTR = ssm[:, 0:288]
        DTd = [ssm[:, 288:576], ssm[:, 576:864]]
        Ad = [ssm[:, 864:1152], ssm[:, 1152:1440]]
        pv = self.pvt
        self.S.tag = "ssd:z"
        self.areset("R", "T")
        wz = self.al("R", 16 * 1024, BF16, "wz").rr("p (k c) -> p k c", c=1024)
        wdt = self.al("R", 16 * 16, BF16, "wdt").rr("p (k c) -> p k c", c=16)
        ZH = [self.al("T", 1024, F32, f"ZH{i}") for i in range(2)]
        TGz = [self.al("T", 1024, F32, f"TGz{i}") for i in range(2)]
        SZt = [self.al("T", 1024, BF16, f"SZt{i}") for i in range(2)]
        self.load_w(wz[:, :, 0:512], l, O_Z, 512)
        self.load_w(wz[:, :, 512:1024], l, O_Z + 512, 512)
        self.load_w(wdt, l, O_DT, 16)
        self.set_pools(z=[0, 1, 2, 3, 4, 5], dt=[6, 7])
        for t in range(NT):
            zh = ZH[t % 2]; tg = TGz[t % 2]; sz = SZt[t % 2]
            ut = lambda k: self.uT[:, k, t * 128:(t + 1) * 128]
            for half in range(2):
                bk = self.bank("z")
                self.mm(bk, [(ut(k), wz[:, k, half * 512:(half + 1) * 512]) for k in range(16)])
                self.act(zh[:, half * 512:(half + 1) * 512], bk, AF.Copy, scale=0.5)
            bk = self.bank("dt")
            self.mm(bk[:, 0:16], [(ut(k), wdt[:, k, :]) for k in range(16)])
            self.copy(DTR[:, t * 16:(t + 1) * 16], bk[:, 0:16])
            self.act(tg, zh, AF.Tanh)
            self.stt(sz, tg, 1.0, zh, ALU.add, ALU.mult)
            self.dma(self.sz_d[t], sz)
        self.S.barrier()
        self.S.tag = "ssd:prep"
        self.areset("R", "T")
        XTOK = self.al("R", NT * 1024, BF16, "XTOK").rr("p (t c) -> p t c", c=1024)
        BTOK = self.al("R", NT * 256, BF16, "BTOK").rr("p (t c) -> p t c", c=256)
        BT = [self.al("R", LP + 1, BF16, f"BT{g}") for g in range(2)]
        CT = [self.al("R", LP + 1, BF16, f"CT{g}") for g in range(2)]
        XPs = [self.al("T", XPW, F32, "sXP0"), self.al("T", XPW, F32, "sXP1")]
        XH = self.al("T", XPW, F32, "sXH"); TG = self.al("T", XPW, F32, "sTG")
        SIb = self.al("T", LP + 1, BF16, "SIb")
        wl = [self.al("T", 16 * 128, BF16, f"swl{i}").rr("p (k c) -> p k c", c=128) for i in range(2)]
        self.set_pools(proj=[0, 1, 2, 3, 4], tr=[5, 6, 7])
        for XP in XPs:
            self.memset(XP[:, 0:2], 0.0)
            self.memset(XP[:, 258:261], 0.0)
            self.memset(XP[:, 2309:2310], 0.0)
        for cc in range(12):
            w = wl[cc % 2]
            XP = XPs[cc % 2]
            self.load_w(w, l, O_XBC + cc * 128, 128)

            def ev_x(r, a, b, p, XP=XP):
                self.copy(XP[:, a + 2:b + 2] if a < CTX else XP[:, a + 5:b + 5], p, eng=("act" if r % 2 else "dve"))
            self.proj_fm(w, 0, ev_x, pool="proj")
            cw = self.scwh
            self.act(XH[:, 0:LP], XP[:, 0:LP], AF.Identity, bias=self.scbh[:, cc:cc + 1], scale=cw[:, cc:cc + 1])
            for k in range(1, 4):
                self.stt(XH[:, 0:LP], XP[:, k:k + LP], cw[:, k * 12 + cc:k * 12 + cc + 1], XH[:, 0:LP], ALU.mult, ALU.add)
            self.act(TG[:, 0:LP], XH[:, 0:LP], AF.Tanh)
            if cc < 8:
                dest = SIb
            elif cc < 10:
                dest = BT[cc - 8]
            else:
                dest = CT[cc - 10]
            self.stt(dest[:, 0:LP], TG[:, 0:LP], 1.0, XH[:, 0:LP], ALU.add, ALU.mult)
            if cc < 10:
                for t0 in range(0, NT, 8):
                    nt = min(8, NT - t0)
                    bkb = self.bank("tr").bitcast(BF16)
                    for i in range(nt):
                        tp = tile_pc(t0 + i)
                        self.tr(bkb[:, i * 128:(i + 1) * 128], dest[:, tp:tp + 128], self.identb)
                    if cc < 8:
                        dd = XTOK[:, t0:t0 + nt, cc * 128:(cc + 1) * 128]
                    else:
                        dd = BTOK[:, t0:t0 + nt, (cc - 8) * 128:(cc - 7) * 128]
                    self.copy(dd, bkb[:, 0:nt * 128].rr("p (t c) -> p t c", c=128), eng=("act" if (t0 // 8) % 2 else "dve"))
        for d in range(2):
            d3 = DTd[d].rr("p (t h) -> p t h", h=16)
            self.tt(d3, DTR.rr("p (t h) -> p t h", h=16), self.prt[:, PR_DTB + d * 16:PR_DTB + (d + 1) * 16].bc(1, NT), ALU.add)
            self.act(DTd[d], DTd[d], AF.Exp)
            self.act(DTd[d], DTd[d], AF.Ln, bias=1.0)
            self.tt(Ad[d].rr("p (t h) -> p t h", h=16), d3, self.negA[:, d * 16:(d + 1) * 16].bc(1, NT), ALU.mult)
        self.S.barrier()
        self.S.tag = "ssd:loop"
        self.areset("UT")
        A_ = lambda n, dt, nm: [self.al("UT", n, dt, f"{nm}{i}") for i in range(2)]
        RSEG = A_(2048, F32, "RSEG"); E = A_(2048, BF16, "E"); WMT = A_(2048, BF16, "WMT"); CBM = A_(256, BF16, "CBM")
        XDT = A_(1024, BF16, "XDT"); XW = A_(1024, BF16, "XW"); HBF = A_(1024, BF16, "HBF")
        H = self.al("UT", 1024, F32, "H")
        SMALL = A_(128, F32, "SSM")
        YO = A_(1024, F32, "YO"); Y = A_(1024, F32, "Y"); SZl = A_(1024, BF16, "SZl"); YFL = A_(1024, F32, "YFL")
        YN = A_(1024, BF16, "YN"); MTt = A_(1024, BF16, "MTt")
        Dsk = self.prt[:, PR_D:PR_D + 16]
        snw = self.prt[:, PR_SNW:PR_SNW + 1024]
        self.set_pools(seg=[0, 1, 2, 3], ac=[4], rot=[5, 6, 7])
        mixT_ssd_bufs = sum([p.bufs for p in self.mixT_p[16:24]], [])
        pre = (l + 1 < self.nlayers)
        if pre:
            wtp = [self.al("UT", 16 * 512, BF16, f"pwt{i}").rr("p (k c) -> p k c", c=512) for i in range(2)]
            modps = V(self.psb[4].ap[:, 64:160], [Buf("modps")])
            self.dma(self.pvn, self.pv_d[l + 1][:, 0:64])

            def mod_dma(ct4):
                src = V(self.adaw_d.ap[l + 1, :, ct4 * 512:(ct4 + 1) * 512].rearrange("(k p) c -> p k c", p=128), self.adaw_d.bufs)
                self.dma(wtp[ct4 % 2], src, q="pool")

            def mod_mm(ct4):
                w3 = wtp[ct4 % 2]
                for sub in range(4):
                    ct = ct4 * 4 + sub
                    self.mm(modps[:, 2 * ct:2 * ct + 2],
                            [(w3[:, k, sub * 128:(sub + 1) * 128], self.scb[:, k:32:16]) for k in range(16)])

            def mod_fin():
                mL, mC, gL, gC = self.modset(l + 1)
                adab = self.pvn[:, 16:64]; nw = self.pvn[:, 0:16]
                self.tt(mL, modps[:, 0:96:2], adab, ALU.add)
                self.tt(mC, modps[:, 1:96:2], adab, ALU.add)
                self.stt(gL, mL[:, 16:32], 1.0, nw, ALU.add, ALU.mult)
                self.stt(gC, mC[:, 16:32], 1.0, nw, ALU.add, ALU.mult)
        it = 0
        for d in range(2):
            order = list(range(NT)) if d == 0 else [1, 0] + list(range(NT - 1, 1, -1))
            M2 = self.cst[:, C_LE:C_LE + 128] if d == 0 else self.cst[:, C_GE:C_GE + 128]
            M1 = self.cst[:, C_GT:C_GT + 128] if d == 0 else self.cst[:, C_LT:C_LT + 128]
            MK = M2
            self.memset(H, 0.0)
            self.memset(HBF[it % 2], 0.0)
            for c in order:
                if pre and it % 3 == 0:
                    stp = it // 3
                    if stp < 12:
                        mod_dma(stp)
                    if stp >= 1:
                        mod_mm(stp - 1)
                i2 = it % 2
                tp = tile_pc(c)
                a_c = Ad[d][:, c * 16:(c + 1) * 16]
                dt_c = DTd[d][:, c * 16:(c + 1) * 16]
                rseg = RSEG[i2]; e_ = E[i2]; wmt = WMT[i2]; cbm = CBM[i2]; xdt = XDT[i2]; xw = XW[i2]
                hbf = HBF[i2]; hbf_n = HBF[(it + 1) % 2]; sml = SMALL[i2]; yo = YO[i2]; y = Y[i2]
                self.tt(rseg.rr("p (h i) -> p h i", i=128), M2.bc(1, 16), a_c.bc(2, 128), ALU.mult, eng="pool")
                for q in range(4):
                    bk = self.bank("seg")
                    self.mm1(bk, M1, rseg[:, q * 512:(q + 1) * 512], True, True)
                    self.act(e_[:, q * 512:(q + 1) * 512], bk, AF.Exp)
                bkA = self.bank("ac")
                self.mm1(bkA[:, 0:16], M2, a_c, True, True)
                self.mm1(bkA[:, 16:32], self.ones, a_c, True, True)
                ACS = sml[:, 0:32]; EA = sml[:, 32:48]; DEC = sml[:, 48:64]; WS = sml[:, 64:80]
                self.copy(ACS, bkA[:, 0:32])
                self.act(sml[:, 32:64], ACS, AF.Exp)
                self.tt(WS, ACS[:, 16:32], ACS[:, 0:16], ALU.subtract)
                self.act(WS, WS, AF.Exp)
                bkC = bkA
                for g in range(2):
                    self.mm1(bkC[:, 256 + g * 128:256 + (g + 1) * 128], BT[g][:, tp:tp + 128], CT[g][:, tp:tp + 128], True, True)
                self.tt(cbm.rr("p (g i) -> p g i", i=128), bkC[:, 256:512].rr("p (g i) -> p g i", i=128), MK.bc(1, 2), ALU.mult)
                self.tt(wmt.rr("p (g r i) -> p g r i", g=2, i=128), e_.rr("p (g r i) -> p g r i", g=2, i=128),
                        cbm.rr("p (g i) -> p g i", i=128).bc(2, 8), ALU.mult)
                x3 = XTOK[:, c, :].rr("p (h q) -> p h q", q=64)
                self.tt(xdt.rr("p (h q) -> p h q", q=64), x3, dt_c.bc(2, 64), ALU.mult, eng="pool")
                self.tt(xw.rr("p (h q) -> p h q", q=64), xdt.rr("p (h q) -> p h q", q=64), WS.bc(2, 64), ALU.mult, eng="pool")
                for g in range(2):
                    bkO = self.bank("rot")
                    self.mm1(bkO, CT[g][:, tp:tp + 128], hbf[:, g * 512:(g + 1) * 512], True, True)
                    self.tt(yo[:, g * 512:(g + 1) * 512].rr("p (h q) -> p h q", q=64), bkO.rr("p (h q) -> p h q", q=64),
                            EA[:, g * 8:(g + 1) * 8].bc(2, 64), ALU.mult)
                w3 = wmt.rr("p (h i) -> p h i", i=128)
                xd3 = xdt.rr("p (h q) -> p h q", q=64)
                for g in range(2):
                    bkY = self.bank("rot")
                    for r in range(8):
                        h = g * 8 + r
                        self.mm1(bkY[:, r * 64:(r + 1) * 64], w3[:, h, :], xd3[:, h, :], True, True)
                    self.tt(y[:, g * 512:(g + 1) * 512], bkY, yo[:, g * 512:(g + 1) * 512], ALU.add)
                self.tt(H.rr("p (h q) -> p h q", q=64), H.rr("p (h q) -> p h q", q=64), DEC.bc(2, 64), ALU.mult)
                for g in range(2):
                    bkH = self.bank("rot")
                    self.mm1(bkH, BTOK[:, c, g * 128:(g + 1) * 128], xw[:, g * 512:(g + 1) * 512], True, True)
                    self.tt(H[:, g * 512:(g + 1) * 512], bkH, H[:, g * 512:(g + 1) * 512], ALU.add)
                self.copy(hbf_n, H, eng="act")
                if d == 0:
                    self.tt(yo.rr("p (h q) -> p h q", q=64), x3, Dsk.bc(2, 64), ALU.mult)
                    self.tt(y, y, yo, ALU.add)
                    self.dma(self.yf_d[c], y)
                else:
                    yfl = YFL[i2]; szl = SZl[i2]; yn = YN[i2]; mtt = MTt[i2]
                    self.dma(yfl, self.yf_d[c])
                    self.dma(szl, self.sz_d[c])
                    self.tt(y, y, yfl, ALU.add)
                    self.tt(y, y, szl, ALU.mult)
                    SSQ = sml[:, 80:81]; RS = sml[:, 81:82]
                    self.act(yo, y, AF.Square, accum=SSQ)
                    self.ts(RS, SSQ, 1.0 / 1024, ALU.mult, EPS, ALU.add)
                    self.tt(RS, RS, self.nhalf, ALU.pow, eng="pool")
                    self.stt(yn, y, RS, snw, ALU.mult, ALU.mult)
                    bkb = self.bank("rot").bitcast(BF16)
                    for k in range(8):
                        self.tr(bkb[:, k * 128:(k + 1) * 128], yn[:, k * 128:(k + 1) * 128], self.identb)
                    self.copy(mtt, bkb, eng="act")
                    dst = V(self.mixT_d.ap[16:24, :, c * 128:(c + 1) * 128].rearrange("k p t -> p k t"), mixT_ssd_bufs)
                    self.dma(dst, mtt.rr("p (k t) -> p k t", t=128))
                it += 1
        if pre:
            mod_mm(11)
            mod_fin()

    def phase_out(self, l):
        self.areset("ALL")
        W = self.al("ALL", 24 * D, BF16, "wout").rr("p (k d) -> p k d", d=D)
        for k0 in range(0, 24, 4):
            src = V(self.wout_d.ap[l, k0 * 128:(k0 + 4) * 128, :].rearrange("(k p) d -> p k d", p=128), self.wout_d.bufs)
            self.dma(W[:, k0:k0 + 4, :], src, q="pool")
        MT = [self.al("ALL", 24 * 512, BF16, f"MT{i}").rr("p (k t) -> p k t", t=512) for i in range(2)]
        XI = [self.al("ALL", 512, F32, f"XI{i}") for i in range(3)]
        XO = [self.al("ALL", 512, F32, f"XO{i}") for i in range(3)]
        all_mix = sum([p.bufs for p in self.mixT_p], [])
        i = 0
        for r, (a, b) in enumerate(RNG_T):
            if r == 0 and l == DEPTH - 1:
                continue
            n = b - a
            gate = self.modC[:, 32:48] if r == 0 else self.modL[:, 32:48]
            mt = MT[r % 2]
            self.dma(mt[:, :, 0:n], V(self.mixT_d.ap[:, :, a:b].rearrange("k p t -> p k t"), all_mix))
            for m in range(16):
                bk = self.bank()
                self.mm(bk[:, 0:n], [(W[:, k, m * 128:(m + 1) * 128], mt[:, k, 0:n]) for k in range(24)])
                xi = XI[i % 3]; xo = XO[i % 3]
                i += 1
                self.dma(xi[:, 0:n], self.xT_p[m][:, a:b])
                self.stt(xo[:, 0:n], bk[:, 0:n], gate[:, m:m + 1], xi[:, 0:n], ALU.mult, ALU.add)
                self.dma(self.xT_p[m][:, a:b], xo[:, 0:n])


def _consts():
    i = np.arange(128)
    a = i[:, None]; b = i[None, :]
    cst = np.zeros((128, NCST), np.float32)
    cst[:, C_ID:C_ID + 128] = (a == b)
    cst[:, C_ONE:C_ONE + 128] = 1.0
    cst[:, C_LE:C_LE + 128] = (a <= b)
    cst[:, C_GE:C_GE + 128] = (a >= b)
    cst[:, C_GT:C_GT + 128] = (a > b)
    cst[:, C_LT:C_LT + 128] = (a < b)
    cst[:, C_SW:C_SW + 128] = (a == ((b + 64) % 128))
    t = np.arange(2048)
    row = (t // 64).astype(np.float32); colp = (t % 64).astype(np.float32)
    inv = (np.float32(10000.0) ** (-np.arange(32, dtype=np.float32) / np.float32(32))).astype(np.float32)
    ang = np.concatenate([row[:, None] * inv[None, :], colp[:, None] * inv[None, :]], axis=-1).astype(np.float32)
    cos = np.cos(ang).astype(np.float32).T; sin = np.sin(ang).astype(np.float32).T
    rope = np.zeros((128, 4096), np.float32)
    rope[0:64, 0:2048] = cos; rope[64:128, 0:2048] = cos
    rope[0:64, 2048:4096] = -sin; rope[64:128, 2048:4096] = sin
    return cst, rope


def _fm(v, n):
    return np.ascontiguousarray(v.reshape(n, 128).T)


def _layout_params(I):
    pv = np.zeros((DEPTH, 128, NPV), np.float32)
    pr = np.zeros((DEPTH, NPR), np.float32)
    for l in range(DEPTH):
        pv[l, :, PV_NW:PV_NW + 16] = _fm(I["norm_w"][l], 16)
        pv[l, :, PV_ADAB:PV_ADAB + 48] = _fm(I["ada_b"][l], 48)
        for k in range(4):
            pv[l, :, PV_LCW + k * 8:PV_LCW + k * 8 + 8] = _fm(I["lru_conv_w"][l, k], 8)
            pv[l, :, PV_SCW + k * 12:PV_SCW + k * 12 + 12] = _fm(I["ssd_conv_w"][l, k], 12)
        pv[l, :, PV_LCB:PV_LCB + 8] = _fm(I["lru_conv_b"][l], 8)
        for d in range(2):
            pv[l, :, PV_GAB + d * 8:PV_GAB + d * 8 + 8] = _fm(I["lru_ga_b"][l, d], 8)
            pv[l, :, PV_GXB + d * 8:PV_GXB + d * 8 + 8] = _fm(I["lru_gx_b"][l, d], 8)
            pv[l, :, PV_LAM + d * 8:PV_LAM + d * 8 + 8] = _fm(I["lru_lambda"][l, d], 8)
        pv[l, :, PV_SCB:PV_SCB + 12] = _fm(I["ssd_conv_b"][l], 12)
        pv[l, :, PV_QN] = I["att_q_norm"][l]
        pv[l, :, PV_KN] = I["att_k_norm"][l]
        pr[l, PR_SNW:PR_SNW + 1024] = I["ssd_norm_w"][l]
        pr[l, PR_DTB:PR_DTB + 32] = I["ssd_dt_bias"][l].reshape(32)
        pr[l, PR_ALOG:PR_ALOG + 32] = I["ssd_A_log"][l].reshape(32)
        pr[l, PR_D:PR_D + 16] = I["ssd_D"][l]
        pr[l, PR_SINK:PR_SINK + 8] = I["att_sink"][l]
    return pv, pr


def make_in_maps(I, cores):
    cst, rope = _consts()
    pv, pr = _layout_params(I)
    f = lambda a: np.ascontiguousarray(a, dtype=np.float32)
    shared = dict(pv=pv, pr=pr, cst=cst, rope=rope, ada_w=f(I["ada_w"]), w_in=f(I["w_in"]), w_out=f(I["w_out"]),
                  lru_ga_w=f(I["lru_ga_w"]), lru_gx_w=f(I["lru_gx_w"]))
    maps = []
    for b in cores:
        cc = np.concatenate([_fm(f(I["c"][b]), 16), _fm(f(I["c_ctx"]), 16)], axis=1)
        m = dict(shared)
        m.update(x=f(I["x"][b]), ctx=f(I["ctx"][b]), cc=np.ascontiguousarray(cc))
        maps.append(m)
    return maps


_NC_CACHE = {}


def kernel(**inputs):
    if "nc" not in _NC_CACHE:
        kb = KB()
        _NC_CACHE["nc"] = kb.build()
    nc = _NC_CACHE["nc"]
    maps = make_in_maps(inputs, list(range(8)))
    res = run_bass_kernel_spmd(nc, maps, core_ids=list(range(8)))
    out = np.stack([np.asarray(res.results[b]["out"], dtype=np.float32) for b in range(8)], axis=0)
    return out
```
